# Optimizing a Trainium2 kernel written in Bass

```python
import math
import jax
import jax.numpy as jnp
from jax import lax
import numpy as np

D_MODEL = 1024
BATCH = 4
SEQ = 8192
DEPTH = 4

GRID_W = 64
CTX_LEN = 256
HEAD_DIM = 64
Q_BLOCK = 128
ROPE_THETA = 10000.0
EPS = 1e-6
MIX_HALF = D_MODEL // 2
NA_HEADS = MIX_HALF // HEAD_DIM
NA_WIN_R = 8
NA_WIN_C = 16
DA_VDIM = 2 * HEAD_DIM
DA_HEADS = MIX_HALF // DA_VDIM
GQA_Q_HEADS = MIX_HALF // HEAD_DIM
GQA_KV_HEADS = GQA_Q_HEADS // 4
HY_WIDTH = MIX_HALF
HY_EMB_DIM = 33
HY_FILTER_ORDER = 64
HY_FAST_DECAY = 0.3
HY_SLOW_DECAY = 1.5
HY_DECAY_TARGET = 1e-2
FFN_HIDDEN = ((8 * D_MODEL // 3 + 255) // 256) * 256

NA_WIDTH = NA_HEADS * HEAD_DIM
DA_QK_WIDTH = DA_HEADS * 2 * HEAD_DIM
DA_V_WIDTH = DA_HEADS * DA_VDIM
EVEN_IN = 3 * NA_WIDTH + 2 * DA_QK_WIDTH + DA_V_WIDTH
EVEN_MIX = NA_WIDTH + DA_V_WIDTH
GQA_Q_WIDTH = GQA_Q_HEADS * HEAD_DIM
GQA_KV_WIDTH = GQA_KV_HEADS * HEAD_DIM
ODD_IN = GQA_Q_WIDTH + 2 * GQA_KV_WIDTH + 3 * HY_WIDTH
ODD_MIX = GQA_Q_WIDTH + HY_WIDTH
N_EVEN = (DEPTH + 1) // 2
N_ODD = DEPTH // 2

kernel_name = 'hybrid_na_diff_gqa_hyena_dit'

F32 = jnp.float32


def rms_norm(x, gain=None):
    xf = x.astype(F32)
    y = xf * lax.rsqrt(jnp.mean(xf * xf, axis=-1, keepdims=True) + EPS)
    if gain is not None:
        y = y * gain.astype(F32)
    return y.astype(x.dtype)


def modulate(x, shift, scale):
    return rms_norm(x) * (1 + scale) + shift


def split_heads(x, n, d):
    b, l = x.shape[:2]
    return x.reshape(b, l, n, d).transpose(0, 2, 1, 3)


def merge_heads(x):
    b, h, l, d = x.shape
    return x.transpose(0, 2, 1, 3).reshape(b, l, h * d)


def dwconv3(x, w, b):
    xp = jnp.pad(x, ((0, 0), (1, 1), (0, 0)))
    return xp[:, :-2] * w[0] + xp[:, 1:-1] * w[1] + xp[:, 2:] * w[2] + b


def axial_rope(length, head_dim):
    t = jnp.arange(length, dtype=jnp.int32)
    row = (t // GRID_W).astype(F32)
    col = (t % GRID_W).astype(F32)
    n_pairs = head_dim // 4
    inv_freq = ROPE_THETA ** (-jnp.arange(n_pairs, dtype=F32) / n_pairs)
    ang = jnp.concatenate([row[:, None] * inv_freq, col[:, None] * inv_freq], axis=-1)
    return jnp.cos(ang), jnp.sin(ang)


def apply_rope(x, cos, sin):
    xf = x.astype(F32).reshape(x.shape[:-1] + (x.shape[-1] // 2, 2))
    x0, x1 = xf[..., 0], xf[..., 1]
    out = jnp.stack([x0 * cos - x1 * sin, x0 * sin + x1 * cos], axis=-1)
    return out.reshape(x.shape).astype(x.dtype)


def sweep_query_blocks(fn, *qs):
    b, h, l = qs[0].shape[:3]
    n = l // Q_BLOCK
    blocks = tuple(jnp.moveaxis(q.reshape(b, h, n, Q_BLOCK, q.shape[-1]), 2, 0) for q in qs)
    out = lax.map(lambda qb: fn(*qb), blocks)
    return jnp.moveaxis(out, 0, 2).reshape(b, h, l, out.shape[-1])


def attend(q, k, v):
    scale = q.shape[-1] ** -0.5
    s = jnp.einsum('bhqd,bhsd->bhqs', q, k).astype(F32) * scale
    p = jax.nn.softmax(s, axis=-1).astype(v.dtype)
    return jnp.einsum('bhqs,bhsd->bhqd', p, v)


def diff_attend(q1, q2, k1, k2, v, lam):
    scale = q1.shape[-1] ** -0.5
    p1 = jax.nn.softmax(jnp.einsum('bhqd,bhsd->bhqs', q1, k1).astype(F32) * scale, axis=-1)
    p2 = jax.nn.softmax(jnp.einsum('bhqd,bhsd->bhqs', q2, k2).astype(F32) * scale, axis=-1)
    return jnp.einsum('bhqs,bhsd->bhqd', (p1 - lam * p2).astype(v.dtype), v)


def gqa_attend(q, k, v):
    b, hq, nq, d = q.shape
    hkv = k.shape[1]
    qg = q.reshape(b, hkv, hq // hkv, nq, d)
    s = jnp.einsum('bkgqd,bksd->bkgqs', qg, k).astype(F32) * (d ** -0.5)
    p = jax.nn.softmax(s, axis=-1).astype(v.dtype)
    return jnp.einsum('bkgqs,bksd->bkgqd', p, v).reshape(b, hq, nq, d)


def neighbourhood_attention(q, k, v, k_ctx, v_ctx, rpb):
    b, h, l, d = q.shape
    rows = l // GRID_W
    win_r = min(NA_WIN_R, rows)
    n_win = win_r * NA_WIN_C
    scale = d ** -0.5
    qg = q.reshape(b, h, rows, GRID_W, d)
    kg = k.reshape(b, h, rows, GRID_W, d)
    vg = v.reshape(b, h, rows, GRID_W, d)
    row_start = jnp.clip(jnp.arange(rows) - win_r // 2, 0, rows - win_r)
    col_ids = jnp.arange(GRID_W)
    col_start = jnp.clip(col_ids - NA_WIN_C // 2, 0, GRID_W - NA_WIN_C)
    col_idx = col_start[:, None] + jnp.arange(NA_WIN_C)
    col_off = col_idx - col_ids[:, None] + NA_WIN_C - 1
    rpb32 = rpb.astype(F32)

    def one_row(args):
        q_row, r, r0 = args
        k_rows = lax.dynamic_slice_in_dim(kg, r0, win_r, axis=2)
        v_rows = lax.dynamic_slice_in_dim(vg, r0, win_r, axis=2)
        k_win = k_rows[:, :, :, col_idx]
        v_win = v_rows[:, :, :, col_idx]
        row_off = r0 + jnp.arange(win_r) - r + NA_WIN_R - 1
        bias = rpb32[:, row_off[None, :, None], col_off[:, None, :]]
        s_win = jnp.einsum('bhwd,bhrwcd->bhwrc', q_row, k_win).astype(F32) * scale + bias
        s_ctx = jnp.einsum('bhwd,bhsd->bhws', q_row, k_ctx).astype(F32) * scale
        s = jnp.concatenate([s_win.reshape(b, h, GRID_W, n_win), s_ctx], axis=-1)
        p = jax.nn.softmax(s, axis=-1).astype(v.dtype)
        p_win = p[..., :n_win].reshape(b, h, GRID_W, win_r, NA_WIN_C)
        return (jnp.einsum('bhwrc,bhrwcd->bhwd', p_win, v_win)
                + jnp.einsum('bhws,bhsd->bhwd', p[..., n_win:], v_ctx))

    out = lax.map(one_row, (jnp.moveaxis(qg, 2, 0), jnp.arange(rows), row_start))
    return jnp.moveaxis(out, 0, 2).reshape(b, h, l, d)


def hyena_filter(length, w1, b1, w2, b2, w3, b3, w4, freq):
    t = jnp.linspace(0.0, 1.0, length, dtype=F32)[:, None]
    bands = (HY_EMB_DIM - 1) // 2
    w = 2.0 * math.pi * jnp.arange(length, dtype=F32)[:, None] / length
    f = jnp.linspace(1e-4, bands - 1, bands, dtype=F32)[None, :]
    z = jnp.concatenate([t, jnp.cos(f * w), -jnp.sin(f * w)], axis=-1)
    fr = freq.astype(F32)
    hdn = jnp.sin(fr * (z @ w1.astype(F32) + b1.astype(F32)))
    hdn = jnp.sin(fr * (hdn @ w2.astype(F32) + b2.astype(F32)))
    hdn = jnp.sin(fr * (hdn @ w3.astype(F32) + b3.astype(F32)))
    hdn = hdn @ w4.astype(F32)
    max_decay = math.log(HY_DECAY_TARGET) / HY_FAST_DECAY
    min_decay = math.log(HY_DECAY_TARGET) / HY_SLOW_DECAY
    deltas = jnp.linspace(min_decay, max_decay, HY_WIDTH, dtype=F32)
    decay = jnp.exp(-t * jnp.abs(deltas))
    h_fwd = hdn[:, :HY_WIDTH] * decay
    h_bwd = hdn[:, HY_WIDTH:] * decay
    k_two = jnp.concatenate([h_fwd, jnp.zeros((1, HY_WIDTH), F32), h_bwd[:0:-1]], axis=0)
    return k_two / jnp.sum(jnp.abs(k_two), axis=0, keepdims=True)


def hyena(u, conv_w, conv_b, w1, b1, w2, b2, w3, b3, w4, freq, skip):
    length = u.shape[1]
    u = dwconv3(u, conv_w, conv_b)
    x0, x1, val = jnp.split(u, 3, axis=-1)
    z = (val * x1).astype(F32)
    kf = jnp.fft.rfft(hyena_filter(length, w1, b1, w2, b2, w3, b3, w4, freq), n=2 * length, axis=0)
    zf = jnp.fft.rfft(z, n=2 * length, axis=1)
    y = jnp.fft.irfft(zf * kf, n=2 * length, axis=1)[:, :length]
    return (x0.astype(F32) * (y + skip.astype(F32) * z)).astype(u.dtype)


def even_mixer(h_l, h_c, w_in, w_out, na_qg, na_kg, rpb, da_qg, da_kg, lq1, lk1, lq2, lk2, subln,
               lam_init, cos, sin, with_ctx):
    offs = [NA_WIDTH, 2 * NA_WIDTH, 3 * NA_WIDTH, 3 * NA_WIDTH + DA_QK_WIDTH, 3 * NA_WIDTH + 2 * DA_QK_WIDTH]
    qa_l, ka_l, va_l, qb_l, kb_l, vb_l = jnp.split(h_l @ w_in, offs, axis=-1)
    qa_c, ka_c, va_c, qb_c, kb_c, vb_c = jnp.split(h_c @ w_in, offs, axis=-1)
    qa_l = rms_norm(split_heads(qa_l, NA_HEADS, HEAD_DIM), na_qg)
    ka_l = rms_norm(split_heads(ka_l, NA_HEADS, HEAD_DIM), na_kg)
    va_l = split_heads(va_l, NA_HEADS, HEAD_DIM)
    ka_c = rms_norm(split_heads(ka_c, NA_HEADS, HEAD_DIM), na_kg)
    va_c = split_heads(va_c, NA_HEADS, HEAD_DIM)
    ya_l = neighbourhood_attention(qa_l, ka_l, va_l, ka_c, va_c, rpb)
    lam = (jnp.exp(jnp.sum(lq1.astype(F32) * lk1.astype(F32)))
           - jnp.exp(jnp.sum(lq2.astype(F32) * lk2.astype(F32))) + lam_init)
    qb_l = apply_rope(rms_norm(split_heads(qb_l, 2 * DA_HEADS, HEAD_DIM), da_qg), cos, sin)
    kb_l = apply_rope(rms_norm(split_heads(kb_l, 2 * DA_HEADS, HEAD_DIM), da_kg), cos, sin)
    vb_l = split_heads(vb_l, DA_HEADS, DA_VDIM)
    kb_c = rms_norm(split_heads(kb_c, 2 * DA_HEADS, HEAD_DIM), da_kg)
    vb_c = split_heads(vb_c, DA_HEADS, DA_VDIM)
    k1_all = jnp.concatenate([kb_l[:, 0::2], kb_c[:, 0::2]], axis=2)
    k2_all = jnp.concatenate([kb_l[:, 1::2], kb_c[:, 1::2]], axis=2)
    v_all = jnp.concatenate([vb_l, vb_c], axis=2)
    yb_l = sweep_query_blocks(lambda q1, q2: diff_attend(q1, q2, k1_all, k2_all, v_all, lam),
                              qb_l[:, 0::2], qb_l[:, 1::2])
    y_l = jnp.concatenate([merge_heads(ya_l), merge_heads(rms_norm(yb_l, subln) * (1.0 - lam_init))],
                          axis=-1) @ w_out
    if not with_ctx:
        return y_l, None
    qa_c = rms_norm(split_heads(qa_c, NA_HEADS, HEAD_DIM), na_qg)
    ya_c = attend(qa_c, ka_c, va_c)
    qb_c = rms_norm(split_heads(qb_c, 2 * DA_HEADS, HEAD_DIM), da_qg)
    yb_c = diff_attend(qb_c[:, 0::2], qb_c[:, 1::2], kb_c[:, 0::2], kb_c[:, 1::2], vb_c, lam)
    y_c = jnp.concatenate([merge_heads(ya_c), merge_heads(rms_norm(yb_c, subln) * (1.0 - lam_init))],
                          axis=-1) @ w_out
    return y_l, y_c


def odd_mixer(h_l, h_c, w_in, w_out, qg, kg, conv_w, conv_b, w1, b1, w2, b2, w3, b3, w4, freq, skip,
              cos, sin, with_ctx):
    offs = [GQA_Q_WIDTH, GQA_Q_WIDTH + GQA_KV_WIDTH, GQA_Q_WIDTH + 2 * GQA_KV_WIDTH]
    q_l, k_l, v_l, hy_l = jnp.split(h_l @ w_in, offs, axis=-1)
    q_c, k_c, v_c, hy_c = jnp.split(h_c @ w_in, offs, axis=-1)
    q_l = apply_rope(rms_norm(split_heads(q_l, GQA_Q_HEADS, HEAD_DIM), qg), cos, sin)
    k_l = apply_rope(rms_norm(split_heads(k_l, GQA_KV_HEADS, HEAD_DIM), kg), cos, sin)
    v_l = split_heads(v_l, GQA_KV_HEADS, HEAD_DIM)
    k_c = rms_norm(split_heads(k_c, GQA_KV_HEADS, HEAD_DIM), kg)
    v_c = split_heads(v_c, GQA_KV_HEADS, HEAD_DIM)
    k_all = jnp.concatenate([k_l, k_c], axis=2)
    v_all = jnp.concatenate([v_l, v_c], axis=2)
    yc_l = sweep_query_blocks(lambda qb: gqa_attend(qb, k_all, v_all), q_l)
    yd_l = hyena(hy_l, conv_w, conv_b, w1, b1, w2, b2, w3, b3, w4, freq, skip)
    y_l = jnp.concatenate([merge_heads(yc_l), yd_l], axis=-1) @ w_out
    if not with_ctx:
        return y_l, None
    q_c = rms_norm(split_heads(q_c, GQA_Q_HEADS, HEAD_DIM), qg)
    yc_c = gqa_attend(q_c, k_c, v_c)
    yd_c = hyena(hy_c, conv_w, conv_b, w1, b1, w2, b2, w3, b3, w4, freq, skip)
    y_c = jnp.concatenate([merge_heads(yc_c), yd_c], axis=-1) @ w_out
    return y_l, y_c


def conv_ffn(h, w_up, conv_w, conv_b, w_down):
    g, v = jnp.split(h @ w_up, 2, axis=-1)
    g = dwconv3(g, conv_w, conv_b)
    return (jax.nn.silu(g) * v) @ w_down


def setup_inputs(seed: int = 0) -> dict:
    key = jax.random.key(seed)
    D = D_MODEL
    specs = [
        ('x', (BATCH, SEQ, D), 1.0, 0.0),
        ('c', (BATCH, D), 1.0, 0.0),
        ('ctx', (BATCH, CTX_LEN, D), 1.0, 0.0),
        ('c_ctx', (D,), 1.0, 0.0),
        ('w_ada', (DEPTH, D, 6 * D), D ** -0.5, 0.0),
        ('b_ada', (DEPTH, 6 * D), 0.02, 0.0),
        ('w_up', (DEPTH, D, 2 * FFN_HIDDEN), D ** -0.5, 0.0),
        ('ffn_conv_w', (DEPTH, 3, FFN_HIDDEN), 3 ** -0.5, 0.0),
        ('ffn_conv_b', (DEPTH, FFN_HIDDEN), 0.02, 0.0),
        ('w_down', (DEPTH, FFN_HIDDEN, D), FFN_HIDDEN ** -0.5, 0.0),
        ('w_in_e', (N_EVEN, D, EVEN_IN), D ** -0.5, 0.0),
        ('w_out_e', (N_EVEN, EVEN_MIX, D), EVEN_MIX ** -0.5, 0.0),
        ('na_q_gain', (N_EVEN, HEAD_DIM), 0.05, 1.0),
        ('na_k_gain', (N_EVEN, HEAD_DIM), 0.05, 1.0),
        ('na_rpb', (N_EVEN, NA_HEADS, 2 * NA_WIN_R - 1, 2 * NA_WIN_C - 1), 0.02, 0.0),
        ('da_q_gain', (N_EVEN, HEAD_DIM), 0.05, 1.0),
        ('da_k_gain', (N_EVEN, HEAD_DIM), 0.05, 1.0),
        ('da_lambda_q1', (N_EVEN, HEAD_DIM), 0.1, 0.0),
        ('da_lambda_k1', (N_EVEN, HEAD_DIM), 0.1, 0.0),
        ('da_lambda_q2', (N_EVEN, HEAD_DIM), 0.1, 0.0),
        ('da_lambda_k2', (N_EVEN, HEAD_DIM), 0.1, 0.0),
        ('da_subln_gain', (N_EVEN, DA_VDIM), 0.05, 1.0),
        ('w_in_o', (N_ODD, D, ODD_IN), D ** -0.5, 0.0),
        ('w_out_o', (N_ODD, ODD_MIX, D), ODD_MIX ** -0.5, 0.0),
        ('gqa_q_gain', (N_ODD, HEAD_DIM), 0.05, 1.0),
        ('gqa_k_gain', (N_ODD, HEAD_DIM), 0.05, 1.0),
        ('hy_conv_w', (N_ODD, 3, 3 * HY_WIDTH), 3 ** -0.5, 0.0),
        ('hy_conv_b', (N_ODD, 3 * HY_WIDTH), 0.02, 0.0),
        ('hy_w1', (N_ODD, HY_EMB_DIM, HY_FILTER_ORDER), HY_EMB_DIM ** -0.5, 0.0),
        ('hy_b1', (N_ODD, HY_FILTER_ORDER), HY_EMB_DIM ** -0.5, 0.0),
        ('hy_w2', (N_ODD, HY_FILTER_ORDER, HY_FILTER_ORDER), HY_FILTER_ORDER ** -0.5, 0.0),
        ('hy_b2', (N_ODD, HY_FILTER_ORDER), HY_FILTER_ORDER ** -0.5, 0.0),
        ('hy_w3', (N_ODD, HY_FILTER_ORDER, HY_FILTER_ORDER), HY_FILTER_ORDER ** -0.5, 0.0),
        ('hy_b3', (N_ODD, HY_FILTER_ORDER), HY_FILTER_ORDER ** -0.5, 0.0),
        ('hy_w4', (N_ODD, HY_FILTER_ORDER, 2 * HY_WIDTH), HY_FILTER_ORDER ** -0.5, 0.0),
        ('hy_freq', (N_ODD, HY_FILTER_ORDER), 0.05, 1.0),
        ('hy_skip', (N_ODD, HY_WIDTH), 1.0, 0.0),
    ]
    keys = jax.random.split(key, len(specs))
    return {name: off + scale * jax.random.normal(keys[i], shape, F32)
            for i, (name, shape, scale, off) in enumerate(specs)}


def reference(x, c, ctx, c_ctx, w_ada, b_ada, w_up, ffn_conv_w, ffn_conv_b, w_down, w_in_e, w_out_e,
              na_q_gain, na_k_gain, na_rpb, da_q_gain, da_k_gain, da_lambda_q1, da_lambda_k1, da_lambda_q2,
              da_lambda_k2, da_subln_gain, w_in_o, w_out_o, gqa_q_gain, gqa_k_gain, hy_conv_w, hy_conv_b,
              hy_w1, hy_b1, hy_w2, hy_b2, hy_w3, hy_b3, hy_w4, hy_freq, hy_skip):
    cos, sin = axial_rope(x.shape[1], HEAD_DIM)
    x_l, x_c = x, ctx
    for layer in range(DEPTH):
        with_ctx = layer < DEPTH - 1
        mod_l = (jax.nn.silu(c) @ w_ada[layer] + b_ada[layer])[:, None, :]
        mod_c = jax.nn.silu(c_ctx) @ w_ada[layer] + b_ada[layer]
        sh1_l, sc1_l, g1_l, sh2_l, sc2_l, g2_l = jnp.split(mod_l, 6, axis=-1)
        sh1_c, sc1_c, g1_c, sh2_c, sc2_c, g2_c = jnp.split(mod_c, 6, axis=-1)
        h_l = modulate(x_l, sh1_l, sc1_l)
        h_c = modulate(x_c, sh1_c, sc1_c)
        i = layer // 2
        if layer % 2 == 0:
            lam_init = 0.8 - 0.6 * math.exp(-0.3 * layer)
            y_l, y_c = even_mixer(h_l, h_c, w_in_e[i], w_out_e[i], na_q_gain[i], na_k_gain[i], na_rpb[i],
                                  da_q_gain[i], da_k_gain[i], da_lambda_q1[i], da_lambda_k1[i],
                                  da_lambda_q2[i], da_lambda_k2[i], da_subln_gain[i], lam_init, cos, sin,
                                  with_ctx)
        else:
            y_l, y_c = odd_mixer(h_l, h_c, w_in_o[i], w_out_o[i], gqa_q_gain[i], gqa_k_gain[i],
                                 hy_conv_w[i], hy_conv_b[i], hy_w1[i], hy_b1[i], hy_w2[i], hy_b2[i],
                                 hy_w3[i], hy_b3[i], hy_w4[i], hy_freq[i], hy_skip[i], cos, sin, with_ctx)
        x_l = x_l + g1_l * y_l
        x_l = x_l + g2_l * conv_ffn(modulate(x_l, sh2_l, sc2_l), w_up[layer], ffn_conv_w[layer],
                                    ffn_conv_b[layer], w_down[layer])
        if with_ctx:
            x_c = x_c + g1_c * y_c
            x_c = x_c + g2_c * conv_ffn(modulate(x_c, sh2_c, sc2_c), w_up[layer], ffn_conv_w[layer],
                                        ffn_conv_b[layer], w_down[layer])
    return x_l
```

```python
import contextlib
import math
import numpy as np
import concourse.bass as bass
import concourse.mybir as mybir
from concourse.bass_utils import run_bass_kernel_spmd

F32 = mybir.dt.float32
BF16 = mybir.dt.bfloat16
AF = mybir.ActivationFunctionType
ALU = mybir.AluOpType

L = 8192
C = 256
LT = L + C
D = 1024
DEPTH = 4
EPS = 1e-6
FH = 2816
NCH_F = 22
EP = 16000
KSLOT = 8
EPD = 200


class Tile:
    __slots__ = ("t", "w", "r", "name")

    def __init__(self, t, name=""):
        self.t = t
        self.w = None
        self.r = {}
        self.name = name

    def __getitem__(self, k):
        return self.t[k]


class LTile(Tile):
    __slots__ = ("base",)

    def __init__(self, t, name="", base=0):
        Tile.__init__(self, t, name)
        self.base = base

    def __getitem__(self, k):
        b = self.base
        if not isinstance(k, tuple):
            k = (k,)
        f = k[0]
        if isinstance(f, slice):
            f = slice(f.start - b, f.stop - b)
        else:
            f = f - b
        k = (f,) + tuple(k[1:])
        return self.t[k if len(k) > 1 else k[0]]


class K:
    def __init__(self, nc):
        self.nc = nc
        self.es = contextlib.ExitStack()
        self.eng = {"pe": nc.tensor, "act": nc.scalar, "dve": nc.vector, "pool": nc.gpsimd, "sp": nc.sync}
        self.cnt = {e: 0 for e in self.eng}
        self.csem = {e: [] for e in self.eng}
        self.seen = {e: {} for e in self.eng}
        self.dcnt = {}
        self.dsem = {}
        self.uid = 0
        self.phase_stack = None
        self.gen = 0

    def _name(self, p):
        self.uid += 1
        return "%s_%d" % (p, self.uid)

    def sb(self, shape, dt, name="t"):
        n = self._name(name)
        return Tile(self.phase_stack.enter_context(self.nc.sbuf_tensor(n, list(shape), dt)), n)

    def sb_global(self, shape, dt, name="g"):
        n = self._name(name)
        return Tile(self.es.enter_context(self.nc.sbuf_tensor(n, list(shape), dt)), n)

    def ps(self, name="ps"):
        n = self._name(name)
        return Tile(self.es.enter_context(self.nc.psum_tensor(n, [128, 512], F32)), n)

    def dram(self, name, shape, dt, kind="Internal"):
        return Tile(self.nc.dram_tensor(name, list(shape), dt, kind=kind).ap(), name)

    def reg(self, name="r"):
        return Tile(None, name)

    def _csem(self, e, idx):
        ep = (idx - 1) // EP
        lst = self.csem[e]
        while len(lst) <= ep:
            lst.append(self.es.enter_context(self.nc.semaphore(self._name("s" + e))))
        return lst[ep], (idx - 1) % EP + 1

    def _dsem(self, q, i):
        k = i % KSLOT
        u = i // KSLOT
        ep = u // EPD
        d = self.dsem.setdefault(q, {})
        key = (k, ep)
        if key not in d:
            d[key] = self.es.enter_context(self.nc.semaphore(self._name("d" + q)))
        return d[key], 16 * (u % EPD + 1)

    def _waits(self, e, deps):
        out = {}
        seen = self.seen[e]
        for tok in deps:
            if tok is None or tok[-1] != self.gen:
                continue
            if tok[0] == "c":
                _, de, idx, _g = tok
                if de == e and e in ("pe", "sp"):
                    continue
                if de == e:
                    if idx >= self.cnt[e] - 0 and False:
                        pass
                key = ("c", de)
                if seen.get(key, 0) >= idx:
                    continue
                if out.get(key, 0) < idx:
                    out[key] = idx
            else:
                _, q, i, _g = tok
                key = ("d", q, i % KSLOT)
                if seen.get(key, -1) >= i:
                    continue
                if out.get(key, -1) < i:
                    out[key] = i
        h = self.eng[e]
        for key, v in out.items():
            seen[key] = v
            if key[0] == "c":
                s, val = self._csem(key[1], v)
            else:
                s, val = self._dsem(key[1], v)
            h.wait_ge(s, val)

    def _deps(self, reads, writes):
        deps = []
        for t in reads:
            deps.append(t.w)
        for t in writes:
            deps.append(t.w)
            deps.extend(t.r.values())
        return deps

    def _mark(self, tok, reads, writes):
        for t in reads:
            if tok[0] == "c":
                t.r[("c", tok[1])] = tok
            else:
                t.r[("d", tok[1], tok[2] % KSLOT)] = tok
        for t in writes:
            t.w = tok
            t.r = {}

    def op(self, e, fn, reads=(), writes=()):
        self._waits(e, self._deps(reads, writes))
        ins = fn(self.eng[e])
        self.cnt[e] += 1
        idx = self.cnt[e]
        s, val = self._csem(e, idx)
        ins.then_inc(s, 1)
        self._mark(("c", e, idx, self.gen), reads, writes)

    def dma(self, q, out, in_, reads=(), writes=(), **kw):
        i = self.dcnt.get(q, 0)
        deps = self._deps(reads, writes)
        if i >= KSLOT:
            deps.append(("d", q, i - KSLOT, self.gen))
        self._waits(q, deps)
        s, val = self._dsem(q, i)
        self.eng[q].dma_start(out=out, in_=in_, **kw).then_inc(s, 16)
        self.dcnt[q] = i + 1
        self._mark(("d", q, i, self.gen), reads, writes)

    def barrier(self):
        toks = []
        for e in ("pe", "act", "dve", "pool"):
            if self.cnt[e]:
                toks.append(("c", e, self.cnt[e], self.gen))
        for q, n in self.dcnt.items():
            for i in range(max(0, n - KSLOT), n):
                toks.append(("d", q, i, self.gen))
        for e in self.eng:
            self._waits(e, [t for t in toks if not (t[0] == "c" and t[1] == e)])

    def fresh(self):
        self.barrier()
        self.gen += 1
        self.cnt = {e: 0 for e in self.eng}
        self.csem = {e: [] for e in self.eng}
        self.seen = {e: {} for e in self.eng}
        self.dcnt = {}
        self.dsem = {}

    @contextlib.contextmanager
    def phase(self):
        old = self.phase_stack
        self.phase_stack = contextlib.ExitStack()
        try:
            yield
        finally:
            self.barrier()
            self.phase_stack.close()
            self.phase_stack = old


def _rope_tables():
    t = np.arange(L, dtype=np.int32)
    row = (t // 64).astype(np.float32)
    col = (t % 64).astype(np.float32)
    inv = (np.float32(10000.0) ** (-np.arange(16, dtype=np.float32) / np.float32(16))).astype(np.float32)
    ang = np.concatenate([row[:, None] * inv, col[:, None] * inv], axis=-1).astype(np.float32)
    cos = np.cos(ang).astype(np.float32)
    sin = np.sin(ang).astype(np.float32)
    p = np.arange(128)
    i = (p % 64) // 2
    return np.ascontiguousarray(cos[:, i].T), np.ascontiguousarray(sin[:, i].T)


def _na_cls(qt):
    return 0 if qt == 0 else 1 if qt == 1 else 3 if qt == 62 else 4 if qt == 63 else 2


def _na_kt0(qt):
    return min(max(qt - 2, 0), 59)


def _na_tables(rpb):
    out = np.full((4, 5, 5, 128, 2, 128), -30000.0, np.float32)
    reps = {0: 0, 1: 1, 2: 2, 3: 62, 4: 63}
    p = np.arange(128)
    for cls, qt in reps.items():
        kt0 = _na_kt0(qt)
        r = 2 * qt + p // 64
        c = p % 64
        r0 = np.clip(r - 4, 0, 120)
        c0 = np.clip(c - 8, 0, 48)
        for rel in range(5):
            kt = kt0 + rel
            kr = 2 * kt + p // 64
            kc = p % 64
            valid = ((kr[:, None] >= r0[None, :]) & (kr[:, None] < r0[None, :] + 8)
                     & (kc[:, None] >= c0[None, :]) & (kc[:, None] < c0[None, :] + 16))
            ro = np.clip(kr[:, None] - r[None, :] + 7, 0, 14)
            co = np.clip(kc[:, None] - c[None, :] + 15, 0, 30)
            for h in range(8):
                g = rpb[h][ro, co]
                out[h // 2, cls, rel, :, h % 2, :] = np.where(valid, g, np.float32(-30000.0))
    return out


def _hy_tables():
    def zemb(n):
        t = np.linspace(0.0, 1.0, n, dtype=np.float32)[:, None]
        w = (np.float32(2.0 * math.pi) * np.arange(n, dtype=np.float32)[:, None] / np.float32(n)).astype(np.float32)
        f = np.linspace(1e-4, 15, 16, dtype=np.float32)[None, :]
        z = np.concatenate([t, np.cos(f * w), -np.sin(f * w)], axis=-1).astype(np.float32)
        return z, t[:, 0]
    zl, tl = zemb(L)
    zc, tc = zemb(C)
    zT = np.zeros((2, 33, L), np.float32)
    zT[0] = zl.T
    zT[1, :, :C] = zc.T
    tlin = np.full((2, L), 1.0e4, np.float32)
    tlin[0] = tl
    tlin[1, :C] = tc
    max_decay = math.log(1e-2) / 0.3
    min_decay = math.log(1e-2) / 1.5
    deltas = np.abs(np.linspace(min_decay, max_decay, 512, dtype=np.float32))
    ndel = np.ascontiguousarray((-deltas).reshape(4, 128).T)
    n = np.arange(128, dtype=np.float64)
    a = 2.0 * math.pi * np.outer(n, n) / 128.0
    fc = np.cos(a).astype(np.float32)
    fs = (-np.sin(a)).astype(np.float32)
    a2 = 2.0 * math.pi * np.outer(n, n) / 16384.0
    tc_ = np.cos(a2).astype(np.float32)
    ts_ = (-np.sin(a2)).astype(np.float32)
    return zT, tlin, ndel, fc, fs, tc_, ts_


def _consts():
    ident = np.eye(128, dtype=np.float32)
    bd = np.zeros((128, 128), np.float32)
    bd[:64, :64] = 1.0 / 64
    bd[64:, 64:] = 1.0 / 64
    o128 = np.full((128, 128), 1.0 / 128, np.float32)
    rot = np.zeros((128, 128), np.float32)
    for i in range(64):
        rot[2 * i + 1, 2 * i] = -1.0
        rot[2 * i, 2 * i + 1] = 1.0
    return np.stack([ident, bd, o128, rot, np.ones((128, 128), np.float32)])


SH1, SC1, SH2, SC2 = 0, 1, 2, 3
_PER4 = ("w_ada", "b_ada", "w_up", "ffn_conv_w", "ffn_conv_b", "w_down")
_PER2E = ("w_in_e", "w_out_e", "na_q_gain", "na_k_gain", "da_q_gain", "da_k_gain", "da_lambda_q1", "da_lambda_k1",
          "da_lambda_q2", "da_lambda_k2", "da_subln_gain", "na_tab")
_PER2O = ("w_in_o", "w_out_o", "gqa_q_gain", "gqa_k_gain", "hy_conv_w", "hy_conv_b", "hy_w1", "hy_b1", "hy_w2", "hy_b2",
          "hy_w3", "hy_b3", "hy_w4", "hy_freq", "hy_skip")
_PER2 = _PER2E + _PER2O


class Prog:
    def __init__(self, n_layers=DEPTH, debug=False, single=None):
        self.debug = debug
        self.n_layers = n_layers
        self.single = single
        if single is not None:
            self.layers = [single]
        nc = bass.Bass("TRN2", target_bir_lowering=False)
        self.nc = nc
        k = K(nc)
        self.k = k
        def I(name, shape, dt=F32):
            if single is not None and name in _PER4:
                return LTile(nc.dram_tensor(name, [1] + list(shape[1:]), dt, kind="ExternalInput").ap(), name, single)
            if single is not None and name in _PER2:
                return LTile(nc.dram_tensor(name, [1] + list(shape[1:]), dt, kind="ExternalInput").ap(), name, single // 2)
            return k.dram(name, shape, dt, kind="ExternalInput")
        self.x = I("x", [L, D])
        self.ctx = I("ctx", [C, D])
        self.cc = I("cc", [2, D])
        self.w_ada = I("w_ada", [4, D, 6 * D])
        self.b_ada = I("b_ada", [4, 6 * D])
        self.w_up = I("w_up", [4, D, 2 * FH])
        self.fcw = I("ffn_conv_w", [4, 3, FH])
        self.fcb = I("ffn_conv_b", [4, FH])
        self.w_down = I("w_down", [4, FH, D])
        self.w_in_e = I("w_in_e", [2, D, 3072])
        self.w_out_e = I("w_out_e", [2, D, D])
        self.w_in_o = I("w_in_o", [2, D, 2304])
        self.w_out_o = I("w_out_o", [2, D, D])
        self.vec64 = {}
        for n in ("na_q_gain", "na_k_gain", "da_q_gain", "da_k_gain", "da_lambda_q1", "da_lambda_k1",
                  "da_lambda_q2", "da_lambda_k2", "gqa_q_gain", "gqa_k_gain", "hy_b1", "hy_b2", "hy_b3", "hy_freq"):
            self.vec64[n] = I(n, [2, 64])
        self.subln = I("da_subln_gain", [2, 128])
        self.hy_conv_w = I("hy_conv_w", [2, 3, 1536])
        self.hy_conv_b = I("hy_conv_b", [2, 1536])
        self.hy_w1 = I("hy_w1", [2, 33, 64])
        self.hy_w2 = I("hy_w2", [2, 64, 64])
        self.hy_w3 = I("hy_w3", [2, 64, 64])
        self.hy_w4 = I("hy_w4", [2, 64, 1024])
        self.hy_skip = I("hy_skip", [2, 512])
        self.rope_cos = I("rope_cos", [128, L])
        self.rope_sin = I("rope_sin", [128, L])
        self.na_tab = I("na_tab", [2, 4, 25, 128, 256])
        self.hy_zT = I("hy_zT", [2, 33, L])
        self.hy_tlin = I("hy_tlin", [2, L])
        self.hy_ndel = I("hy_ndel", [128, 4])
        self.dft = I("dft", [4, 128, 128])
        self.consts_d = I("consts", [5, 128, 128])
        self.y = k.dram("y", [L, D], F32, kind="ExternalOutput")
        sk = "ExternalOutput" if debug else "Internal"
        self.XR = k.dram("XR", [LT, D], F32, kind="ExternalOutput" if (debug or single is not None) else "Internal")
        self.QT = k.dram("QT", [1024, LT], BF16, kind="Internal")
        self.KT = k.dram("KT", [1024, LT], BF16, kind="Internal")
        self.V = k.dram("V", [LT, 1024], BF16, kind="Internal")
        self.YT = k.dram("YT", [1024, LT], BF16, kind=sk)
        self.X0T = k.dram("X0T", [1024, L], F32, kind="Internal")
        self.ZT = k.dram("ZT", [1024, L], F32, kind="Internal")
        self.HF = k.dram("HF", [2, 1024, L], F32, kind="Internal")
        self.YH = k.dram("YH", [1024, L], F32, kind="Internal")
        self.r_xr = [k.reg("xr%d" % i) for i in range(LT // 128)]
        self.r_q = k.reg("q")
        self.r_k = k.reg("k")
        self.r_v = k.reg("v")
        self.r_y = k.reg("y")
        self.r_in = k.reg("in")
        self.psb = [k.ps() for _ in range(8)]
        self.cst = k.sb_global([128, 5, 128], F32, "cst")
        k.dma("sp", self.cst[:], self.consts_d[:].rearrange("a p n -> p a n"), writes=[self.cst])
        self.ident = self.cst[:, 0, :]
        self.bd64 = self.cst[:, 1, :]
        self.o128 = self.cst[:, 2, :]
        self.rot = self.cst[:, 3, :]
        self.ones32 = self.cst[:, 4, :]
        self.epsc = k.sb_global([128, 1], F32, "epsc")
        k.op("dve", lambda e: e.memset(self.epsc[:], EPS), writes=[self.epsc])
        self.modT = k.sb_global([128, 4, 8, 2], F32, "modT")
        self.G = [k.sb_global([128, D], F32, "gate") for _ in range(4)]

    def xr_rows(self, r0, n):
        return self.XR[r0:r0 + n, :]

    def copy_in(self):
        k = self.k
        for i in range(0, L, 1024):
            k.dma("sp", self.XR[i:i + 1024, :], self.x[i:i + 1024, :],
                  writes=self.r_xr[i // 128:(i + 1024) // 128])
        k.dma("sp", self.XR[L:LT, :], self.ctx[:, :], writes=self.r_xr[64:66])

    def phase_ada(self, layer):
        k = self.k
        with k.phase():
            sT = k.sb([128, 2, 8], F32, "sT")
            for j in range(2):
                k.dma("sp", sT[:, j, :], self.cc[j:j + 1, :].rearrange("o (kc p) -> p (o kc)", p=128), writes=[sT],
                      allow_slow_non_contiguous=True)
            k.op("act", lambda e: e.activation(out=sT[:], in_=sT[:], func=AF.Silu), reads=[sT], writes=[sT])
            srep = [k.sb([128, 8, 128], F32, "srep") for _ in range(2)]
            for j in range(2):
                for kc in range(8):
                    k.op("dve", lambda e, j=j, kc=kc: e.tensor_copy(out=srep[j][:, kc, :],
                                                                     in_=sT[:, j, kc:kc + 1].to_broadcast([128, 128])),
                         reads=[sT], writes=[srep[j]])
            brow = k.sb([1, 6 * D], F32, "brow")
            k.dma("sp", brow[:], self.b_ada[layer:layer + 1, :], writes=[brow])
            one1 = k.sb([1, 128], F32, "one1")
            k.op("dve", lambda e: e.memset(one1[:], 1.0), writes=[one1])
            wb = [k.sb([128, 8, 512], F32, "wada") for _ in range(2)]
            pcol = self.psb[0]
            roles = [("col", SH1, 0), ("col", SH1, 4), ("col", SC1, 0), ("col", SC1, 4),
                     ("g", 0, 0), ("g", 0, 512), ("col", SH2, 0), ("col", SH2, 4),
                     ("col", SC2, 0), ("col", SC2, 4), ("g", 2, 0), ("g", 2, 512)]
            for blk in range(12):
                w = wb[blk % 2]
                n0 = blk * 512
                k.dma("sp", w[:], self.w_ada[layer, :, n0:n0 + 512].rearrange("(kc p) n -> p kc n", p=128),
                      writes=[w])
                kind, a, b = roles[blk]
                if kind == "col":
                    for c4 in range(4):
                        ch = b + c4
                        col = (a * 8 + ch) * 2
                        for kc in range(8):
                            k.op("pe", lambda e, kc=kc, c4=c4, col=col, w=w: e.matmul(
                                pcol[:, col:col + 2], lhsT=w[:, kc, c4 * 128:(c4 + 1) * 128], rhs=sT[:, :, kc],
                                start=(kc == 0), stop=False), reads=[w, sT], writes=[pcol])
                        k.op("pe", lambda e, c4=c4, col=col, n0=n0: e.matmul(
                            pcol[:, col:col + 2], lhsT=brow[0:1, n0 + c4 * 128:n0 + (c4 + 1) * 128], rhs=one1[0:1, 0:2],
                            start=False, stop=True), reads=[brow, one1], writes=[pcol])
                else:
                    for j in range(2):
                        pg = self.psb[1 + j]
                        for kc in range(8):
                            k.op("pe", lambda e, kc=kc, j=j, pg=pg, w=w: e.matmul(
                                pg[:, :], lhsT=srep[j][:, kc, :], rhs=w[:, kc, :], start=(kc == 0), stop=False),
                                reads=[w, srep[j]], writes=[pg])
                        k.op("pe", lambda e, pg=pg, n0=n0: e.matmul(
                            pg[:, :], lhsT=one1[0:1, :], rhs=brow[0:1, n0:n0 + 512], start=False, stop=True),
                            reads=[brow, one1], writes=[pg])
                        gt = self.G[a + j]
                        k.op("act", lambda e, pg=pg, gt=gt, b=b: e.activation(out=gt[:, b:b + 512], in_=pg[:, :], func=AF.Copy),
                             reads=[pg], writes=[gt])
            mflat = self.modT[:].rearrange("p a c j -> p (a c j)")
            k.op("dve", lambda e: e.tensor_copy(out=mflat, in_=pcol[:, 0:64]), reads=[pcol], writes=[self.modT])
            for a in (SC1, SC2):
                v = self.modT[:, a, :, :]
                k.op("dve", lambda e, v=v: e.tensor_scalar(out=v, in0=v, scalar1=1.0, scalar2=None, op0=ALU.add),
                     reads=[self.modT], writes=[self.modT])

    def load_weight(self, dst, src2d, N, stage, bw=512, engs=("dve", "pool")):
        k = self.k
        for bi, n0 in enumerate(range(0, N, bw)):
            n = min(bw, N - n0)
            st = stage[bi % len(stage)]
            k.dma("sp", st[:, :, :n], src2d[:, n0:n0 + n].rearrange("(kc p) n -> p kc n", p=128), writes=[st])
            k.op(engs[bi % len(engs)], lambda e, st=st, n0=n0, n=n: e.tensor_copy(out=dst[:, :, n0:n0 + n], in_=st[:, :, :n]),
                 reads=[st], writes=[dst])

    def col64x2(self, dst, j, src_row):
        k = self.k
        for h in range(2):
            k.dma("sp", dst[h * 64:(h + 1) * 64, j:j + 1], src_row.rearrange("o d -> d o"), writes=[dst],
                  allow_slow_non_contiguous=True)

    def mk_build(self, a_sh, a_sc):
        k = self.k
        xts = [k.sb([128, D], F32, "xt") for _ in range(2)]
        xns = [k.sb([128, D], F32, "xn") for _ in range(2)]
        junk = k.sb([128, D], BF16, "junk")
        ssqs = [k.sb([128, 1], F32, "ssq") for _ in range(2)]
        rstds = [k.sb([128, 1], F32, "rstd") for _ in range(2)]
        state = {"i": 0}

        def build(ht, col0, r0, lc, n=128):
            i = state["i"]
            state["i"] += 1
            xt, xn, ssq, rstd = xts[i % 2], xns[i % 2], ssqs[i % 2], rstds[i % 2]
            k.dma("sp", xt[:n, :], self.XR[r0:r0 + n, :], writes=[xt])
            k.op("act", lambda e: e.activation(out=junk[:n, :], in_=xt[:n, :], func=AF.Square, accum_out=ssq[:n, 0:1]),
                 reads=[xt], writes=[junk, ssq])
            k.op("dve", lambda e: e.tensor_scalar(out=rstd[:n, :], in0=ssq[:n, :], scalar1=1.0 / D, scalar2=EPS,
                                                  op0=ALU.mult, op1=ALU.add), reads=[ssq], writes=[rstd])
            k.op("act", lambda e: e.activation(out=rstd[:n, :], in_=rstd[:n, :], func=AF.Sqrt), reads=[rstd], writes=[rstd])
            k.op("dve", lambda e: e.reciprocal(out=rstd[:n, :], in_=rstd[:n, :]), reads=[rstd], writes=[rstd])
            k.op("dve", lambda e: e.tensor_scalar(out=xn[:n, :], in0=xt[:n, :], scalar1=rstd[:n, 0:1], scalar2=None, op0=ALU.mult),
                 reads=[xt, rstd], writes=[xn])
            for half in range(2):
                pt = self.psb[half]
                for j in range(4):
                    kc = half * 4 + j
                    k.op("pe", lambda e, j=j, kc=kc, pt=pt: e.transpose(out=pt[:, j * 128:j * 128 + n], in_=xn[:n, kc * 128:(kc + 1) * 128],
                                                                        identity=self.ident[:n, :n]), reads=[xn, self.cst], writes=[pt])
                for j in range(4):
                    kc = half * 4 + j
                    k.op("act", lambda e, j=j, kc=kc, pt=pt: e.activation(
                        out=ht[:, kc, col0:col0 + n], in_=pt[:, j * 128:j * 128 + n], func=AF.Identity,
                        scale=self.modT[:, a_sc, kc, lc:lc + 1], bias=self.modT[:, a_sh, kc, lc:lc + 1]),
                        reads=[pt, self.modT], writes=[ht])
        return build

    def phase_in(self, layer):
        k = self.k
        odd = layer % 2
        li = layer // 2
        NIN = 2304 if odd else 3072
        wsrc = (self.w_in_o if odd else self.w_in_e)[li]
        with k.phase():
            W = k.sb([128, 8, NIN], BF16, "win")
            stage = [k.sb([128, 8, 512], F32, "stg") for _ in range(2)]
            self.load_weight(W, wsrc, NIN, stage)
            gcol = k.sb([128, 4], F32, "gcol")
            names = ["gqa_q_gain", "gqa_k_gain"] if odd else ["na_q_gain", "na_k_gain", "da_q_gain", "da_k_gain"]
            for j, nme in enumerate(names):
                self.col64x2(gcol, j, self.vec64[nme][li:li + 1, :])
            if odd:
                fm = [(c * 128, 0, True, self.QT, c * 128) for c in range(4)] + [(512, 1, True, self.KT, 0)]
                vblocks = [(640, 128, 0)]
                cw = k.sb([128, 3, 12], F32, "hcw")
                cb = k.sb([128, 12], F32, "hcb")
                for j in range(3):
                    k.dma("sp", cw[:, j, :], self.hy_conv_w[li, j:j + 1, :].rearrange("o (c p) -> p (o c)", p=128), writes=[cw],
                          allow_slow_non_contiguous=True)
                k.dma("sp", cb[:], self.hy_conv_b[li:li + 1, :].rearrange("o (c p) -> p (o c)", p=128), writes=[cb],
                      allow_slow_non_contiguous=True)
            else:
                fm = ([(c * 128, 0, False, self.QT, c * 128) for c in range(4)]
                      + [(512 + c * 128, 1, False, self.KT, c * 128) for c in range(4)]
                      + [(1536 + c * 128, 2, True, self.QT, 512 + c * 128) for c in range(4)]
                      + [(2048 + c * 128, 3, True, self.KT, 512 + c * 128) for c in range(4)])
                vblocks = [(1024, 512, 0), (2560, 512, 512)]
            build = self.mk_build(SH1, SC1)
            hts = [k.sb([128, 8, 514], BF16, "ht") for _ in range(2)]
            css = [k.sb([128, 2, 512], F32, "cs") for _ in range(2)]
            mk = lambda shape, dt, nm, n=2: [k.sb(shape, dt, nm) for _ in range(n)]
            sqs, rss, qns, t1s, t2s = (mk([128, 512], F32, x) for x in ("sq", "rs", "qn", "t1", "t2"))
            obs = mk([128, 512], BF16, "ob", 3)
            vts = mk([128, 512], BF16, "vt", 3)
            if odd:
                us = mk([128, 514], F32, "u", 3)
                accs = mk([128, 512], F32, "acc", 3)
            groups = [(g * 512, 512, 0) for g in range(16)] + [(L, 256, 1)]
            ctr = {"c": 0}

            def build_group(gi):
                r0, nt, lc = groups[gi]
                ht = hts[gi % 2]
                for i in range(nt // 128):
                    build(ht, 1 + i * 128, r0 + i * 128, lc)
                if odd:
                    prev = hts[(gi - 1) % 2]
                    if gi == 0 or lc == 1:
                        k.op("pool", lambda e: e.memset(ht[:, :, 0:1], 0.0), writes=[ht])
                    else:
                        k.op("pool", lambda e: e.tensor_copy(out=ht[:, :, 0:1], in_=prev[:, :, 512:513]), reads=[prev], writes=[ht])
                    if lc == 1:
                        k.op("pool", lambda e: e.memset(ht[:, :, nt + 1:nt + 2], 0.0), writes=[ht])
                        k.op("pool", lambda e: e.memset(prev[:, :, 513:514], 0.0), writes=[prev])
                    elif gi > 0:
                        k.op("pool", lambda e: e.tensor_copy(out=prev[:, :, 513:514], in_=ht[:, :, 1:2]), reads=[ht], writes=[prev])

            def project(gi):
                r0, nt, lc = groups[gi]
                ht = hts[gi % 2]
                cs = css[gi % 2]
                if lc == 0:
                    k.dma("sp", cs[:, 0, :], self.rope_cos[:, r0:r0 + nt], writes=[cs])
                    k.dma("sp", cs[:, 1, :], self.rope_sin[:, r0:r0 + nt], writes=[cs])
                for (wc, gk, rope, dst, drow) in fm:
                    c = ctr["c"]
                    ctr["c"] += 1
                    pp, pm, pr = self.psb[2 + c % 2], self.psb[4 + c % 2], self.psb[6 + c % 2]
                    sq, rs, qn, t1, t2, ob = sqs[c % 2], rss[c % 2], qns[c % 2], t1s[c % 2], t2s[c % 2], obs[c % 3]
                    for kc in range(8):
                        k.op("pe", lambda e, kc=kc: e.matmul(pp[:, :nt], lhsT=W[:, kc, wc:wc + 128], rhs=ht[:, kc, 1:1 + nt],
                                                              start=(kc == 0), stop=(kc == 7)), reads=[W, ht], writes=[pp])
                    k.op("act", lambda e: e.activation(out=sq[:, :nt], in_=pp[:, :nt], func=AF.Square), reads=[pp], writes=[sq])
                    k.op("pe", lambda e: e.matmul(pm[:, :nt], lhsT=self.bd64, rhs=sq[:, :nt], start=True, stop=True),
                         reads=[sq, self.cst], writes=[pm])
                    k.op("act", lambda e: e.activation(out=rs[:, :nt], in_=pm[:, :nt], func=AF.Sqrt, bias=self.epsc[:, 0:1]),
                         reads=[pm, self.epsc], writes=[rs])
                    k.op("dve", lambda e: e.reciprocal(out=rs[:, :nt], in_=rs[:, :nt]), reads=[rs], writes=[rs])
                    do_rope = rope and lc == 0
                    qo = qn if do_rope else ob
                    k.op("dve", lambda e: e.scalar_tensor_tensor(out=qo[:, :nt], in0=pp[:, :nt], scalar=gcol[:, gk:gk + 1],
                                                                 in1=rs[:, :nt], op0=ALU.mult, op1=ALU.mult),
                         reads=[pp, rs, gcol], writes=[qo])
                    if do_rope:
                        k.op("pe", lambda e: e.matmul(pr[:, :nt], lhsT=self.rot, rhs=qn[:, :nt], start=True, stop=True),
                             reads=[qn, self.cst], writes=[pr])
                        k.op("pool", lambda e: e.tensor_tensor(out=t1[:, :nt], in0=qn[:, :nt], in1=cs[:, 0, :nt], op=ALU.mult),
                             reads=[qn, cs], writes=[t1])
                        k.op("dve", lambda e: e.tensor_tensor(out=t2[:, :nt], in0=pr[:, :nt], in1=cs[:, 1, :nt], op=ALU.mult),
                             reads=[pr, cs], writes=[t2])
                        k.op("pool", lambda e: e.tensor_tensor(out=ob[:, :nt], in0=t1[:, :nt], in1=t2[:, :nt], op=ALU.add),
                             reads=[t1, t2], writes=[ob])
                    k.dma("pool", dst[drow:drow + 128, r0:r0 + nt], ob[:, :nt], reads=[ob])
                for (wc, wn, vcol) in vblocks:
                    for i in range(nt // 128):
                        c = ctr["c"]
                        ctr["c"] += 1
                        pp = self.psb[2 + c % 2]
                        vt = vts[c % 3]
                        for kc in range(8):
                            k.op("pe", lambda e, kc=kc: e.matmul(pp[:, :wn], lhsT=ht[:, kc, 1 + i * 128:1 + (i + 1) * 128],
                                                                  rhs=W[:, kc, wc:wc + wn], start=(kc == 0), stop=(kc == 7)),
                                 reads=[W, ht], writes=[pp])
                        k.op("act", lambda e: e.activation(out=vt[:, :wn], in_=pp[:, :wn], func=AF.Copy), reads=[pp], writes=[vt])
                        k.dma("pool", self.V[r0 + i * 128:r0 + (i + 1) * 128, vcol:vcol + wn], vt[:, :wn], reads=[vt])
                if odd:
                    hw = (nt + 2) // 2
                    zrow = 0 if lc == 0 else 512
                    tcol = r0 if lc == 0 else 0
                    for c4 in range(4):
                        cv = []
                        for part in range(3):
                            ch = part * 4 + c4
                            wc = 768 + ch * 128
                            c = ctr["c"]
                            ctr["c"] += 1
                            pa, pb = self.psb[2 + 2 * (c % 3)], self.psb[3 + 2 * (c % 3)]
                            u, acc = us[c % 3], accs[c % 3]
                            for hlf, pz in ((0, pa), (1, pb)):
                                for kc in range(8):
                                    k.op("pe", lambda e, kc=kc, hlf=hlf, pz=pz: e.matmul(
                                        pz[:, :hw], lhsT=W[:, kc, wc:wc + 128], rhs=ht[:, kc, hlf * hw:(hlf + 1) * hw],
                                        start=(kc == 0), stop=(kc == 7)), reads=[W, ht], writes=[pz])
                                k.op("act", lambda e, hlf=hlf, pz=pz: e.activation(out=u[:, hlf * hw:(hlf + 1) * hw], in_=pz[:, :hw],
                                                                                   func=AF.Copy), reads=[pz], writes=[u])
                            k.op("dve", lambda e: e.tensor_scalar(out=acc[:, :nt], in0=u[:, 0:nt], scalar1=cw[:, 0, ch:ch + 1],
                                                                  scalar2=cb[:, ch:ch + 1], op0=ALU.mult, op1=ALU.add),
                                 reads=[u, cw, cb], writes=[acc])
                            for j in (1, 2):
                                k.op("dve", lambda e, j=j: e.scalar_tensor_tensor(out=acc[:, :nt], in0=u[:, j:j + nt],
                                                                                  scalar=cw[:, j, ch:ch + 1], in1=acc[:, :nt],
                                                                                  op0=ALU.mult, op1=ALU.add),
                                     reads=[u, cw, acc], writes=[acc])
                            cv.append(acc)
                        k.dma("pool", self.X0T[zrow + c4 * 128:zrow + (c4 + 1) * 128, tcol:tcol + nt], cv[0][:, :nt], reads=[cv[0]])
                        k.op("pool", lambda e: e.tensor_tensor(out=cv[2][:, :nt], in0=cv[2][:, :nt], in1=cv[1][:, :nt], op=ALU.mult),
                             reads=[cv[1], cv[2]], writes=[cv[2]])
                        k.dma("pool", self.ZT[zrow + c4 * 128:zrow + (c4 + 1) * 128, tcol:tcol + nt], cv[2][:, :nt], reads=[cv[2]])

            build_group(0)
            for gi in range(len(groups)):
                if gi + 1 < len(groups):
                    build_group(gi + 1)
                project(gi)

    def attn_unit(self, S, keys, PV, Wd, pts, st, hook=None):
        k = self.k
        nk = len(keys)

        def scores(ki):
            kt, _ = keys[ki]
            ps = self.psb[ki % 3]
            for (kfn, ktile, rhs, rtile, c0, n) in S:
                k.op("pe", lambda e, kfn=kfn, rhs=rhs, c0=c0, n=n: e.matmul(ps[:, c0:c0 + n], lhsT=kfn(kt), rhs=rhs,
                                                                           start=True, stop=True),
                     reads=[ktile, rtile], writes=[ps])

        scores(0)
        if nk > 1:
            scores(1)
        for ki in range(nk):
            kt, bias = keys[ki]
            ps = self.psb[ki % 3]
            pt = pts[st["p"] % len(pts)]
            st["p"] += 1
            if bias is not None:
                btile, bap = bias
                tmp = st["tmp"][ki % 2]
                k.op("dve", lambda e, bap=bap, tmp=tmp: e.scalar_tensor_tensor(out=tmp[:, :Wd], in0=ps[:, :Wd], scalar=0.125, in1=bap,
                                                                               op0=ALU.mult, op1=ALU.add),
                     reads=[ps, btile], writes=[tmp])
                k.op("act", lambda e, tmp=tmp: e.activation(out=pt[:, :Wd], in_=tmp[:, :Wd], func=AF.Exp), reads=[tmp], writes=[pt])
            else:
                k.op("act", lambda e: e.activation(out=pt[:, :Wd], in_=ps[:, :Wd], func=AF.Exp, scale=0.125), reads=[ps], writes=[pt])
            for (vfn, vtile, c0, n, acc_ap, acct) in PV:
                k.op("pe", lambda e, vfn=vfn, c0=c0, n=n, acc_ap=acc_ap: e.matmul(acc_ap, lhsT=vfn(kt), rhs=pt[:, c0:c0 + n],
                                                                                 start=(ki == 0), stop=(ki == nk - 1)),
                     reads=[vtile, pt], writes=[acct])
            if hook is not None:
                hook(ki, pt)
            if ki + 2 < nk:
                scores(ki + 2)

    def load_vx(self, vx, c0, w, ones):
        k = self.k
        if ones:
            k.op("pool", lambda e: e.memset(vx[:, :, w:128], 1.0), writes=[vx])
        for a in range(0, 66, 11):
            k.dma("sp", vx[:, a:a + 11, 0:w], self.V[a * 128:(a + 11) * 128, c0:c0 + w].rearrange("(kt p) d -> p kt d", p=128),
                  writes=[vx])

    def epi_shift(self, pieces, Wd, dshs, yts, ui, dst_ap):
        k = self.k
        dsh, yt = dshs[ui % 2], yts[ui % 2]
        for (acct, a0, o0, n) in pieces:
            k.op("dve", lambda e, acct=acct, a0=a0, o0=o0, n=n: e.tensor_copy(out=dsh[0:64, o0:o0 + n], in_=acct[64:128, a0:a0 + n]),
                 reads=[acct], writes=[dsh])
        k.op("dve", lambda e: e.reciprocal(out=dsh[0:64, :Wd], in_=dsh[0:64, :Wd]), reads=[dsh], writes=[dsh])
        for (acct, a0, o0, n) in pieces:
            k.op("dve", lambda e, acct=acct, a0=a0, o0=o0, n=n: e.tensor_tensor(out=yt[0:64, o0:o0 + n], in0=acct[0:64, a0:a0 + n],
                                                                              in1=dsh[0:64, o0:o0 + n], op=ALU.mult),
                 reads=[acct, dsh], writes=[yt])
        k.dma("pool", dst_ap, yt[0:64, :Wd], reads=[yt])

    def phase_gqa(self, layer, with_ctx):
        k = self.k
        with k.phase():
            pts = [k.sb([128, 512], BF16, "pt") for _ in range(3)]
            dshs = [k.sb([64, 512], F32, "dsh") for _ in range(2)]
            yts = [k.sb([64, 512], BF16, "yt") for _ in range(2)]
            qs = [k.sb([64, 512], BF16, "q") for _ in range(3)]
            st = {"p": 0}
            ui = 0
            for g in range(2):
              with k.phase():
                ktg = k.sb([64, LT], BF16, "ktg")
                vx = k.sb([128, 66, 128], BF16, "vx")
                k.dma("sp", ktg[:, :], self.KT[g * 64:(g + 1) * 64, :], writes=[ktg])
                self.load_vx(vx, g * 64, 64, True)
                units = [(qb * 128, list(range(66))) for qb in range(64)]
                if with_ctx:
                    units += [(L + qb * 128, [64, 65]) for qb in range(2)]
                for (c0, kts) in units:
                    q = qs[ui % 3]
                    k.dma("sp", q[:, :].rearrange("d (h t) -> d h t", h=4),
                          self.QT[g * 256:(g + 1) * 256, c0:c0 + 128].rearrange("(h d) t -> d h t", d=64), writes=[q])
                    acct = self.psb[3 + ui % 2]
                    S = [(lambda kt: ktg[:, kt * 128:(kt + 1) * 128], ktg, q[:, :], q, 0, 512)]
                    PV = [(lambda kt: vx[:, kt, :], vx, 0, 512, acct[:, 0:512], acct)]
                    self.attn_unit(S, [(kt, None) for kt in kts], PV, 512, pts, st)
                    dst = self.YT[g * 256:(g + 1) * 256, c0:c0 + 128].rearrange("(h d) t -> d h t", d=64)
                    self.epi_shift([(acct, 0, 0, 512)], 512, dshs, yts, ui, dst)
                    ui += 1

    def phase_na(self, layer, with_ctx):
        k = self.k
        li = layer // 2
        with k.phase():
            pts = [k.sb([128, 512], BF16, "pt") for _ in range(3)]
            dshs = [k.sb([64, 512], F32, "dsh") for _ in range(2)]
            yts = [k.sb([64, 512], BF16, "yt") for _ in range(2)]
            qs = [k.sb([64, 256], BF16, "q") for _ in range(3)]
            st = {"p": 0, "tmp": [k.sb([128, 256], F32, "tmp") for _ in range(2)]}
            ui = 0
            for hg in range(4):
              with k.phase():
                kth = [k.sb([64, LT], BF16, "kth") for _ in range(2)]
                vxs = [k.sb([128, 66, 128], BF16, "vx") for _ in range(2)]
                tab = k.sb([128, 25, 256], F32, "tab")
                k.dma("sp", tab[:], self.na_tab[li, hg, :, :, :].rearrange("t p n -> p t n"), writes=[tab])
                for i in range(2):
                    h = 2 * hg + i
                    k.dma("sp", kth[i][:, :], self.KT[h * 64:(h + 1) * 64, :], writes=[kth[i]])
                    self.load_vx(vxs[i], h * 64, 64, True)
                units = []
                for qt in range(64):
                    cls, kt0 = _na_cls(qt), _na_kt0(qt)
                    units.append((qt * 128, [(kt0 + r, (tab, tab[:, cls * 5 + r, :])) for r in range(5)] + [(64, None), (65, None)]))
                if with_ctx:
                    units += [(L + cq * 128, [(64, None), (65, None)]) for cq in range(2)]
                for (c0, keys) in units:
                    q = qs[ui % 3]
                    k.dma("sp", q[:, :].rearrange("d (h t) -> d h t", h=2),
                          self.QT[hg * 128:(hg + 1) * 128, c0:c0 + 128].rearrange("(h d) t -> d h t", d=64), writes=[q])
                    accs_ = [self.psb[3 + 2 * i + ui % 2] for i in range(2)]
                    S = [((lambda kt, i=i: kth[i][:, kt * 128:(kt + 1) * 128]), kth[i], q[:, i * 128:(i + 1) * 128], q, i * 128, 128)
                         for i in range(2)]
                    PV = [((lambda kt, i=i: vxs[i][:, kt, :]), vxs[i], i * 128, 128, accs_[i][:, 0:128], accs_[i])
                          for i in range(2)]
                    self.attn_unit(S, keys, PV, 256, pts, st)
                    dst = self.YT[hg * 128:(hg + 1) * 128, c0:c0 + 128].rearrange("(h d) t -> d h t", d=64)
                    self.epi_shift([(accs_[i], 0, i * 128, 128) for i in range(2)], 256, dshs, yts, ui, dst)
                    ui += 1

    def phase_da(self, layer, with_ctx):
        k = self.k
        li = layer // 2
        lam_init = 0.8 - 0.6 * math.exp(-0.3 * layer)
        with k.phase():
            lq = k.sb([128, 4, 64], F32, "lq")
            for j, nme in enumerate(("da_lambda_q1", "da_lambda_k1", "da_lambda_q2", "da_lambda_k2")):
                k.dma("sp", lq[:, j, :], self.vec64[nme][li:li + 1, :].partition_broadcast(128), writes=[lq])
            pr = k.sb([128, 2, 64], F32, "lpr")
            ssum = k.sb([128, 2], F32, "lsum")
            neglam = k.sb([128, 1], F32, "neglam")
            for j in range(2):
                k.op("dve", lambda e, j=j: e.tensor_tensor(out=pr[:, j, :], in0=lq[:, 2 * j, :], in1=lq[:, 2 * j + 1, :], op=ALU.mult),
                     reads=[lq], writes=[pr])
                k.op("dve", lambda e, j=j: e.tensor_reduce(out=ssum[:, j:j + 1], in_=pr[:, j, :], axis=mybir.AxisListType.X, op=ALU.add),
                     reads=[pr], writes=[ssum])
            k.op("act", lambda e: e.activation(out=ssum[:, :], in_=ssum[:, :], func=AF.Exp), reads=[ssum], writes=[ssum])
            k.op("dve", lambda e: e.tensor_tensor(out=neglam[:, :], in0=ssum[:, 1:2], in1=ssum[:, 0:1], op=ALU.subtract),
                 reads=[ssum], writes=[neglam])
            k.op("dve", lambda e: e.tensor_scalar(out=neglam[:, :], in0=neglam[:, :], scalar1=-lam_init, scalar2=None, op0=ALU.add),
                 reads=[neglam], writes=[neglam])
            slcol = k.sb([128, 1], F32, "slcol")
            k.dma("sp", slcol[:, :], self.subln[li:li + 1, :].rearrange("o d -> d o"), writes=[slcol], allow_slow_non_contiguous=True)
            k.op("dve", lambda e: e.tensor_scalar(out=slcol[:, :], in0=slcol[:, :], scalar1=1.0 - lam_init, scalar2=None, op0=ALU.mult),
                 reads=[slcol], writes=[slcol])
            pts = [k.sb([128, 512], BF16, "pt") for _ in range(3)]
            qs = [k.sb([64, 512], BF16, "q") for _ in range(3)]
            mk = lambda nm: [k.sb([128, 512], F32, nm) for _ in range(2)]
            paccs, rds, ys = mk("pacc"), mk("rd"), mk("yy")
            outs = [k.sb([128, 256], BF16, "ob") for _ in range(2)]
            st = {"p": 0}
            ui = 0
            for j in range(4):
              with k.phase():
                k1 = k.sb([64, LT], BF16, "k1")
                k2 = k.sb([64, LT], BF16, "k2")
                vj = k.sb([128, 66, 128], BF16, "vj")
                r0 = 512 + j * 128
                k.dma("sp", k1[:, :], self.KT[r0:r0 + 64, :], writes=[k1])
                k.dma("sp", k2[:, :], self.KT[r0 + 64:r0 + 128, :], writes=[k2])
                self.load_vx(vj, r0, 128, False)
                units = [(qb * 256, list(range(66))) for qb in range(32)]
                if with_ctx:
                    units.append((L, [64, 65]))
                for (c0, kts) in units:
                    q = qs[ui % 3]
                    k.dma("sp", q[:, :].rearrange("d (h t) -> d h t", h=2),
                          self.QT[r0:r0 + 128, c0:c0 + 256].rearrange("(h d) t -> d h t", d=64), writes=[q])
                    accn = self.psb[3 + ui % 2]
                    pd, pm = self.psb[5], self.psb[6]
                    pacc, rd, y = paccs[ui % 2], rds[ui % 2], ys[ui % 2]
                    ob = outs[ui % 2]
                    S = [(lambda kt: k1[:, kt * 128:(kt + 1) * 128], k1, q[:, 0:256], q, 0, 256),
                         (lambda kt: k2[:, kt * 128:(kt + 1) * 128], k2, q[:, 256:512], q, 256, 256)]
                    PV = [(lambda kt: vj[:, kt, :], vj, 0, 512, accn[:, 0:512], accn)]

                    def hook(ki, pt, pacc=pacc):
                        if ki == 0:
                            k.op("dve", lambda e: e.tensor_copy(out=pacc[:, :], in_=pt[:, :]), reads=[pt], writes=[pacc])
                        else:
                            k.op("dve", lambda e: e.tensor_tensor(out=pacc[:, :], in0=pacc[:, :], in1=pt[:, :], op=ALU.add),
                                 reads=[pt, pacc], writes=[pacc])
                    self.attn_unit(S, [(kt, None) for kt in kts], PV, 512, pts, st, hook=hook)
                    k.op("pe", lambda e: e.matmul(pd[:, :], lhsT=self.ones32, rhs=pacc[:, :], start=True, stop=True),
                         reads=[pacc, self.cst], writes=[pd])
                    k.op("dve", lambda e: e.reciprocal(out=rd[:, :], in_=pd[:, :]), reads=[pd], writes=[rd])
                    k.op("dve", lambda e: e.tensor_tensor(out=rd[:, :], in0=accn[:, :], in1=rd[:, :], op=ALU.mult),
                         reads=[accn, rd], writes=[rd])
                    k.op("dve", lambda e: e.scalar_tensor_tensor(out=y[:, 0:256], in0=rd[:, 256:512], scalar=neglam[:, 0:1],
                                                                 in1=rd[:, 0:256], op0=ALU.mult, op1=ALU.add),
                         reads=[rd, neglam], writes=[y])
                    k.op("act", lambda e: e.activation(out=y[:, 256:512], in_=y[:, 0:256], func=AF.Square), reads=[y], writes=[y])
                    k.op("pe", lambda e: e.matmul(pm[:, 0:256], lhsT=self.o128, rhs=y[:, 256:512], start=True, stop=True),
                         reads=[y, self.cst], writes=[pm])
                    k.op("act", lambda e: e.activation(out=rd[:, 0:256], in_=pm[:, 0:256], func=AF.Sqrt, bias=self.epsc[:, 0:1]),
                         reads=[pm, self.epsc], writes=[rd])
                    k.op("dve", lambda e: e.reciprocal(out=rd[:, 0:256], in_=rd[:, 0:256]), reads=[rd], writes=[rd])
                    k.op("dve", lambda e: e.scalar_tensor_tensor(out=ob[:, :], in0=y[:, 0:256], scalar=slcol[:, 0:1], in1=rd[:, 0:256],
                                                                 op0=ALU.mult, op1=ALU.mult), reads=[y, slcol, rd], writes=[ob])
                    k.dma("pool", self.YT[r0:r0 + 128, c0:c0 + 256], ob[:, :], reads=[ob])
                    ui += 1


    def phase_hyena(self, layer, with_ctx):
        k = self.k
        li = layer // 2
        PI = math.pi
        sets = [0, 1] if with_ctx else [0]
        rnorms = {}
        with k.phase():
            rn_all = k.sb([128, 2, 4], F32, "rnorm")
            skipc = k.sb([128, 4], F32, "skipc")
            k.dma("sp", skipc[:, :], self.hy_skip[li:li + 1, :].rearrange("o (c p) -> p (o c)", p=128), writes=[skipc],
                  allow_slow_non_contiguous=True)
            with k.phase():
                w1 = k.sb([33, 64], F32, "w1")
                w2 = k.sb([64, 64], F32, "w2")
                w3 = k.sb([64, 64], F32, "w3")
                w4 = k.sb([64, 1024], F32, "w4")
                k.dma("sp", w1[:, :], self.hy_w1[li, :, :], writes=[w1])
                k.dma("sp", w2[:, :], self.hy_w2[li, :, :], writes=[w2])
                k.dma("sp", w3[:, :], self.hy_w3[li, :, :], writes=[w3])
                k.dma("sp", w4[:, :], self.hy_w4[li, :, :], writes=[w4])
                fc = k.sb([64, 4], F32, "fcol")
                for j, nme in enumerate(("hy_freq", "hy_b1", "hy_b2", "hy_b3")):
                    k.dma("sp", fc[:, j:j + 1], self.vec64[nme][li:li + 1, :].rearrange("o d -> d o"), writes=[fc],
                          allow_slow_non_contiguous=True)
                for j in (1, 2, 3):
                    k.op("dve", lambda e, j=j: e.tensor_tensor(out=fc[:, j:j + 1], in0=fc[:, j:j + 1], in1=fc[:, 0:1], op=ALU.mult),
                         reads=[fc], writes=[fc])
                negpi = k.sb([64, 1], F32, "negpi")
                k.op("dve", lambda e: e.memset(negpi[:, :], -PI), writes=[negpi])
                ndel = k.sb([128, 4], F32, "ndel")
                k.dma("sp", ndel[:, :], self.hy_ndel[:, :], writes=[ndel])
                zts = [k.sb([33, 512], F32, "zt") for _ in range(2)]
                tls = [k.sb([128, 512], F32, "tl") for _ in range(2)]
                hs = [k.sb([64, 512], F32, "hh") for _ in range(3)]
                decs = [k.sb([128, 512], F32, "dec") for _ in range(2)]
                hks = [k.sb([128, 512], F32, "hk") for _ in range(3)]
                junk = k.sb([128, 512], F32, "junkf")
                hi_ = k.sb([64, 512], mybir.dt.int32, "hi")
                hf_ = k.sb([64, 512], F32, "hf")
                part = k.sb([128, 8, 16], F32, "part")
                nrm = k.sb([128, 8], F32, "nrm")
                bi = 0
                for s_ in sets:
                    k.op("dve", lambda e: e.memset(part[:], 0.0), writes=[part])
                    nblk = 16 if s_ == 0 else 1
                    for blk in range(nblk):
                        zt, tl = zts[bi % 2], tls[bi % 2]
                        c0 = blk * 512
                        k.dma("sp", zt[:, :], self.hy_zT[s_, :, c0:c0 + 512], writes=[zt])
                        k.dma("sp", tl[:, :], self.hy_tlin[s_:s_ + 1, c0:c0 + 512].partition_broadcast(128), writes=[tl])
                        src, srct, kk = zt[:, :], zt, 33
                        for li_, wm in enumerate((w1, w2, w3)):
                            ps = self.psb[li_ % 2]
                            h = hs[li_]
                            k.op("pe", lambda e, wm=wm, src=src, kk=kk, ps=ps: e.matmul(ps[0:64, :], lhsT=wm[0:kk, :], rhs=src,
                                                                                      start=True, stop=True),
                                 reads=[wm, srct], writes=[ps])
                            k.op("dve", lambda e, ps=ps, h=h, li_=li_: e.tensor_scalar(
                                out=h[:, :], in0=ps[0:64, :], scalar1=fc[:, 0:1], scalar2=fc[:, li_ + 1:li_ + 2],
                                op0=ALU.mult, op1=ALU.add), reads=[ps, fc], writes=[h])
                            k.op("dve", lambda e, h=h: e.tensor_scalar(out=h[:, :], in0=h[:, :], scalar1=1.0 / (2.0 * PI), scalar2=16.5,
                                                                       op0=ALU.mult, op1=ALU.add), reads=[h], writes=[h])
                            k.op("dve", lambda e, h=h: e.tensor_copy(out=hi_[:, :], in_=h[:, :]), reads=[h], writes=[hi_])
                            k.op("dve", lambda e, h=h: e.tensor_copy(out=hf_[:, :], in_=hi_[:, :]), reads=[hi_], writes=[hf_])
                            k.op("dve", lambda e, h=h: e.tensor_tensor(out=h[:, :], in0=h[:, :], in1=hf_[:, :], op=ALU.subtract),
                                 reads=[h, hf_], writes=[h])
                            k.op("dve", lambda e, h=h: e.scalar_tensor_tensor(out=h[:, :], in0=h[:, :], scalar=0.0, in1=h[:, :],
                                                                              op0=ALU.is_lt, op1=ALU.add), reads=[h], writes=[h])
                            k.op("act", lambda e, h=h: e.activation(out=h[:, :], in_=h[:, :], func=AF.Sin, bias=negpi[:, 0:1], scale=2.0 * PI),
                                 reads=[h, negpi], writes=[h])
                            src, srct, kk = h[:, :], h, 64
                        h3 = hs[2]
                        for cch in range(8):
                            cc = cch % 4
                            ps = self.psb[2 + cch % 2]
                            dec, hk = decs[cch % 2], hks[cch % 3]
                            k.op("pe", lambda e, cch=cch, ps=ps: e.matmul(ps[:, :], lhsT=w4[:, cch * 128:(cch + 1) * 128], rhs=h3[:, :],
                                                                         start=True, stop=True), reads=[w4, h3], writes=[ps])
                            k.op("act", lambda e, dec=dec, cc=cc: e.activation(out=dec[:, :], in_=tl[:, :], func=AF.Exp,
                                                                               scale=ndel[:, cc:cc + 1]), reads=[tl, ndel], writes=[dec])
                            k.op("dve", lambda e, ps=ps, dec=dec, hk=hk: e.tensor_tensor(out=hk[:, :], in0=ps[:, :], in1=dec[:, :],
                                                                                       op=ALU.mult), reads=[ps, dec], writes=[hk])
                            if cch >= 4 and blk == 0:
                                k.op("dve", lambda e, hk=hk: e.memset(hk[:, 0:1], 0.0), writes=[hk])
                            k.op("act", lambda e, hk=hk, cch=cch, blk=blk: e.activation(out=junk[:, :], in_=hk[:, :], func=AF.Abs,
                                                                                       accum_out=part[:, cch, blk:blk + 1]),
                                 reads=[hk], writes=[junk, part])
                            k.dma("pool", self.HF[s_, cch * 128:(cch + 1) * 128, c0:c0 + 512], hk[:, :], reads=[hk])
                        bi += 1
                    k.op("dve", lambda e: e.tensor_reduce(out=nrm[:, :], in_=part[:, :, :], axis=mybir.AxisListType.X, op=ALU.add),
                         reads=[part], writes=[nrm])
                    k.op("dve", lambda e, s_=s_: e.tensor_tensor(out=rn_all[:, s_, :], in0=nrm[:, 0:4], in1=nrm[:, 4:8], op=ALU.add),
                         reads=[nrm], writes=[rn_all])
                    k.op("dve", lambda e, s_=s_: e.reciprocal(out=rn_all[:, s_, :], in_=rn_all[:, s_, :]), reads=[rn_all], writes=[rn_all])
            with k.phase():
                dft = k.sb([128, 4, 128], F32, "dft")
                k.dma("sp", dft[:], self.dft[:, :, :].rearrange("a p n -> p a n"), writes=[dft])
                Fc, Fs, TWc, TWs = (dft[:, a, :] for a in range(4))
                fx = k.sb([128, 6, 128], F32, "fx")
                k.op("dve", lambda e: e.tensor_copy(out=fx[:, 0, :], in_=Fc), reads=[dft], writes=[fx])
                k.op("dve", lambda e: e.tensor_copy(out=fx[:, 1, :], in_=Fs), reads=[dft], writes=[fx])
                k.op("dve", lambda e: e.tensor_scalar(out=fx[:, 2, :], in0=Fs, scalar1=-1.0, scalar2=None, op0=ALU.mult),
                     reads=[dft], writes=[fx])
                k.op("dve", lambda e: e.tensor_scalar(out=fx[:, 3, :], in0=Fc, scalar1=-1.0, scalar2=None, op0=ALU.mult),
                     reads=[dft], writes=[fx])
                k.op("dve", lambda e: e.tensor_copy(out=fx[:, 4, :], in_=Fs), reads=[dft], writes=[fx])
                k.op("dve", lambda e: e.tensor_copy(out=fx[:, 5, :], in_=Fc), reads=[dft], writes=[fx])
                FcFs = fx[:, 0:2, :].rearrange("p a n -> p (a n)")
                FcnFs = None
                nFs, nFc = fx[:, 2, :], fx[:, 3, :]
                r1 = k.sb([128, 2, 128], F32, "r1")
                k.op("dve", lambda e: e.tensor_copy(out=r1[:, 0, :], in_=Fc), reads=[dft], writes=[r1])
                k.op("dve", lambda e: e.tensor_copy(out=r1[:, 1, :], in_=nFs), reads=[fx], writes=[r1])
                R1 = r1[:, :, :].rearrange("p a n -> p (a n)")
                R2 = fx[:, 4:6, :].rearrange("p a n -> p (a n)")
                mk = lambda shape, nm, n=2: [k.sb(shape, F32, nm) for _ in range(n)]
                dts = mk([64, 3, 4, 128], "dt")
                p1s, p2s = mk([128, 512], "p1"), mk([128, 512], "p2")
                Bs = mk([128, 3, 2, 512], "B", 2)
                kfs = mk([128, 2, 512], "kf")
                Ys = mk([128, 2, 512], "Y")
                tts = mk([128, 2, 512], "tt")
                Ds = mk([128, 2, 512], "D")
                youts = mk([64, 512], "yo")
                gi = 0
                for s_ in sets:
                    nr = 64 if s_ == 0 else 2
                    ncol = nr * 128
                    zrow = 0 if s_ == 0 else 512
                    for g4 in range(128):
                        c0 = g4 * 4
                        dt_, B, kf, Y, tt, Dd, yo = dts[gi % 2], Bs[gi % 2], kfs[gi % 2], Ys[gi % 2], tts[gi % 2], Ds[gi % 2], youts[gi % 2]
                        srcs = (self.ZT[zrow + c0:zrow + c0 + 4, 0:ncol], self.HF[s_, c0:c0 + 4, 0:ncol],
                                self.HF[s_, 512 + c0:512 + c0 + 4, 0:ncol])
                        for sg in range(3):
                            k.dma("sp", dt_[0:nr, sg, :, :], srcs[sg].rearrange("c (a b) -> a c b", b=128), writes=[dt_])
                        for sg in range(3):
                            for pr_ in range(2):
                                pb = self.psb[(sg * 2 + pr_) % 2]
                                p1, p2 = p1s[(sg * 2 + pr_) % 2], p2s[(sg * 2 + pr_) % 2]
                                for jj in range(2):
                                    j = pr_ * 2 + jj
                                    k.op("pe", lambda e, sg=sg, j=j, jj=jj, pb=pb: e.matmul(
                                        pb[:, jj * 256:(jj + 1) * 256], lhsT=dt_[0:nr, sg, j, :], rhs=FcFs[0:nr, :], start=True, stop=True),
                                        reads=[dt_, fx], writes=[pb])
                                pbv = pb[:, :].rearrange("p (a n) -> p a n", n=128)
                                k.op("dve", lambda e, pbv=pbv, p1=p1: e.tensor_tensor(
                                    out=p1[:, :].rearrange("p (a n) -> p a n", n=128), in0=pbv, in1=TWc.unsqueeze(1).to_broadcast([128, 4, 128]),
                                    op=ALU.mult), reads=[pb, dft], writes=[p1])
                                k.op("dve", lambda e, pbv=pbv, p2=p2: e.tensor_tensor(
                                    out=p2[:, :].rearrange("p (a n) -> p a n", n=128), in0=pbv, in1=TWs.unsqueeze(1).to_broadcast([128, 4, 128]),
                                    op=ALU.mult), reads=[pb, dft], writes=[p2])
                                for jj in range(2):
                                    j = pr_ * 2 + jj
                                    k.op("pool", lambda e, sg=sg, j=j, jj=jj, p1=p1, p2=p2: e.tensor_tensor(
                                        out=B[:, sg, 0, j * 128:(j + 1) * 128], in0=p1[:, jj * 256:jj * 256 + 128],
                                        in1=p2[:, jj * 256 + 128:jj * 256 + 256], op=ALU.subtract), reads=[p1, p2], writes=[B])
                                    k.op("pool", lambda e, sg=sg, j=j, jj=jj, p1=p1, p2=p2: e.tensor_tensor(
                                        out=B[:, sg, 1, j * 128:(j + 1) * 128], in0=p2[:, jj * 256:jj * 256 + 128],
                                        in1=p1[:, jj * 256 + 128:jj * 256 + 256], op=ALU.add), reads=[p1, p2], writes=[B])
                        xre, xim, kre, kim = self.psb[2], self.psb[3], self.psb[4], self.psb[5]
                        mm = lambda out, lhsT, rhs, st_, sp_, rt: k.op(
                            "pe", lambda e: e.matmul(out[:, :], lhsT=lhsT, rhs=rhs, start=st_, stop=sp_), reads=[rt, fx, dft], writes=[out])
                        mm(xre, Fc, B[:, 0, 0, :], True, False, B)
                        mm(xre, nFs, B[:, 0, 1, :], False, True, B)
                        mm(xim, Fc, B[:, 0, 1, :], True, False, B)
                        mm(xim, Fs, B[:, 0, 0, :], False, True, B)
                        mm(kre, Fc, B[:, 1, 0, :], True, False, B)
                        mm(kre, nFs, B[:, 1, 1, :], False, False, B)
                        mm(kre, Fc, B[:, 2, 0, :], False, False, B)
                        mm(kre, nFs, B[:, 2, 1, :], False, True, B)
                        mm(kim, Fc, B[:, 1, 1, :], True, False, B)
                        mm(kim, Fs, B[:, 1, 0, :], False, False, B)
                        mm(kim, nFc, B[:, 2, 1, :], False, False, B)
                        mm(kim, nFs, B[:, 2, 0, :], False, True, B)
                        k.op("act", lambda e: e.activation(out=kf[:, 0, :], in_=kre[:, :], func=AF.Copy), reads=[kre], writes=[kf])
                        k.op("act", lambda e: e.activation(out=kf[:, 1, :], in_=kim[:, :], func=AF.Copy), reads=[kim], writes=[kf])
                        tt4 = lambda out, a, b, op, rd, wr, eng="dve": k.op(eng, lambda e: e.tensor_tensor(out=out, in0=a, in1=b, op=op),
                                                                            reads=rd, writes=wr)
                        tt4(tt[:, 0, :], xre[:, :], kf[:, 0, :], ALU.mult, [xre, kf], [tt])
                        tt4(tt[:, 1, :], xim[:, :], kf[:, 1, :], ALU.mult, [xim, kf], [tt])
                        tt4(Y[:, 0, :], tt[:, 0, :], tt[:, 1, :], ALU.subtract, [tt], [Y], "pool")
                        tt4(tt[:, 0, :], xre[:, :], kf[:, 1, :], ALU.mult, [xre, kf, Y], [tt])
                        tt4(tt[:, 1, :], xim[:, :], kf[:, 0, :], ALU.mult, [xim, kf], [tt])
                        tt4(Y[:, 1, :], tt[:, 0, :], tt[:, 1, :], ALU.add, [tt], [Y], "pool")
                        for pr_ in range(2):
                            pb = self.psb[6 + pr_]
                            p1, p2 = p1s[pr_], p2s[pr_]
                            for jj in range(2):
                                j = pr_ * 2 + jj
                                k.op("pe", lambda e, j=j, jj=jj, pb=pb: e.matmul(pb[:, jj * 256:(jj + 1) * 256], lhsT=Y[:, 0, j * 128:(j + 1) * 128],
                                                                                rhs=R1, start=True, stop=False), reads=[Y, r1], writes=[pb])
                                k.op("pe", lambda e, j=j, jj=jj, pb=pb: e.matmul(pb[:, jj * 256:(jj + 1) * 256], lhsT=Y[:, 1, j * 128:(j + 1) * 128],
                                                                                rhs=R2, start=False, stop=True), reads=[Y, fx], writes=[pb])
                            pbv = pb[:, :].rearrange("p (a n) -> p a n", n=128)
                            k.op("dve", lambda e, pbv=pbv, p1=p1: e.tensor_tensor(
                                out=p1[:, :].rearrange("p (a n) -> p a n", n=128), in0=pbv, in1=TWc.unsqueeze(1).to_broadcast([128, 4, 128]),
                                op=ALU.mult), reads=[pb, dft], writes=[p1])
                            k.op("dve", lambda e, pbv=pbv, p2=p2: e.tensor_tensor(
                                out=p2[:, :].rearrange("p (a n) -> p a n", n=128), in0=pbv, in1=TWs.unsqueeze(1).to_broadcast([128, 4, 128]),
                                op=ALU.mult), reads=[pb, dft], writes=[p2])
                            for jj in range(2):
                                j = pr_ * 2 + jj
                                k.op("pool", lambda e, j=j, jj=jj, p1=p1, p2=p2: e.tensor_tensor(
                                    out=Dd[:, 0, j * 128:(j + 1) * 128], in0=p1[:, jj * 256:jj * 256 + 128],
                                    in1=p2[:, jj * 256 + 128:jj * 256 + 256], op=ALU.add), reads=[p1, p2], writes=[Dd])
                                k.op("pool", lambda e, j=j, jj=jj, p1=p1, p2=p2: e.tensor_tensor(
                                    out=Dd[:, 1, j * 128:(j + 1) * 128], in0=p1[:, jj * 256 + 128:jj * 256 + 256],
                                    in1=p2[:, jj * 256:jj * 256 + 128], op=ALU.subtract), reads=[p1, p2], writes=[Dd])
                        py = self.psb[0]
                        k.op("pe", lambda e: e.matmul(py[0:nr, :], lhsT=Fc[:, 0:nr], rhs=Dd[:, 0, :], start=True, stop=False),
                             reads=[Dd, dft], writes=[py])
                        k.op("pe", lambda e: e.matmul(py[0:nr, :], lhsT=Fs[:, 0:nr], rhs=Dd[:, 1, :], start=False, stop=True),
                             reads=[Dd, dft], writes=[py])
                        k.op("act", lambda e: e.activation(out=yo[0:nr, :], in_=py[0:nr, :], func=AF.Copy, scale=1.0 / 16384.0),
                             reads=[py], writes=[yo])
                        k.dma("pool", self.YH[zrow + c0:zrow + c0 + 4, 0:ncol].rearrange("c (a b) -> a c b", b=128),
                              yo[0:nr, :].rearrange("a (c b) -> a c b", b=128), reads=[yo])
                        gi += 1
            with k.phase():
                mk = lambda nm: [k.sb([128, 2048], F32, nm) for _ in range(2)]
                ys, zs, x0s = mk("hy"), mk("hz"), mk("hx0")
                obs = [k.sb([128, 2048], BF16, "hob") for _ in range(2)]
                bi = 0
                for s_ in sets:
                    zrow = 0 if s_ == 0 else 512
                    blocks = [(b * 2048, 2048) for b in range(4)] if s_ == 0 else [(0, 256)]
                    for cc in range(4):
                        for (t0, n) in blocks:
                            yt, zt, xt, ob = ys[bi % 2], zs[bi % 2], x0s[bi % 2], obs[bi % 2]
                            rows = slice(zrow + cc * 128, zrow + (cc + 1) * 128)
                            k.dma("sp", yt[:, :n], self.YH[rows, t0:t0 + n], writes=[yt])
                            k.dma("sp", zt[:, :n], self.ZT[rows, t0:t0 + n], writes=[zt])
                            k.dma("sp", xt[:, :n], self.X0T[rows, t0:t0 + n], writes=[xt])
                            k.op("dve", lambda e, yt=yt, cc=cc, n=n, s_=s_: e.tensor_scalar(
                                out=yt[:, :n], in0=yt[:, :n], scalar1=rn_all[:, s_, cc:cc + 1], scalar2=None, op0=ALU.mult),
                                reads=[yt, rn_all], writes=[yt])
                            k.op("dve", lambda e, yt=yt, zt=zt, cc=cc, n=n: e.scalar_tensor_tensor(
                                out=yt[:, :n], in0=zt[:, :n], scalar=skipc[:, cc:cc + 1], in1=yt[:, :n], op0=ALU.mult, op1=ALU.add),
                                reads=[yt, zt, skipc], writes=[yt])
                            k.op("pool", lambda e, yt=yt, xt=xt, ob=ob, n=n: e.tensor_tensor(out=ob[:, :n], in0=yt[:, :n], in1=xt[:, :n],
                                                                                           op=ALU.mult), reads=[yt, xt], writes=[ob])
                            tcol = t0 if s_ == 0 else L
                            k.dma("pool", self.YT[512 + cc * 128:512 + (cc + 1) * 128, tcol:tcol + n], ob[:, :n], reads=[ob])
                            bi += 1

    def phase_out(self, layer, with_ctx):
        k = self.k
        odd = layer % 2
        li = layer // 2
        wsrc = (self.w_out_o if odd else self.w_out_e)[li]
        with k.phase():
            WO = k.sb([128, 8, D], BF16, "wo")
            stage = [k.sb([128, 8, 512], F32, "stg") for _ in range(2)]
            self.load_weight(WO, wsrc, D, stage)
            ytbs = [k.sb([128, 8, 512], BF16, "ytb") for _ in range(2)]
            xts = [k.sb([128, D], F32, "xt") for _ in range(2)]
            tmps = [k.sb([128, D], F32, "tmp") for _ in range(2)]
            groups = [(g * 512, 512, 0) for g in range(16)] + ([(L, 256, 1)] if with_ctx else [])
            c = 0
            for gi, (r0, nt, lc) in enumerate(groups):
                ytb = ytbs[gi % 2]
                for kc in range(8):
                    k.dma("sp", ytb[:, kc, :nt], self.YT[kc * 128:(kc + 1) * 128, r0:r0 + nt], writes=[ytb])
                gt = self.G[lc]
                for i in range(nt // 128):
                    xt, tmp = xts[c % 2], tmps[c % 2]
                    rr = r0 + i * 128
                    k.dma("sp", xt[:, :], self.XR[rr:rr + 128, :], writes=[xt])
                    for half in range(2):
                        pp = self.psb[2 * (c % 2) + half]
                        for kc in range(8):
                            k.op("pe", lambda e, kc=kc, pp=pp, half=half: e.matmul(
                                pp[:, :], lhsT=ytb[:, kc, i * 128:(i + 1) * 128], rhs=WO[:, kc, half * 512:(half + 1) * 512],
                                start=(kc == 0), stop=(kc == 7)), reads=[ytb, WO], writes=[pp])
                        k.op("dve", lambda e, pp=pp, half=half: e.tensor_tensor(
                            out=tmp[:, half * 512:(half + 1) * 512], in0=pp[:, :], in1=gt[:, half * 512:(half + 1) * 512], op=ALU.mult),
                            reads=[pp, gt], writes=[tmp])
                    k.op("pool", lambda e: e.tensor_tensor(out=tmp[:, :], in0=tmp[:, :], in1=xt[:, :], op=ALU.add),
                         reads=[tmp, xt], writes=[tmp])
                    k.dma("pool", self.XR[rr:rr + 128, :], tmp[:, :], reads=[tmp])
                    c += 1

    def phase_ffn(self, layer, with_ctx, last):
        k = self.k
        with k.phase():
            WU = k.sb([128, 8, 2 * FH], BF16, "wu")
            WDn = k.sb([128, NCH_F, D], BF16, "wd")
            with k.phase():
                stage = [k.sb([128, 8, 512], F32, "stg") for _ in range(2)]
                self.load_weight(WU, self.w_up[layer], 2 * FH, stage)
            with k.phase():
                stage = [k.sb([128, NCH_F, 128], F32, "stg") for _ in range(2)]
                self.load_weight(WDn, self.w_down[layer], D, stage, bw=128)
            fw = k.sb([128, 3, NCH_F], F32, "fw")
            fb = k.sb([128, NCH_F], F32, "fb")
            for j in range(3):
                k.dma("sp", fw[:, j, :], self.fcw[layer, j:j + 1, :].rearrange("o (c p) -> p (o c)", p=128), writes=[fw],
                      allow_slow_non_contiguous=True)
            k.dma("sp", fb[:], self.fcb[layer:layer + 1, :].rearrange("o (c p) -> p (o c)", p=128), writes=[fb],
                  allow_slow_non_contiguous=True)
            build = self.mk_build(SH2, SC2)
            hts = [k.sb([128, 8, 258], BF16, "ht") for _ in range(2)]
            AT = k.sb([128, NCH_F, 256], BF16, "at")
            accs = [k.sb([128, 256], F32, "acc") for _ in range(2)]
            sgs = [k.sb([128, 256], F32, "sg") for _ in range(2)]
            xr = k.sb([128, D], F32, "xr")
            xo = k.sb([128, D], F32, "xo")
            sgroups = [(s * 256, 0) for s in range(32)] + ([(L, 1)] if with_ctx else [])

            def build_sg(si):
                r0, lc = sgroups[si]
                ht = hts[si % 2]
                prev = hts[(si - 1) % 2]
                for i in range(2):
                    build(ht, 1 + i * 128, r0 + i * 128, lc)
                if si == 0 or lc == 1:
                    k.op("pool", lambda e: e.memset(ht[:, :, 0:1], 0.0), writes=[ht])
                else:
                    k.op("pool", lambda e: e.tensor_copy(out=ht[:, :, 0:1], in_=prev[:, :, 256:257]), reads=[prev], writes=[ht])
                if lc == 1:
                    k.op("pool", lambda e: e.memset(ht[:, :, 257:258], 0.0), writes=[ht])
                if si > 0:
                    if lc == 1:
                        k.op("pool", lambda e: e.memset(prev[:, :, 257:258], 0.0), writes=[prev])
                    else:
                        k.op("pool", lambda e: e.tensor_copy(out=prev[:, :, 257:258], in_=ht[:, :, 1:2]), reads=[ht], writes=[prev])

            def run_sg(si):
                r0, lc = sgroups[si]
                ht = hts[si % 2]
                if si == len(sgroups) - 1 and lc == 0:
                    k.op("pool", lambda e: e.memset(ht[:, :, 257:258], 0.0), writes=[ht])
                for j in range(NCH_F):
                    pg, pv = self.psb[2 + 2 * (j % 2)], self.psb[3 + 2 * (j % 2)]
                    acc, sg = accs[j % 2], sgs[j % 2]
                    for kc in range(8):
                        k.op("pe", lambda e, kc=kc: e.matmul(pg[:, 0:258], lhsT=WU[:, kc, j * 128:(j + 1) * 128], rhs=ht[:, kc, 0:258],
                                                              start=(kc == 0), stop=(kc == 7)), reads=[WU, ht], writes=[pg])
                    for kc in range(8):
                        k.op("pe", lambda e, kc=kc: e.matmul(pv[:, 0:256], lhsT=WU[:, kc, FH + j * 128:FH + (j + 1) * 128],
                                                              rhs=ht[:, kc, 1:257], start=(kc == 0), stop=(kc == 7)),
                             reads=[WU, ht], writes=[pv])
                    k.op("dve", lambda e: e.tensor_scalar(out=acc[:, :], in0=pg[:, 0:256], scalar1=fw[:, 0, j:j + 1], scalar2=fb[:, j:j + 1],
                                                          op0=ALU.mult, op1=ALU.add), reads=[pg, fw, fb], writes=[acc])
                    for t in (1, 2):
                        k.op("dve", lambda e, t=t: e.scalar_tensor_tensor(out=acc[:, :], in0=pg[:, t:t + 256], scalar=fw[:, t, j:j + 1],
                                                                          in1=acc[:, :], op0=ALU.mult, op1=ALU.add),
                             reads=[pg, fw, acc], writes=[acc])
                    k.op("act", lambda e: e.activation(out=sg[:, :], in_=acc[:, :], func=AF.Silu), reads=[acc], writes=[sg])
                    k.op("dve", lambda e: e.tensor_tensor(out=AT[:, j, :], in0=sg[:, :], in1=pv[:, 0:256], op=ALU.mult),
                         reads=[sg, pv], writes=[AT])
                gt = self.G[2 + lc]
                for i in range(2):
                    rr = r0 + i * 128
                    k.dma("sp", xr[:, :], self.XR[rr:rr + 128, :], writes=[xr])
                    for half in range(2):
                        pp = self.psb[6 + half]
                        for j in range(NCH_F):
                            k.op("pe", lambda e, j=j, pp=pp, half=half: e.matmul(
                                pp[:, :], lhsT=AT[:, j, i * 128:(i + 1) * 128], rhs=WDn[:, j, half * 512:(half + 1) * 512],
                                start=(j == 0), stop=(j == NCH_F - 1)), reads=[AT, WDn], writes=[pp])
                        k.op("dve", lambda e, pp=pp, half=half: e.tensor_tensor(
                            out=xo[:, half * 512:(half + 1) * 512], in0=pp[:, :], in1=gt[:, half * 512:(half + 1) * 512], op=ALU.mult),
                            reads=[pp, gt], writes=[xo])
                    k.op("pool", lambda e: e.tensor_tensor(out=xo[:, :], in0=xo[:, :], in1=xr[:, :], op=ALU.add),
                         reads=[xo, xr], writes=[xo])
                    if last and lc == 0:
                        k.dma("pool", self.y[rr:rr + 128, :], xo[:, :], reads=[xo])
                    else:
                        k.dma("pool", self.XR[rr:rr + 128, :], xo[:, :], reads=[xo])

            build_sg(0)
            for si in range(len(sgroups)):
                if si + 1 < len(sgroups):
                    build_sg(si + 1)
                run_sg(si)


    def build(self):
        k = self.k
        stop = getattr(self, "stop", "full")
        with k.phase():
            self.copy_in()
        for layer in getattr(self, "layers", range(self.n_layers)):
            with_ctx = layer < DEPTH - 1
            last = layer == DEPTH - 1
            fin = layer == list(getattr(self, "layers", range(self.n_layers)))[-1]
            self.phase_ada(layer)
            if fin and stop == "ada":
                break
            self.phase_in(layer)
            if fin and stop == "in":
                break
            if layer % 2 == 0:
                self.phase_na(layer, with_ctx)
                self.phase_da(layer, with_ctx)
            else:
                if "nogqa" not in stop:
                    self.phase_gqa(layer, with_ctx)
                if "nohy" not in stop:
                    self.phase_hyena(layer, with_ctx)
            if fin and stop.startswith("attn"):
                break
            self.phase_out(layer, with_ctx)
            if fin and stop == "out":
                break
            self.phase_ffn(layer, with_ctx, last)
            if not fin:
                k.fresh()
        k.barrier()
        return self.nc


_CACHE = {}


def _host_inputs(inputs):
    f = lambda a: np.ascontiguousarray(np.asarray(a, dtype=np.float32))
    inp = {n: f(v) for n, v in inputs.items()}
    rc, rs = _rope_tables()
    zT, tlin, ndel, fc, fs, tc_, ts_ = _hy_tables()
    shared = {n: inp[n] for n in ("w_ada", "b_ada", "w_up", "ffn_conv_w", "ffn_conv_b", "w_down", "w_in_e", "w_out_e",
                                  "w_in_o", "w_out_o", "na_q_gain", "na_k_gain", "da_q_gain", "da_k_gain", "da_lambda_q1",
                                  "da_lambda_k1", "da_lambda_q2", "da_lambda_k2", "gqa_q_gain", "gqa_k_gain", "hy_b1", "hy_b2",
                                  "hy_b3", "hy_freq", "da_subln_gain", "hy_conv_w", "hy_conv_b", "hy_w1", "hy_w2", "hy_w3",
                                  "hy_w4", "hy_skip")}
    shared["rope_cos"] = rc
    shared["rope_sin"] = rs
    shared["na_tab"] = np.stack([_na_tables(inp["na_rpb"][i]) for i in range(2)]).reshape(2, 4, 25, 128, 256)
    shared["hy_zT"] = zT
    shared["hy_tlin"] = tlin
    shared["hy_ndel"] = ndel
    shared["dft"] = np.stack([fc, fs, tc_, ts_])
    shared["consts"] = _consts()
    maps = []
    for core in range(8):
        b = core % 4
        m = dict(shared)
        m["x"] = inp["x"][b]
        m["ctx"] = inp["ctx"][b]
        m["cc"] = np.ascontiguousarray(np.stack([inp["c"][b], inp["c_ctx"]]))
        maps.append(m)
    return maps


def _layer_maps(maps, layer, xs, cs):
    out = []
    for core, m in enumerate(maps):
        d = {}
        for n, v in m.items():
            if n in _PER4:
                d[n] = np.ascontiguousarray(v[layer:layer + 1])
            elif n in _PER2:
                d[n] = np.ascontiguousarray(v[layer // 2:layer // 2 + 1])
            else:
                d[n] = v
        if xs is not None:
            d["x"] = xs[core]
            d["ctx"] = cs[core]
        out.append(d)
    return out


def kernel(**inputs):
    maps = _host_inputs(inputs)
    if "fused" not in _CACHE:
        _CACHE["fused"] = Prog().build()
    res = run_bass_kernel_spmd(_CACHE["fused"], maps, core_ids=list(range(8)))
    return np.stack([np.asarray(res.results[b]["y"], dtype=np.float32) for b in range(4)])
```

```python
import contextlib
import math
import numpy as np
import concourse.bass as bass
import concourse.mybir as mybir
from concourse.bass_utils import run_bass_kernel_spmd

F32 = mybir.dt.float32
BF16 = mybir.dt.bfloat16
AF = mybir.ActivationFunctionType
ALU = mybir.AluOpType

L = 8192
C = 256
LT = L + C
D = 1024
DEPTH = 4
EPS = 1e-6
FH = 2816
NCH_F = 22
EP = 16000
KSLOT = 8
EPD = 200


class Tile:
    __slots__ = ("t", "w", "r", "name")

    def __init__(self, t, name=""):
        self.t = t
        self.w = None
        self.r = {}
        self.name = name

    def __getitem__(self, k):
        return self.t[k]


class LTile(Tile):
    __slots__ = ("base",)

    def __init__(self, t, name="", base=0):
        Tile.__init__(self, t, name)
        self.base = base

    def __getitem__(self, k):
        b = self.base
        if not isinstance(k, tuple):
            k = (k,)
        f = k[0]
        if isinstance(f, slice):
            f = slice(f.start - b, f.stop - b)
        else:
            f = f - b
        k = (f,) + tuple(k[1:])
        return self.t[k if len(k) > 1 else k[0]]


class K:
    def __init__(self, nc):
        self.nc = nc
        self.es = contextlib.ExitStack()
        self.eng = {"pe": nc.tensor, "act": nc.scalar, "dve": nc.vector, "pool": nc.gpsimd, "sp": nc.sync}
        self.cnt = {e: 0 for e in self.eng}
        self.csem = {e: [] for e in self.eng}
        self.seen = {e: {} for e in self.eng}
        self.dcnt = {}
        self.dsem = {}
        self.uid = 0
        self.phase_stack = None
        self.gen = 0

    def _name(self, p):
        self.uid += 1
        return "%s_%d" % (p, self.uid)

    def sb(self, shape, dt, name="t"):
        n = self._name(name)
        return Tile(self.phase_stack.enter_context(self.nc.sbuf_tensor(n, list(shape), dt)), n)

    def sb_global(self, shape, dt, name="g"):
        n = self._name(name)
        return Tile(self.es.enter_context(self.nc.sbuf_tensor(n, list(shape), dt)), n)

    def ps(self, name="ps"):
        n = self._name(name)
        return Tile(self.es.enter_context(self.nc.psum_tensor(n, [128, 512], F32)), n)

    def dram(self, name, shape, dt, kind="Internal"):
        return Tile(self.nc.dram_tensor(name, list(shape), dt, kind=kind).ap(), name)

    def reg(self, name="r"):
        return Tile(None, name)

    def _csem(self, e, idx):
        ep = (idx - 1) // EP
        lst = self.csem[e]
        while len(lst) <= ep:
            lst.append(self.es.enter_context(self.nc.semaphore(self._name("s" + e))))
        return lst[ep], (idx - 1) % EP + 1

    def _dsem(self, q, i):
        k = i % KSLOT
        u = i // KSLOT
        ep = u // EPD
        d = self.dsem.setdefault(q, {})
        key = (k, ep)
        if key not in d:
            d[key] = self.es.enter_context(self.nc.semaphore(self._name("d" + q)))
        return d[key], 16 * (u % EPD + 1)

    def _waits(self, e, deps):
        out = {}
        seen = self.seen[e]
        for tok in deps:
            if tok is None or tok[-1] != self.gen:
                continue
            if tok[0] == "c":
                _, de, idx, _g = tok
                if de == e and e in ("pe", "sp"):
                    continue
                if de == e:
                    if idx >= self.cnt[e] - 0 and False:
                        pass
                key = ("c", de)
                if seen.get(key, 0) >= idx:
                    continue
                if out.get(key, 0) < idx:
                    out[key] = idx
            else:
                _, q, i, _g = tok
                key = ("d", q, i % KSLOT)
                if seen.get(key, -1) >= i:
                    continue
                if out.get(key, -1) < i:
                    out[key] = i
        h = self.eng[e]
        for key, v in out.items():
            seen[key] = v
            if key[0] == "c":
                s, val = self._csem(key[1], v)
            else:
                s, val = self._dsem(key[1], v)
            h.wait_ge(s, val)

    def _deps(self, reads, writes):
        deps = []
        for t in reads:
            deps.append(t.w)
        for t in writes:
            deps.append(t.w)
            deps.extend(t.r.values())
        return deps

    def _mark(self, tok, reads, writes):
        for t in reads:
            if tok[0] == "c":
                t.r[("c", tok[1])] = tok
            else:
                t.r[("d", tok[1], tok[2] % KSLOT)] = tok
        for t in writes:
            t.w = tok
            t.r = {}

    def op(self, e, fn, reads=(), writes=()):
        self._waits(e, self._deps(reads, writes))
        ins = fn(self.eng[e])
        self.cnt[e] += 1
        idx = self.cnt[e]
        s, val = self._csem(e, idx)
        ins.then_inc(s, 1)
        self._mark(("c", e, idx, self.gen), reads, writes)

    def dma(self, q, out, in_, reads=(), writes=(), **kw):
        i = self.dcnt.get(q, 0)
        deps = self._deps(reads, writes)
        if i >= KSLOT:
            deps.append(("d", q, i - KSLOT, self.gen))
        self._waits(q, deps)
        s, val = self._dsem(q, i)
        self.eng[q].dma_start(out=out, in_=in_, **kw).then_inc(s, 16)
        self.dcnt[q] = i + 1
        self._mark(("d", q, i, self.gen), reads, writes)

    def barrier(self):
        toks = []
        for e in ("pe", "act", "dve", "pool"):
            if self.cnt[e]:
                toks.append(("c", e, self.cnt[e], self.gen))
        for q, n in self.dcnt.items():
            for i in range(max(0, n - KSLOT), n):
                toks.append(("d", q, i, self.gen))
        for e in self.eng:
            self._waits(e, [t for t in toks if not (t[0] == "c" and t[1] == e)])

    def fresh(self):
        self.barrier()
        self.gen += 1
        self.cnt = {e: 0 for e in self.eng}
        self.csem = {e: [] for e in self.eng}
        self.seen = {e: {} for e in self.eng}
        self.dcnt = {}
        self.dsem = {}

    @contextlib.contextmanager
    def phase(self):
        old = self.phase_stack
        self.phase_stack = contextlib.ExitStack()
        try:
            yield
        finally:
            self.barrier()
            self.phase_stack.close()
            self.phase_stack = old


def _rope_tables():
    t = np.arange(L, dtype=np.int32)
    row = (t // 64).astype(np.float32)
    col = (t % 64).astype(np.float32)
    inv = (np.float32(10000.0) ** (-np.arange(16, dtype=np.float32) / np.float32(16))).astype(np.float32)
    ang = np.concatenate([row[:, None] * inv, col[:, None] * inv], axis=-1).astype(np.float32)
    cos = np.cos(ang).astype(np.float32)
    sin = np.sin(ang).astype(np.float32)
    p = np.arange(128)
    i = (p % 64) // 2
    return np.ascontiguousarray(cos[:, i].T), np.ascontiguousarray(sin[:, i].T)


def _na_cls(qt):
    return 0 if qt == 0 else 1 if qt == 1 else 3 if qt == 62 else 4 if qt == 63 else 2


def _na_kt0(qt):
    return min(max(qt - 2, 0), 59)


def _na_tables(rpb):
    out = np.full((4, 5, 7, 128, 2, 128), -30000.0, np.float32)
    out[:, :, 5:7] = 0.0
    reps = {0: 0, 1: 1, 2: 2, 3: 62, 4: 63}
    p = np.arange(128)
    for cls, qt in reps.items():
        kt0 = _na_kt0(qt)
        r = 2 * qt + p // 64
        c = p % 64
        r0 = np.clip(r - 4, 0, 120)
        c0 = np.clip(c - 8, 0, 48)
        for rel in range(5):
            kt = kt0 + rel
            kr = 2 * kt + p // 64
            kc = p % 64
            valid = ((kr[:, None] >= r0[None, :]) & (kr[:, None] < r0[None, :] + 8)
                     & (kc[:, None] >= c0[None, :]) & (kc[:, None] < c0[None, :] + 16))
            ro = np.clip(kr[:, None] - r[None, :] + 7, 0, 14)
            co = np.clip(kc[:, None] - c[None, :] + 15, 0, 30)
            for h in range(8):
                g = rpb[h][ro, co]
                out[h // 2, cls, rel, :, h % 2, :] = np.where(valid, g, np.float32(-30000.0))
    return out


def _hy_tables():
    def zemb(n):
        t = np.linspace(0.0, 1.0, n, dtype=np.float32)[:, None]
        w = (np.float32(2.0 * math.pi) * np.arange(n, dtype=np.float32)[:, None] / np.float32(n)).astype(np.float32)
        f = np.linspace(1e-4, 15, 16, dtype=np.float32)[None, :]
        z = np.concatenate([t, np.cos(f * w), -np.sin(f * w)], axis=-1).astype(np.float32)
        return z, t[:, 0]
    zl, tl = zemb(L)
    zc, tc = zemb(C)
    zT = np.zeros((2, 33, L), np.float32)
    zT[0] = zl.T
    zT[1, :, :C] = zc.T
    tlin = np.full((2, L), 1.0e4, np.float32)
    tlin[0] = tl
    tlin[1, :C] = tc
    max_decay = math.log(1e-2) / 0.3
    min_decay = math.log(1e-2) / 1.5
    deltas = np.abs(np.linspace(min_decay, max_decay, 512, dtype=np.float32))
    ndel = np.ascontiguousarray((-deltas).reshape(4, 128).T)
    n = np.arange(128, dtype=np.float64)
    a = 2.0 * math.pi * np.outer(n, n) / 128.0
    fc = np.cos(a).astype(np.float32)
    fs = (-np.sin(a)).astype(np.float32)
    a2 = 2.0 * math.pi * np.outer(n, n) / 16384.0
    tc_ = np.cos(a2).astype(np.float32)
    ts_ = (-np.sin(a2)).astype(np.float32)
    return zT, tlin, ndel, fc, fs, tc_, ts_


def _consts():
    ident = np.eye(128, dtype=np.float32)
    bd = np.zeros((128, 128), np.float32)
    bd[:64, :64] = 1.0 / 64
    bd[64:, 64:] = 1.0 / 64
    o128 = np.full((128, 128), 1.0 / 128, np.float32)
    rot = np.zeros((128, 128), np.float32)
    for i in range(64):
        rot[2 * i + 1, 2 * i] = -1.0
        rot[2 * i, 2 * i + 1] = 1.0
    return np.stack([ident, bd, o128, rot, np.ones((128, 128), np.float32)])


SH1, SC1, SH2, SC2 = 0, 1, 2, 3
_PER4 = ("w_ada", "b_ada", "w_up", "ffn_conv_w", "ffn_conv_b", "w_down")
_PER2E = ("w_in_e", "w_out_e", "na_q_gain", "na_k_gain", "da_q_gain", "da_k_gain", "da_lambda_q1", "da_lambda_k1",
          "da_lambda_q2", "da_lambda_k2", "da_subln_gain", "na_tab")
_PER2O = ("w_in_o", "w_out_o", "gqa_q_gain", "gqa_k_gain", "hy_conv_w", "hy_conv_b", "hy_w1", "hy_b1", "hy_w2", "hy_b2",
          "hy_w3", "hy_b3", "hy_w4", "hy_freq", "hy_skip")
_PER2 = _PER2E + _PER2O


class Prog:
    def __init__(self, n_layers=DEPTH, debug=False, single=None):
        self.debug = debug
        self.n_layers = n_layers
        self.single = single
        if single is not None:
            self.layers = [single]
        nc = bass.Bass("TRN2", target_bir_lowering=False)
        self.nc = nc
        k = K(nc)
        self.k = k
        def I(name, shape, dt=F32):
            if single is not None and name in _PER4:
                return LTile(nc.dram_tensor(name, [1] + list(shape[1:]), dt, kind="ExternalInput").ap(), name, single)
            if single is not None and name in _PER2:
                return LTile(nc.dram_tensor(name, [1] + list(shape[1:]), dt, kind="ExternalInput").ap(), name, single // 2)
            return k.dram(name, shape, dt, kind="ExternalInput")
        self.x = I("x", [L, D])
        self.ctx = I("ctx", [C, D])
        self.cc = I("cc", [2, D])
        self.w_ada = I("w_ada", [4, D, 6 * D])
        self.b_ada = I("b_ada", [4, 6 * D])
        self.w_up = I("w_up", [4, D, 2 * FH])
        self.fcw = I("ffn_conv_w", [4, 3, FH])
        self.fcb = I("ffn_conv_b", [4, FH])
        self.w_down = I("w_down", [4, FH, D])
        self.w_in_e = I("w_in_e", [2, D, 3072])
        self.w_out_e = I("w_out_e", [2, D, D])
        self.w_in_o = I("w_in_o", [2, D, 2304])
        self.w_out_o = I("w_out_o", [2, D, D])
        self.vec64 = {}
        for n in ("na_q_gain", "na_k_gain", "da_q_gain", "da_k_gain", "da_lambda_q1", "da_lambda_k1",
                  "da_lambda_q2", "da_lambda_k2", "gqa_q_gain", "gqa_k_gain", "hy_b1", "hy_b2", "hy_b3", "hy_freq"):
            self.vec64[n] = I(n, [2, 64])
        self.subln = I("da_subln_gain", [2, 128])
        self.hy_conv_w = I("hy_conv_w", [2, 3, 1536])
        self.hy_conv_b = I("hy_conv_b", [2, 1536])
        self.hy_w1 = I("hy_w1", [2, 33, 64])
        self.hy_w2 = I("hy_w2", [2, 64, 64])
        self.hy_w3 = I("hy_w3", [2, 64, 64])
        self.hy_w4 = I("hy_w4", [2, 64, 1024])
        self.hy_skip = I("hy_skip", [2, 512])
        self.rope_cos = I("rope_cos", [128, L])
        self.rope_sin = I("rope_sin", [128, L])
        self.na_tab = I("na_tab", [2, 4, 35, 128, 256])
        self.hy_zT = I("hy_zT", [2, 33, L])
        self.hy_tlin = I("hy_tlin", [2, L])
        self.hy_ndel = I("hy_ndel", [128, 4])
        self.dft = I("dft", [4, 128, 128])
        self.consts_d = I("consts", [5, 128, 128])
        self.y = k.dram("y", [L, D], F32, kind="ExternalOutput")
        sk = "ExternalOutput" if debug else "Internal"
        self.XR = k.dram("XR", [LT, D], F32, kind="ExternalOutput" if (debug or single is not None) else "Internal")
        self.QT = k.dram("QT", [1024, LT], BF16, kind="Internal")
        self.KT = k.dram("KT", [1024, LT], BF16, kind="Internal")
        self.V = k.dram("V", [LT, 1024], BF16, kind="Internal")
        self.YT = k.dram("YT", [1024, LT], BF16, kind=sk)
        self.X0T = k.dram("X0T", [1024, L], F32, kind="Internal")
        self.ZT = k.dram("ZT", [1024, L], F32, kind="Internal")
        self.HF = k.dram("HF", [2, 1024, L], F32, kind="Internal")
        self.YH = k.dram("YH", [1024, L], F32, kind="Internal")
        self.r_xr = [k.reg("xr%d" % i) for i in range(LT // 128)]
        self.r_q = k.reg("q")
        self.r_k = k.reg("k")
        self.r_v = k.reg("v")
        self.r_y = k.reg("y")
        self.r_in = k.reg("in")
        self.psb = [k.ps() for _ in range(8)]
        self.cst = k.sb_global([128, 5, 128], F32, "cst")
        k.dma("sp", self.cst[:], self.consts_d[:].rearrange("a p n -> p a n"), writes=[self.cst])
        self.ident = self.cst[:, 0, :]
        self.bd64 = self.cst[:, 1, :]
        self.o128 = self.cst[:, 2, :]
        self.rot = self.cst[:, 3, :]
        self.ones32 = self.cst[:, 4, :]
        self.epsc = k.sb_global([128, 1], F32, "epsc")
        k.op("dve", lambda e: e.memset(self.epsc[:], EPS), writes=[self.epsc])
        self.modT = k.sb_global([128, 4, 8, 2], F32, "modT")
        self.G = [k.sb_global([128, D], F32, "gate") for _ in range(4)]

    def xr_rows(self, r0, n):
        return self.XR[r0:r0 + n, :]

    def copy_in(self):
        k = self.k
        for i in range(0, L, 1024):
            k.dma("sp", self.XR[i:i + 1024, :], self.x[i:i + 1024, :],
                  writes=self.r_xr[i // 128:(i + 1024) // 128])
        k.dma("sp", self.XR[L:LT, :], self.ctx[:, :], writes=self.r_xr[64:66])

    def phase_ada(self, layer):
        k = self.k
        with k.phase():
            sT = k.sb([128, 2, 8], F32, "sT")
            for j in range(2):
                k.dma("sp", sT[:, j, :], self.cc[j:j + 1, :].rearrange("o (kc p) -> p (o kc)", p=128), writes=[sT],
                      allow_slow_non_contiguous=True)
            k.op("act", lambda e: e.activation(out=sT[:], in_=sT[:], func=AF.Silu), reads=[sT], writes=[sT])
            srep = [k.sb([128, 8, 128], F32, "srep") for _ in range(2)]
            for j in range(2):
                for kc in range(8):
                    k.op("dve", lambda e, j=j, kc=kc: e.tensor_copy(out=srep[j][:, kc, :],
                                                                     in_=sT[:, j, kc:kc + 1].to_broadcast([128, 128])),
                         reads=[sT], writes=[srep[j]])
            brow = k.sb([1, 6 * D], F32, "brow")
            k.dma("sp", brow[:], self.b_ada[layer:layer + 1, :], writes=[brow])
            one1 = k.sb([1, 128], F32, "one1")
            k.op("dve", lambda e: e.memset(one1[:], 1.0), writes=[one1])
            wb = [k.sb([128, 8, 512], F32, "wada") for _ in range(2)]
            pcol = self.psb[0]
            roles = [("col", SH1, 0), ("col", SH1, 4), ("col", SC1, 0), ("col", SC1, 4),
                     ("g", 0, 0), ("g", 0, 512), ("col", SH2, 0), ("col", SH2, 4),
                     ("col", SC2, 0), ("col", SC2, 4), ("g", 2, 0), ("g", 2, 512)]
            for blk in range(12):
                w = wb[blk % 2]
                n0 = blk * 512
                k.dma("sp", w[:], self.w_ada[layer, :, n0:n0 + 512].rearrange("(kc p) n -> p kc n", p=128),
                      writes=[w])
                kind, a, b = roles[blk]
                if kind == "col":
                    for c4 in range(4):
                        ch = b + c4
                        col = (a * 8 + ch) * 2
                        for kc in range(8):
                            k.op("pe", lambda e, kc=kc, c4=c4, col=col, w=w: e.matmul(
                                pcol[:, col:col + 2], lhsT=w[:, kc, c4 * 128:(c4 + 1) * 128], rhs=sT[:, :, kc],
                                start=(kc == 0), stop=False), reads=[w, sT], writes=[pcol])
                        k.op("pe", lambda e, c4=c4, col=col, n0=n0: e.matmul(
                            pcol[:, col:col + 2], lhsT=brow[0:1, n0 + c4 * 128:n0 + (c4 + 1) * 128], rhs=one1[0:1, 0:2],
                            start=False, stop=True), reads=[brow, one1], writes=[pcol])
                else:
                    for j in range(2):
                        pg = self.psb[1 + j]
                        for kc in range(8):
                            k.op("pe", lambda e, kc=kc, j=j, pg=pg, w=w: e.matmul(
                                pg[:, :], lhsT=srep[j][:, kc, :], rhs=w[:, kc, :], start=(kc == 0), stop=False),
                                reads=[w, srep[j]], writes=[pg])
                        k.op("pe", lambda e, pg=pg, n0=n0: e.matmul(
                            pg[:, :], lhsT=one1[0:1, :], rhs=brow[0:1, n0:n0 + 512], start=False, stop=True),
                            reads=[brow, one1], writes=[pg])
                        gt = self.G[a + j]
                        k.op("act", lambda e, pg=pg, gt=gt, b=b: e.activation(out=gt[:, b:b + 512], in_=pg[:, :], func=AF.Copy),
                             reads=[pg], writes=[gt])
            mflat = self.modT[:].rearrange("p a c j -> p (a c j)")
            k.op("dve", lambda e: e.tensor_copy(out=mflat, in_=pcol[:, 0:64]), reads=[pcol], writes=[self.modT])
            for a in (SC1, SC2):
                v = self.modT[:, a, :, :]
                k.op("dve", lambda e, v=v: e.tensor_scalar(out=v, in0=v, scalar1=1.0, scalar2=None, op0=ALU.add),
                     reads=[self.modT], writes=[self.modT])

    def load_weight(self, dst, src2d, N, stage, bw=512, engs=("dve", "pool")):
        k = self.k
        for bi, n0 in enumerate(range(0, N, bw)):
            n = min(bw, N - n0)
            st = stage[bi % len(stage)]
            k.dma("sp", st[:, :, :n], src2d[:, n0:n0 + n].rearrange("(kc p) n -> p kc n", p=128), writes=[st])
            k.op(engs[bi % len(engs)], lambda e, st=st, n0=n0, n=n: e.tensor_copy(out=dst[:, :, n0:n0 + n], in_=st[:, :, :n]),
                 reads=[st], writes=[dst])

    def col64x2(self, dst, j, src_row):
        k = self.k
        for h in range(2):
            k.dma("sp", dst[h * 64:(h + 1) * 64, j:j + 1], src_row.rearrange("o d -> d o"), writes=[dst],
                  allow_slow_non_contiguous=True)

    def mk_build(self, a_sh, a_sc):
        k = self.k
        xts = [k.sb([128, D], F32, "xt") for _ in range(2)]
        xns = [k.sb([128, D], F32, "xn") for _ in range(2)]
        junk = k.sb([128, D], BF16, "junk")
        ssqs = [k.sb([128, 1], F32, "ssq") for _ in range(2)]
        rstds = [k.sb([128, 1], F32, "rstd") for _ in range(2)]
        state = {"i": 0}

        def build(ht, col0, r0, lc, n=128):
            i = state["i"]
            state["i"] += 1
            xt, xn, ssq, rstd = xts[i % 2], xns[i % 2], ssqs[i % 2], rstds[i % 2]
            k.dma("sp", xt[:n, :], self.XR[r0:r0 + n, :], writes=[xt])
            k.op("act", lambda e: e.activation(out=junk[:n, :], in_=xt[:n, :], func=AF.Square, accum_out=ssq[:n, 0:1]),
                 reads=[xt], writes=[junk, ssq])
            k.op("dve", lambda e: e.tensor_scalar(out=rstd[:n, :], in0=ssq[:n, :], scalar1=1.0 / D, scalar2=EPS,
                                                  op0=ALU.mult, op1=ALU.add), reads=[ssq], writes=[rstd])
            k.op("act", lambda e: e.activation(out=rstd[:n, :], in_=rstd[:n, :], func=AF.Sqrt), reads=[rstd], writes=[rstd])
            k.op("dve", lambda e: e.reciprocal(out=rstd[:n, :], in_=rstd[:n, :]), reads=[rstd], writes=[rstd])
            k.op("dve", lambda e: e.tensor_scalar(out=xn[:n, :], in0=xt[:n, :], scalar1=rstd[:n, 0:1], scalar2=None, op0=ALU.mult),
                 reads=[xt, rstd], writes=[xn])
            for half in range(2):
                pt = self.psb[half]
                for j in range(4):
                    kc = half * 4 + j
                    k.op("pe", lambda e, j=j, kc=kc, pt=pt: e.transpose(out=pt[:, j * 128:j * 128 + n], in_=xn[:n, kc * 128:(kc + 1) * 128],
                                                                        identity=self.ident[:n, :n]), reads=[xn, self.cst], writes=[pt])
                for j in range(4):
                    kc = half * 4 + j
                    k.op("act", lambda e, j=j, kc=kc, pt=pt: e.activation(
                        out=ht[:, kc, col0:col0 + n], in_=pt[:, j * 128:j * 128 + n], func=AF.Identity,
                        scale=self.modT[:, a_sc, kc, lc:lc + 1], bias=self.modT[:, a_sh, kc, lc:lc + 1]),
                        reads=[pt, self.modT], writes=[ht])
        return build

    def phase_in(self, layer):
        k = self.k
        odd = layer % 2
        li = layer // 2
        NIN = 2304 if odd else 3072
        wsrc = (self.w_in_o if odd else self.w_in_e)[li]
        with k.phase():
            W = k.sb([128, 8, NIN], BF16, "win")
            stage = [k.sb([128, 8, 512], F32, "stg") for _ in range(2)]
            self.load_weight(W, wsrc, NIN, stage)
            gcol = k.sb([128, 4], F32, "gcol")
            names = ["gqa_q_gain", "gqa_k_gain"] if odd else ["na_q_gain", "na_k_gain", "da_q_gain", "da_k_gain"]
            for j, nme in enumerate(names):
                self.col64x2(gcol, j, self.vec64[nme][li:li + 1, :])
            if odd:
                fm = [(c * 128, 0, True, self.QT, c * 128) for c in range(4)] + [(512, 1, True, self.KT, 0)]
                vblocks = [(640, 128, 0)]
                cw = k.sb([128, 3, 12], F32, "hcw")
                cb = k.sb([128, 12], F32, "hcb")
                for j in range(3):
                    k.dma("sp", cw[:, j, :], self.hy_conv_w[li, j:j + 1, :].rearrange("o (c p) -> p (o c)", p=128), writes=[cw],
                          allow_slow_non_contiguous=True)
                k.dma("sp", cb[:], self.hy_conv_b[li:li + 1, :].rearrange("o (c p) -> p (o c)", p=128), writes=[cb],
                      allow_slow_non_contiguous=True)
            else:
                fm = ([(c * 128, 0, False, self.QT, c * 128) for c in range(4)]
                      + [(512 + c * 128, 1, False, self.KT, c * 128) for c in range(4)]
                      + [(1536 + c * 128, 2, True, self.QT, 512 + c * 128) for c in range(4)]
                      + [(2048 + c * 128, 3, True, self.KT, 512 + c * 128) for c in range(4)])
                vblocks = [(1024, 512, 0), (2560, 512, 512)]
            build = self.mk_build(SH1, SC1)
            hts = [k.sb([128, 8, 514], BF16, "ht") for _ in range(2)]
            css = [k.sb([128, 2, 512], F32, "cs") for _ in range(2)]
            mk = lambda shape, dt, nm, n=2: [k.sb(shape, dt, nm) for _ in range(n)]
            sqs, rss, qns, t1s, t2s = (mk([128, 512], F32, x) for x in ("sq", "rs", "qn", "t1", "t2"))
            obs = mk([128, 512], BF16, "ob", 3)
            vts = mk([128, 512], BF16, "vt", 3)
            if odd:
                us = mk([128, 514], F32, "u", 3)
                accs = mk([128, 512], F32, "acc", 3)
            groups = [(g * 512, 512, 0) for g in range(16)] + [(L, 256, 1)]
            ctr = {"c": 0}

            def build_group(gi):
                r0, nt, lc = groups[gi]
                ht = hts[gi % 2]
                for i in range(nt // 128):
                    build(ht, 1 + i * 128, r0 + i * 128, lc)
                if odd:
                    prev = hts[(gi - 1) % 2]
                    if gi == 0 or lc == 1:
                        k.op("pool", lambda e: e.memset(ht[:, :, 0:1], 0.0), writes=[ht])
                    else:
                        k.op("pool", lambda e: e.tensor_copy(out=ht[:, :, 0:1], in_=prev[:, :, 512:513]), reads=[prev], writes=[ht])
                    if lc == 1:
                        k.op("pool", lambda e: e.memset(ht[:, :, nt + 1:nt + 2], 0.0), writes=[ht])
                        k.op("pool", lambda e: e.memset(prev[:, :, 513:514], 0.0), writes=[prev])
                    elif gi > 0:
                        k.op("pool", lambda e: e.tensor_copy(out=prev[:, :, 513:514], in_=ht[:, :, 1:2]), reads=[ht], writes=[prev])

            def project(gi):
                r0, nt, lc = groups[gi]
                ht = hts[gi % 2]
                cs = css[gi % 2]
                if lc == 0:
                    k.dma("sp", cs[:, 0, :], self.rope_cos[:, r0:r0 + nt], writes=[cs])
                    k.dma("sp", cs[:, 1, :], self.rope_sin[:, r0:r0 + nt], writes=[cs])
                for (wc, gk, rope, dst, drow) in fm:
                    c = ctr["c"]
                    ctr["c"] += 1
                    pp, pm, pr = self.psb[2 + c % 2], self.psb[4 + c % 2], self.psb[6 + c % 2]
                    sq, rs, qn, t1, t2, ob = sqs[c % 2], rss[c % 2], qns[c % 2], t1s[c % 2], t2s[c % 2], obs[c % 3]
                    for kc in range(8):
                        k.op("pe", lambda e, kc=kc: e.matmul(pp[:, :nt], lhsT=W[:, kc, wc:wc + 128], rhs=ht[:, kc, 1:1 + nt],
                                                              start=(kc == 0), stop=(kc == 7)), reads=[W, ht], writes=[pp])
                    k.op("act", lambda e: e.activation(out=sq[:, :nt], in_=pp[:, :nt], func=AF.Square), reads=[pp], writes=[sq])
                    k.op("pe", lambda e: e.matmul(pm[:, :nt], lhsT=self.bd64, rhs=sq[:, :nt], start=True, stop=True),
                         reads=[sq, self.cst], writes=[pm])
                    k.op("act", lambda e: e.activation(out=rs[:, :nt], in_=pm[:, :nt], func=AF.Sqrt, bias=self.epsc[:, 0:1]),
                         reads=[pm, self.epsc], writes=[rs])
                    k.op("dve", lambda e: e.reciprocal(out=rs[:, :nt], in_=rs[:, :nt]), reads=[rs], writes=[rs])
                    do_rope = rope and lc == 0
                    qo = qn if do_rope else ob
                    k.op("dve", lambda e: e.scalar_tensor_tensor(out=qo[:, :nt], in0=pp[:, :nt], scalar=gcol[:, gk:gk + 1],
                                                                 in1=rs[:, :nt], op0=ALU.mult, op1=ALU.mult),
                         reads=[pp, rs, gcol], writes=[qo])
                    if do_rope:
                        k.op("pe", lambda e: e.matmul(pr[:, :nt], lhsT=self.rot, rhs=qn[:, :nt], start=True, stop=True),
                             reads=[qn, self.cst], writes=[pr])
                        k.op("pool", lambda e: e.tensor_tensor(out=t1[:, :nt], in0=qn[:, :nt], in1=cs[:, 0, :nt], op=ALU.mult),
                             reads=[qn, cs], writes=[t1])
                        k.op("dve", lambda e: e.tensor_tensor(out=t2[:, :nt], in0=pr[:, :nt], in1=cs[:, 1, :nt], op=ALU.mult),
                             reads=[pr, cs], writes=[t2])
                        k.op("pool", lambda e: e.tensor_tensor(out=ob[:, :nt], in0=t1[:, :nt], in1=t2[:, :nt], op=ALU.add),
                             reads=[t1, t2], writes=[ob])
                    k.dma("pool", dst[drow:drow + 128, r0:r0 + nt], ob[:, :nt], reads=[ob])
                for (wc, wn, vcol) in vblocks:
                    for i in range(nt // 128):
                        c = ctr["c"]
                        ctr["c"] += 1
                        pp = self.psb[2 + c % 2]
                        vt = vts[c % 3]
                        for kc in range(8):
                            k.op("pe", lambda e, kc=kc: e.matmul(pp[:, :wn], lhsT=ht[:, kc, 1 + i * 128:1 + (i + 1) * 128],
                                                                  rhs=W[:, kc, wc:wc + wn], start=(kc == 0), stop=(kc == 7)),
                                 reads=[W, ht], writes=[pp])
                        k.op("act", lambda e: e.activation(out=vt[:, :wn], in_=pp[:, :wn], func=AF.Copy), reads=[pp], writes=[vt])
                        k.dma("pool", self.V[r0 + i * 128:r0 + (i + 1) * 128, vcol:vcol + wn], vt[:, :wn], reads=[vt])
                if odd:
                    hw = (nt + 2) // 2
                    zrow = 0 if lc == 0 else 512
                    tcol = r0 if lc == 0 else 0
                    for c4 in range(4):
                        cv = []
                        for part in range(3):
                            ch = part * 4 + c4
                            wc = 768 + ch * 128
                            c = ctr["c"]
                            ctr["c"] += 1
                            pa, pb = self.psb[2 + 2 * (c % 3)], self.psb[3 + 2 * (c % 3)]
                            u, acc = us[c % 3], accs[c % 3]
                            for hlf, pz in ((0, pa), (1, pb)):
                                for kc in range(8):
                                    k.op("pe", lambda e, kc=kc, hlf=hlf, pz=pz: e.matmul(
                                        pz[:, :hw], lhsT=W[:, kc, wc:wc + 128], rhs=ht[:, kc, hlf * hw:(hlf + 1) * hw],
                                        start=(kc == 0), stop=(kc == 7)), reads=[W, ht], writes=[pz])
                                k.op("act", lambda e, hlf=hlf, pz=pz: e.activation(out=u[:, hlf * hw:(hlf + 1) * hw], in_=pz[:, :hw],
                                                                                   func=AF.Copy), reads=[pz], writes=[u])
                            k.op("dve", lambda e: e.tensor_scalar(out=acc[:, :nt], in0=u[:, 0:nt], scalar1=cw[:, 0, ch:ch + 1],
                                                                  scalar2=cb[:, ch:ch + 1], op0=ALU.mult, op1=ALU.add),
                                 reads=[u, cw, cb], writes=[acc])
                            for j in (1, 2):
                                k.op("dve", lambda e, j=j: e.scalar_tensor_tensor(out=acc[:, :nt], in0=u[:, j:j + nt],
                                                                                  scalar=cw[:, j, ch:ch + 1], in1=acc[:, :nt],
                                                                                  op0=ALU.mult, op1=ALU.add),
                                     reads=[u, cw, acc], writes=[acc])
                            cv.append(acc)
                        k.dma("pool", self.X0T[zrow + c4 * 128:zrow + (c4 + 1) * 128, tcol:tcol + nt], cv[0][:, :nt], reads=[cv[0]])
                        k.op("pool", lambda e: e.tensor_tensor(out=cv[2][:, :nt], in0=cv[2][:, :nt], in1=cv[1][:, :nt], op=ALU.mult),
                             reads=[cv[1], cv[2]], writes=[cv[2]])
                        k.dma("pool", self.ZT[zrow + c4 * 128:zrow + (c4 + 1) * 128, tcol:tcol + nt], cv[2][:, :nt], reads=[cv[2]])

            build_group(0)
            for gi in range(len(groups)):
                if gi + 1 < len(groups):
                    build_group(gi + 1)
                project(gi)

    def attn_unit(self, S, keys, PV, Wd, pts, st, hook=None):
        k = self.k
        nk = len(keys)

        def scores(ki):
            kt, _ = keys[ki]
            ps = self.psb[ki % 3]
            for (kfn, ktile, rhs, rtile, c0, n) in S:
                k.op("pe", lambda e, kfn=kfn, rhs=rhs, c0=c0, n=n: e.matmul(ps[:, c0:c0 + n], lhsT=kfn(kt), rhs=rhs,
                                                                           start=True, stop=True),
                     reads=[ktile, rtile], writes=[ps])

        scores(0)
        if nk > 1:
            scores(1)
        for ki in range(nk):
            kt, bias = keys[ki]
            ps = self.psb[ki % 3]
            pt = pts[st["p"] % len(pts)]
            st["p"] += 1
            if bias is not None:
                btile, bap = bias
                tmp = st["tmp"][ki % 2]
                k.op("dve", lambda e, bap=bap, tmp=tmp: e.scalar_tensor_tensor(out=tmp[:, :Wd], in0=ps[:, :Wd], scalar=0.125, in1=bap,
                                                                               op0=ALU.mult, op1=ALU.add),
                     reads=[ps, btile], writes=[tmp])
                k.op("act", lambda e, tmp=tmp: e.activation(out=pt[:, :Wd], in_=tmp[:, :Wd], func=AF.Exp), reads=[tmp], writes=[pt])
            else:
                k.op("act", lambda e: e.activation(out=pt[:, :Wd], in_=ps[:, :Wd], func=AF.Exp, scale=0.125), reads=[ps], writes=[pt])
            for (vfn, vtile, c0, n, acc_ap, acct) in PV:
                k.op("pe", lambda e, vfn=vfn, c0=c0, n=n, acc_ap=acc_ap: e.matmul(acc_ap, lhsT=vfn(kt), rhs=pt[:, c0:c0 + n],
                                                                                 start=(ki == 0), stop=(ki == nk - 1)),
                     reads=[vtile, pt], writes=[acct])
            if hook is not None:
                hook(ki, pt)
            if ki + 2 < nk:
                scores(ki + 2)

    def load_vx(self, vx, c0, w, ones):
        k = self.k
        if ones:
            k.op("pool", lambda e: e.memset(vx[:, :, w:128], 1.0), writes=[vx])
        for a in range(0, 66, 11):
            k.dma("sp", vx[:, a:a + 11, 0:w], self.V[a * 128:(a + 11) * 128, c0:c0 + w].rearrange("(kt p) d -> p kt d", p=128),
                  writes=[vx])

    def epi_shift(self, pieces, Wd, dshs, yts, ui, dst_ap):
        k = self.k
        dsh, yt = dshs[ui % 2], yts[ui % 2]
        for (acct, a0, o0, n) in pieces:
            k.op("dve", lambda e, acct=acct, a0=a0, o0=o0, n=n: e.tensor_copy(out=dsh[0:64, o0:o0 + n], in_=acct[64:128, a0:a0 + n]),
                 reads=[acct], writes=[dsh])
        k.op("dve", lambda e: e.reciprocal(out=dsh[0:64, :Wd], in_=dsh[0:64, :Wd]), reads=[dsh], writes=[dsh])
        for (acct, a0, o0, n) in pieces:
            k.op("dve", lambda e, acct=acct, a0=a0, o0=o0, n=n: e.tensor_tensor(out=yt[0:64, o0:o0 + n], in0=acct[0:64, a0:a0 + n],
                                                                              in1=dsh[0:64, o0:o0 + n], op=ALU.mult),
                 reads=[acct, dsh], writes=[yt])
        k.dma("pool", dst_ap, yt[0:64, :Wd], reads=[yt])

    def phase_gqa(self, layer, with_ctx):
        k = self.k
        with k.phase():
            pts = [k.sb([128, 512], BF16, "pt") for _ in range(3)]
            dshs = [k.sb([64, 512], F32, "dsh") for _ in range(2)]
            yts = [k.sb([64, 512], BF16, "yt") for _ in range(2)]
            qs = [k.sb([64, 512], BF16, "q") for _ in range(3)]
            st = {"p": 0}
            ui = 0
            for g in range(2):
              with k.phase():
                ktg = k.sb([64, LT], BF16, "ktg")
                vx = k.sb([128, 66, 128], BF16, "vx")
                k.dma("sp", ktg[:, :], self.KT[g * 64:(g + 1) * 64, :], writes=[ktg])
                self.load_vx(vx, g * 64, 64, True)
                units = [(qb * 128, list(range(66))) for qb in range(64)]
                if with_ctx:
                    units += [(L + qb * 128, [64, 65]) for qb in range(2)]
                for (c0, kts) in units:
                    q = qs[ui % 3]
                    k.dma("sp", q[:, :].rearrange("d (h t) -> d h t", h=4),
                          self.QT[g * 256:(g + 1) * 256, c0:c0 + 128].rearrange("(h d) t -> d h t", d=64), writes=[q])
                    acct = self.psb[3 + ui % 2]
                    S = [(lambda kt: ktg[:, kt * 128:(kt + 1) * 128], ktg, q[:, :], q, 0, 512)]
                    PV = [(lambda kt: vx[:, kt, :], vx, 0, 512, acct[:, 0:512], acct)]
                    self.attn_unit(S, [(kt, None) for kt in kts], PV, 512, pts, st)
                    dst = self.YT[g * 256:(g + 1) * 256, c0:c0 + 128].rearrange("(h d) t -> d h t", d=64)
                    self.epi_shift([(acct, 0, 0, 512)], 512, dshs, yts, ui, dst)
                    ui += 1

    def phase_na(self, layer, with_ctx):
        k = self.k
        li = layer // 2
        with k.phase():
            pts = [k.sb([128, 512], BF16, "pt") for _ in range(3)]
            tmps = [k.sb([128, 512], F32, "tmp") for _ in range(2)]
            dshs = [k.sb([64, 512], F32, "dsh") for _ in range(2)]
            yts = [k.sb([64, 512], BF16, "yt") for _ in range(2)]
            qs = [k.sb([64, 256], BF16, "q") for _ in range(3)]
            ui = 0
            pc = 0
            for hg in range(4):
              with k.phase():
                kth = [k.sb([64, LT], BF16, "kth") for _ in range(2)]
                vxs = [k.sb([128, 66, 128], BF16, "vx") for _ in range(2)]
                tab = k.sb([128, 35, 256], F32, "tab")
                k.dma("sp", tab[:], self.na_tab[li, hg, :, :, :].rearrange("t p n -> p t n"), writes=[tab])
                for i in range(2):
                    h = 2 * hg + i
                    k.dma("sp", kth[i][:, :], self.KT[h * 64:(h + 1) * 64, :], writes=[kth[i]])
                    self.load_vx(vxs[i], h * 64, 64, True)
                units = []
                for qt in range(64):
                    cls, kt0 = _na_cls(qt), _na_kt0(qt)
                    keys = [(kt0 + r, cls * 7 + r) for r in range(5)] + [(64, cls * 7 + 5), (65, cls * 7 + 6)]
                    units.append((qt * 128, [keys[0:2], keys[2:4], keys[4:6], keys[6:7]]))
                if with_ctx:
                    units += [(L + cq * 128, [[(64, 5), (65, 6)]]) for cq in range(2)]
                for (c0, steps) in units:
                    q = qs[ui % 3]
                    k.dma("sp", q[:, :].rearrange("d (h t) -> d h t", h=2),
                          self.QT[hg * 128:(hg + 1) * 128, c0:c0 + 128].rearrange("(h d) t -> d h t", d=64), writes=[q])
                    accs_ = [self.psb[3 + 2 * i + ui % 2] for i in range(2)]
                    ns = len(steps)

                    def scores(si):
                        ps = self.psb[si % 3]
                        for a, (kt, _) in enumerate(steps[si]):
                            for i in range(2):
                                k.op("pe", lambda e, a=a, kt=kt, i=i, ps=ps: e.matmul(
                                    ps[:, a * 256 + i * 128:a * 256 + (i + 1) * 128], lhsT=kth[i][:, kt * 128:(kt + 1) * 128],
                                    rhs=q[:, i * 128:(i + 1) * 128], start=True, stop=True), reads=[kth[i], q], writes=[ps])

                    scores(0)
                    if ns > 1:
                        scores(1)
                    for si in range(ns):
                        ps = self.psb[si % 3]
                        pt, tmp = pts[pc % 3], tmps[pc % 2]
                        pc += 1
                        n = len(steps[si])
                        Wd = 256 * n
                        t0 = steps[si][0][1]
                        bap = tab[:, t0:t0 + n, :].rearrange("p a n -> p (a n)")
                        k.op("dve", lambda e, ps=ps, tmp=tmp, bap=bap, Wd=Wd: e.scalar_tensor_tensor(
                            out=tmp[:, :Wd], in0=ps[:, :Wd], scalar=0.125, in1=bap, op0=ALU.mult, op1=ALU.add),
                            reads=[ps, tab], writes=[tmp])
                        k.op("act", lambda e, tmp=tmp, pt=pt, Wd=Wd: e.activation(out=pt[:, :Wd], in_=tmp[:, :Wd], func=AF.Exp),
                             reads=[tmp], writes=[pt])
                        for a, (kt, _) in enumerate(steps[si]):
                            first = (si == 0 and a == 0)
                            lastk = (si == ns - 1 and a == n - 1)
                            for i in range(2):
                                k.op("pe", lambda e, a=a, kt=kt, i=i, pt=pt, first=first, lastk=lastk: e.matmul(
                                    accs_[i][:, 0:128], lhsT=vxs[i][:, kt, :], rhs=pt[:, a * 256 + i * 128:a * 256 + (i + 1) * 128],
                                    start=first, stop=lastk), reads=[vxs[i], pt], writes=[accs_[i]])
                        if si + 2 < ns:
                            scores(si + 2)
                    dst = self.YT[hg * 128:(hg + 1) * 128, c0:c0 + 128].rearrange("(h d) t -> d h t", d=64)
                    self.epi_shift([(accs_[i], 0, i * 128, 128) for i in range(2)], 256, dshs, yts, ui, dst)
                    ui += 1

    def phase_da(self, layer, with_ctx):
        k = self.k
        li = layer // 2
        lam_init = 0.8 - 0.6 * math.exp(-0.3 * layer)
        with k.phase():
            lq = k.sb([128, 4, 64], F32, "lq")
            for j, nme in enumerate(("da_lambda_q1", "da_lambda_k1", "da_lambda_q2", "da_lambda_k2")):
                k.dma("sp", lq[:, j, :], self.vec64[nme][li:li + 1, :].partition_broadcast(128), writes=[lq])
            pr = k.sb([128, 2, 64], F32, "lpr")
            ssum = k.sb([128, 2], F32, "lsum")
            neglam = k.sb([128, 1], F32, "neglam")
            for j in range(2):
                k.op("dve", lambda e, j=j: e.tensor_tensor(out=pr[:, j, :], in0=lq[:, 2 * j, :], in1=lq[:, 2 * j + 1, :], op=ALU.mult),
                     reads=[lq], writes=[pr])
                k.op("dve", lambda e, j=j: e.tensor_reduce(out=ssum[:, j:j + 1], in_=pr[:, j, :], axis=mybir.AxisListType.X, op=ALU.add),
                     reads=[pr], writes=[ssum])
            k.op("act", lambda e: e.activation(out=ssum[:, :], in_=ssum[:, :], func=AF.Exp), reads=[ssum], writes=[ssum])
            k.op("dve", lambda e: e.tensor_tensor(out=neglam[:, :], in0=ssum[:, 1:2], in1=ssum[:, 0:1], op=ALU.subtract),
                 reads=[ssum], writes=[neglam])
            k.op("dve", lambda e: e.tensor_scalar(out=neglam[:, :], in0=neglam[:, :], scalar1=-lam_init, scalar2=None, op0=ALU.add),
                 reads=[neglam], writes=[neglam])
            slcol = k.sb([128, 1], F32, "slcol")
            k.dma("sp", slcol[:, :], self.subln[li:li + 1, :].rearrange("o d -> d o"), writes=[slcol], allow_slow_non_contiguous=True)
            k.op("dve", lambda e: e.tensor_scalar(out=slcol[:, :], in0=slcol[:, :], scalar1=1.0 - lam_init, scalar2=None, op0=ALU.mult),
                 reads=[slcol], writes=[slcol])
            pts = [k.sb([128, 512], BF16, "pt") for _ in range(3)]
            qs = [k.sb([64, 512], BF16, "q") for _ in range(3)]
            mk = lambda nm: [k.sb([128, 512], F32, nm) for _ in range(2)]
            paccs, rds, ys = mk("pacc"), mk("rd"), mk("yy")
            outs = [k.sb([128, 256], BF16, "ob") for _ in range(2)]
            st = {"p": 0}
            ui = 0
            for j in range(4):
              with k.phase():
                k1 = k.sb([64, LT], BF16, "k1")
                k2 = k.sb([64, LT], BF16, "k2")
                vj = k.sb([128, 66, 128], BF16, "vj")
                r0 = 512 + j * 128
                k.dma("sp", k1[:, :], self.KT[r0:r0 + 64, :], writes=[k1])
                k.dma("sp", k2[:, :], self.KT[r0 + 64:r0 + 128, :], writes=[k2])
                self.load_vx(vj, r0, 128, False)
                units = [(qb * 256, list(range(66))) for qb in range(32)]
                if with_ctx:
                    units.append((L, [64, 65]))
                for (c0, kts) in units:
                    q = qs[ui % 3]
                    k.dma("sp", q[:, :].rearrange("d (h t) -> d h t", h=2),
                          self.QT[r0:r0 + 128, c0:c0 + 256].rearrange("(h d) t -> d h t", d=64), writes=[q])
                    accn = self.psb[3 + ui % 2]
                    pd, pm = self.psb[5], self.psb[6]
                    pacc, rd, y = paccs[ui % 2], rds[ui % 2], ys[ui % 2]
                    ob = outs[ui % 2]
                    S = [(lambda kt: k1[:, kt * 128:(kt + 1) * 128], k1, q[:, 0:256], q, 0, 256),
                         (lambda kt: k2[:, kt * 128:(kt + 1) * 128], k2, q[:, 256:512], q, 256, 256)]
                    PV = [(lambda kt: vj[:, kt, :], vj, 0, 512, accn[:, 0:512], accn)]

                    def hook(ki, pt, pacc=pacc):
                        if ki == 0:
                            k.op("dve", lambda e: e.tensor_copy(out=pacc[:, :], in_=pt[:, :]), reads=[pt], writes=[pacc])
                        else:
                            k.op("dve", lambda e: e.tensor_tensor(out=pacc[:, :], in0=pacc[:, :], in1=pt[:, :], op=ALU.add),
                                 reads=[pt, pacc], writes=[pacc])
                    self.attn_unit(S, [(kt, None) for kt in kts], PV, 512, pts, st, hook=hook)
                    k.op("pe", lambda e: e.matmul(pd[:, :], lhsT=self.ones32, rhs=pacc[:, :], start=True, stop=True),
                         reads=[pacc, self.cst], writes=[pd])
                    k.op("dve", lambda e: e.reciprocal(out=rd[:, :], in_=pd[:, :]), reads=[pd], writes=[rd])
                    k.op("dve", lambda e: e.tensor_tensor(out=rd[:, :], in0=accn[:, :], in1=rd[:, :], op=ALU.mult),
                         reads=[accn, rd], writes=[rd])
                    k.op("dve", lambda e: e.scalar_tensor_tensor(out=y[:, 0:256], in0=rd[:, 256:512], scalar=neglam[:, 0:1],
                                                                 in1=rd[:, 0:256], op0=ALU.mult, op1=ALU.add),
                         reads=[rd, neglam], writes=[y])
                    k.op("act", lambda e: e.activation(out=y[:, 256:512], in_=y[:, 0:256], func=AF.Square), reads=[y], writes=[y])
                    k.op("pe", lambda e: e.matmul(pm[:, 0:256], lhsT=self.o128, rhs=y[:, 256:512], start=True, stop=True),
                         reads=[y, self.cst], writes=[pm])
                    k.op("act", lambda e: e.activation(out=rd[:, 0:256], in_=pm[:, 0:256], func=AF.Sqrt, bias=self.epsc[:, 0:1]),
                         reads=[pm, self.epsc], writes=[rd])
                    k.op("dve", lambda e: e.reciprocal(out=rd[:, 0:256], in_=rd[:, 0:256]), reads=[rd], writes=[rd])
                    k.op("dve", lambda e: e.scalar_tensor_tensor(out=ob[:, :], in0=y[:, 0:256], scalar=slcol[:, 0:1], in1=rd[:, 0:256],
                                                                 op0=ALU.mult, op1=ALU.mult), reads=[y, slcol, rd], writes=[ob])
                    k.dma("pool", self.YT[r0:r0 + 128, c0:c0 + 256], ob[:, :], reads=[ob])
                    ui += 1


    def phase_hyena(self, layer, with_ctx):
        k = self.k
        li = layer // 2
        PI = math.pi
        sets = [0, 1] if with_ctx else [0]
        rnorms = {}
        with k.phase():
            rn_all = k.sb([128, 2, 4], F32, "rnorm")
            skipc = k.sb([128, 4], F32, "skipc")
            k.dma("sp", skipc[:, :], self.hy_skip[li:li + 1, :].rearrange("o (c p) -> p (o c)", p=128), writes=[skipc],
                  allow_slow_non_contiguous=True)
            with k.phase():
                w1 = k.sb([33, 64], F32, "w1")
                w2 = k.sb([64, 64], F32, "w2")
                w3 = k.sb([64, 64], F32, "w3")
                w4 = k.sb([64, 1024], F32, "w4")
                k.dma("sp", w1[:, :], self.hy_w1[li, :, :], writes=[w1])
                k.dma("sp", w2[:, :], self.hy_w2[li, :, :], writes=[w2])
                k.dma("sp", w3[:, :], self.hy_w3[li, :, :], writes=[w3])
                k.dma("sp", w4[:, :], self.hy_w4[li, :, :], writes=[w4])
                fc = k.sb([64, 4], F32, "fcol")
                for j, nme in enumerate(("hy_freq", "hy_b1", "hy_b2", "hy_b3")):
                    k.dma("sp", fc[:, j:j + 1], self.vec64[nme][li:li + 1, :].rearrange("o d -> d o"), writes=[fc],
                          allow_slow_non_contiguous=True)
                for j in (1, 2, 3):
                    k.op("dve", lambda e, j=j: e.tensor_tensor(out=fc[:, j:j + 1], in0=fc[:, j:j + 1], in1=fc[:, 0:1], op=ALU.mult),
                         reads=[fc], writes=[fc])
                negpi = k.sb([64, 1], F32, "negpi")
                k.op("dve", lambda e: e.memset(negpi[:, :], -PI), writes=[negpi])
                ndel = k.sb([128, 4], F32, "ndel")
                k.dma("sp", ndel[:, :], self.hy_ndel[:, :], writes=[ndel])
                zts = [k.sb([33, 512], F32, "zt") for _ in range(2)]
                tls = [k.sb([128, 512], F32, "tl") for _ in range(2)]
                hs = [k.sb([64, 512], F32, "hh") for _ in range(3)]
                decs = [k.sb([128, 512], F32, "dec") for _ in range(2)]
                hks = [k.sb([128, 512], F32, "hk") for _ in range(3)]
                junk = k.sb([128, 512], F32, "junkf")
                hi_ = k.sb([64, 512], mybir.dt.int32, "hi")
                hf_ = k.sb([64, 512], F32, "hf")
                part = k.sb([128, 8, 16], F32, "part")
                nrm = k.sb([128, 8], F32, "nrm")
                bi = 0
                for s_ in sets:
                    k.op("dve", lambda e: e.memset(part[:], 0.0), writes=[part])
                    nblk = 16 if s_ == 0 else 1
                    for blk in range(nblk):
                        zt, tl = zts[bi % 2], tls[bi % 2]
                        c0 = blk * 512
                        k.dma("sp", zt[:, :], self.hy_zT[s_, :, c0:c0 + 512], writes=[zt])
                        k.dma("sp", tl[:, :], self.hy_tlin[s_:s_ + 1, c0:c0 + 512].partition_broadcast(128), writes=[tl])
                        src, srct, kk = zt[:, :], zt, 33
                        for li_, wm in enumerate((w1, w2, w3)):
                            ps = self.psb[li_ % 2]
                            h = hs[li_]
                            k.op("pe", lambda e, wm=wm, src=src, kk=kk, ps=ps: e.matmul(ps[0:64, :], lhsT=wm[0:kk, :], rhs=src,
                                                                                      start=True, stop=True),
                                 reads=[wm, srct], writes=[ps])
                            k.op("dve", lambda e, ps=ps, h=h, li_=li_: e.tensor_scalar(
                                out=h[:, :], in0=ps[0:64, :], scalar1=fc[:, 0:1], scalar2=fc[:, li_ + 1:li_ + 2],
                                op0=ALU.mult, op1=ALU.add), reads=[ps, fc], writes=[h])
                            k.op("dve", lambda e, h=h: e.tensor_scalar(out=h[:, :], in0=h[:, :], scalar1=1.0 / (2.0 * PI), scalar2=16.5,
                                                                       op0=ALU.mult, op1=ALU.add), reads=[h], writes=[h])
                            k.op("dve", lambda e, h=h: e.tensor_copy(out=hi_[:, :], in_=h[:, :]), reads=[h], writes=[hi_])
                            k.op("dve", lambda e, h=h: e.tensor_copy(out=hf_[:, :], in_=hi_[:, :]), reads=[hi_], writes=[hf_])
                            k.op("dve", lambda e, h=h: e.tensor_tensor(out=h[:, :], in0=h[:, :], in1=hf_[:, :], op=ALU.subtract),
                                 reads=[h, hf_], writes=[h])
                            k.op("dve", lambda e, h=h: e.scalar_tensor_tensor(out=h[:, :], in0=h[:, :], scalar=0.0, in1=h[:, :],
                                                                              op0=ALU.is_lt, op1=ALU.add), reads=[h], writes=[h])
                            k.op("act", lambda e, h=h: e.activation(out=h[:, :], in_=h[:, :], func=AF.Sin, bias=negpi[:, 0:1], scale=2.0 * PI),
                                 reads=[h, negpi], writes=[h])
                            src, srct, kk = h[:, :], h, 64
                        h3 = hs[2]
                        for cch in range(8):
                            cc = cch % 4
                            ps = self.psb[2 + cch % 2]
                            dec, hk = decs[cch % 2], hks[cch % 3]
                            k.op("pe", lambda e, cch=cch, ps=ps: e.matmul(ps[:, :], lhsT=w4[:, cch * 128:(cch + 1) * 128], rhs=h3[:, :],
                                                                         start=True, stop=True), reads=[w4, h3], writes=[ps])
                            k.op("act", lambda e, dec=dec, cc=cc: e.activation(out=dec[:, :], in_=tl[:, :], func=AF.Exp,
                                                                               scale=ndel[:, cc:cc + 1]), reads=[tl, ndel], writes=[dec])
                            k.op("dve", lambda e, ps=ps, dec=dec, hk=hk: e.tensor_tensor(out=hk[:, :], in0=ps[:, :], in1=dec[:, :],
                                                                                       op=ALU.mult), reads=[ps, dec], writes=[hk])
                            if cch >= 4 and blk == 0:
                                k.op("dve", lambda e, hk=hk: e.memset(hk[:, 0:1], 0.0), writes=[hk])
                            k.op("act", lambda e, hk=hk, cch=cch, blk=blk: e.activation(out=junk[:, :], in_=hk[:, :], func=AF.Abs,
                                                                                       accum_out=part[:, cch, blk:blk + 1]),
                                 reads=[hk], writes=[junk, part])
                            k.dma("pool", self.HF[s_, cch * 128:(cch + 1) * 128, c0:c0 + 512], hk[:, :], reads=[hk])
                        bi += 1
                    k.op("dve", lambda e: e.tensor_reduce(out=nrm[:, :], in_=part[:, :, :], axis=mybir.AxisListType.X, op=ALU.add),
                         reads=[part], writes=[nrm])
                    k.op("dve", lambda e, s_=s_: e.tensor_tensor(out=rn_all[:, s_, :], in0=nrm[:, 0:4], in1=nrm[:, 4:8], op=ALU.add),
                         reads=[nrm], writes=[rn_all])
                    k.op("dve", lambda e, s_=s_: e.reciprocal(out=rn_all[:, s_, :], in_=rn_all[:, s_, :]), reads=[rn_all], writes=[rn_all])
            with k.phase():
                dft = k.sb([128, 4, 128], F32, "dft")
                k.dma("sp", dft[:], self.dft[:, :, :].rearrange("a p n -> p a n"), writes=[dft])
                Fc, Fs, TWc, TWs = (dft[:, a, :] for a in range(4))
                fx = k.sb([128, 6, 128], F32, "fx")
                k.op("dve", lambda e: e.tensor_copy(out=fx[:, 0, :], in_=Fc), reads=[dft], writes=[fx])
                k.op("dve", lambda e: e.tensor_copy(out=fx[:, 1, :], in_=Fs), reads=[dft], writes=[fx])
                k.op("dve", lambda e: e.tensor_scalar(out=fx[:, 2, :], in0=Fs, scalar1=-1.0, scalar2=None, op0=ALU.mult),
                     reads=[dft], writes=[fx])
                k.op("dve", lambda e: e.tensor_scalar(out=fx[:, 3, :], in0=Fc, scalar1=-1.0, scalar2=None, op0=ALU.mult),
                     reads=[dft], writes=[fx])
                k.op("dve", lambda e: e.tensor_copy(out=fx[:, 4, :], in_=Fs), reads=[dft], writes=[fx])
                k.op("dve", lambda e: e.tensor_copy(out=fx[:, 5, :], in_=Fc), reads=[dft], writes=[fx])
                FcFs = fx[:, 0:2, :].rearrange("p a n -> p (a n)")
                FcnFs = None
                nFs, nFc = fx[:, 2, :], fx[:, 3, :]
                r1 = k.sb([128, 2, 128], F32, "r1")
                k.op("dve", lambda e: e.tensor_copy(out=r1[:, 0, :], in_=Fc), reads=[dft], writes=[r1])
                k.op("dve", lambda e: e.tensor_copy(out=r1[:, 1, :], in_=nFs), reads=[fx], writes=[r1])
                R1 = r1[:, :, :].rearrange("p a n -> p (a n)")
                R2 = fx[:, 4:6, :].rearrange("p a n -> p (a n)")
                mk = lambda shape, nm, n=2: [k.sb(shape, F32, nm) for _ in range(n)]
                dts = mk([64, 3, 4, 128], "dt")
                p1s, p2s = mk([128, 512], "p1"), mk([128, 512], "p2")
                Bs = mk([128, 3, 2, 512], "B", 2)
                kfs = mk([128, 2, 512], "kf")
                Ys = mk([128, 2, 512], "Y")
                tts = mk([128, 2, 512], "tt")
                Ds = mk([128, 2, 512], "D")
                youts = mk([64, 512], "yo")
                gi = 0
                for s_ in sets:
                    nr = 64 if s_ == 0 else 2
                    ncol = nr * 128
                    zrow = 0 if s_ == 0 else 512
                    for g4 in range(128):
                        c0 = g4 * 4
                        dt_, B, kf, Y, tt, Dd, yo = dts[gi % 2], Bs[gi % 2], kfs[gi % 2], Ys[gi % 2], tts[gi % 2], Ds[gi % 2], youts[gi % 2]
                        srcs = (self.ZT[zrow + c0:zrow + c0 + 4, 0:ncol], self.HF[s_, c0:c0 + 4, 0:ncol],
                                self.HF[s_, 512 + c0:512 + c0 + 4, 0:ncol])
                        for sg in range(3):
                            k.dma("sp", dt_[0:nr, sg, :, :], srcs[sg].rearrange("c (a b) -> a c b", b=128), writes=[dt_])
                        for sg in range(3):
                            for pr_ in range(2):
                                pb = self.psb[(sg * 2 + pr_) % 2]
                                p1, p2 = p1s[(sg * 2 + pr_) % 2], p2s[(sg * 2 + pr_) % 2]
                                for jj in range(2):
                                    j = pr_ * 2 + jj
                                    k.op("pe", lambda e, sg=sg, j=j, jj=jj, pb=pb: e.matmul(
                                        pb[:, jj * 256:(jj + 1) * 256], lhsT=dt_[0:nr, sg, j, :], rhs=FcFs[0:nr, :], start=True, stop=True),
                                        reads=[dt_, fx], writes=[pb])
                                pbv = pb[:, :].rearrange("p (a n) -> p a n", n=128)
                                k.op("dve", lambda e, pbv=pbv, p1=p1: e.tensor_tensor(
                                    out=p1[:, :].rearrange("p (a n) -> p a n", n=128), in0=pbv, in1=TWc.unsqueeze(1).to_broadcast([128, 4, 128]),
                                    op=ALU.mult), reads=[pb, dft], writes=[p1])
                                k.op("dve", lambda e, pbv=pbv, p2=p2: e.tensor_tensor(
                                    out=p2[:, :].rearrange("p (a n) -> p a n", n=128), in0=pbv, in1=TWs.unsqueeze(1).to_broadcast([128, 4, 128]),
                                    op=ALU.mult), reads=[pb, dft], writes=[p2])
                                for jj in range(2):
                                    j = pr_ * 2 + jj
                                    k.op("pool", lambda e, sg=sg, j=j, jj=jj, p1=p1, p2=p2: e.tensor_tensor(
                                        out=B[:, sg, 0, j * 128:(j + 1) * 128], in0=p1[:, jj * 256:jj * 256 + 128],
                                        in1=p2[:, jj * 256 + 128:jj * 256 + 256], op=ALU.subtract), reads=[p1, p2], writes=[B])
                                    k.op("pool", lambda e, sg=sg, j=j, jj=jj, p1=p1, p2=p2: e.tensor_tensor(
                                        out=B[:, sg, 1, j * 128:(j + 1) * 128], in0=p2[:, jj * 256:jj * 256 + 128],
                                        in1=p1[:, jj * 256 + 128:jj * 256 + 256], op=ALU.add), reads=[p1, p2], writes=[B])
                        xre, xim, kre, kim = self.psb[2], self.psb[3], self.psb[4], self.psb[5]
                        mm = lambda out, lhsT, rhs, st_, sp_, rt: k.op(
                            "pe", lambda e: e.matmul(out[:, :], lhsT=lhsT, rhs=rhs, start=st_, stop=sp_), reads=[rt, fx, dft], writes=[out])
                        mm(xre, Fc, B[:, 0, 0, :], True, False, B)
                        mm(xre, nFs, B[:, 0, 1, :], False, True, B)
                        mm(xim, Fc, B[:, 0, 1, :], True, False, B)
                        mm(xim, Fs, B[:, 0, 0, :], False, True, B)
                        mm(kre, Fc, B[:, 1, 0, :], True, False, B)
                        mm(kre, nFs, B[:, 1, 1, :], False, False, B)
                        mm(kre, Fc, B[:, 2, 0, :], False, False, B)
                        mm(kre, nFs, B[:, 2, 1, :], False, True, B)
                        mm(kim, Fc, B[:, 1, 1, :], True, False, B)
                        mm(kim, Fs, B[:, 1, 0, :], False, False, B)
                        mm(kim, nFc, B[:, 2, 1, :], False, False, B)
                        mm(kim, nFs, B[:, 2, 0, :], False, True, B)
                        k.op("act", lambda e: e.activation(out=kf[:, 0, :], in_=kre[:, :], func=AF.Copy), reads=[kre], writes=[kf])
                        k.op("act", lambda e: e.activation(out=kf[:, 1, :], in_=kim[:, :], func=AF.Copy), reads=[kim], writes=[kf])
                        tt4 = lambda out, a, b, op, rd, wr, eng="dve": k.op(eng, lambda e: e.tensor_tensor(out=out, in0=a, in1=b, op=op),
                                                                            reads=rd, writes=wr)
                        tt4(tt[:, 0, :], xre[:, :], kf[:, 0, :], ALU.mult, [xre, kf], [tt])
                        tt4(tt[:, 1, :], xim[:, :], kf[:, 1, :], ALU.mult, [xim, kf], [tt])
                        tt4(Y[:, 0, :], tt[:, 0, :], tt[:, 1, :], ALU.subtract, [tt], [Y], "pool")
                        tt4(tt[:, 0, :], xre[:, :], kf[:, 1, :], ALU.mult, [xre, kf, Y], [tt])
                        tt4(tt[:, 1, :], xim[:, :], kf[:, 0, :], ALU.mult, [xim, kf], [tt])
                        tt4(Y[:, 1, :], tt[:, 0, :], tt[:, 1, :], ALU.add, [tt], [Y], "pool")
                        for pr_ in range(2):
                            pb = self.psb[6 + pr_]
                            p1, p2 = p1s[pr_], p2s[pr_]
                            for jj in range(2):
                                j = pr_ * 2 + jj
                                k.op("pe", lambda e, j=j, jj=jj, pb=pb: e.matmul(pb[:, jj * 256:(jj + 1) * 256], lhsT=Y[:, 0, j * 128:(j + 1) * 128],
                                                                                rhs=R1, start=True, stop=False), reads=[Y, r1], writes=[pb])
                                k.op("pe", lambda e, j=j, jj=jj, pb=pb: e.matmul(pb[:, jj * 256:(jj + 1) * 256], lhsT=Y[:, 1, j * 128:(j + 1) * 128],
                                                                                rhs=R2, start=False, stop=True), reads=[Y, fx], writes=[pb])
                            pbv = pb[:, :].rearrange("p (a n) -> p a n", n=128)
                            k.op("dve", lambda e, pbv=pbv, p1=p1: e.tensor_tensor(
                                out=p1[:, :].rearrange("p (a n) -> p a n", n=128), in0=pbv, in1=TWc.unsqueeze(1).to_broadcast([128, 4, 128]),
                                op=ALU.mult), reads=[pb, dft], writes=[p1])
                            k.op("dve", lambda e, pbv=pbv, p2=p2: e.tensor_tensor(
                                out=p2[:, :].rearrange("p (a n) -> p a n", n=128), in0=pbv, in1=TWs.unsqueeze(1).to_broadcast([128, 4, 128]),
                                op=ALU.mult), reads=[pb, dft], writes=[p2])
                            for jj in range(2):
                                j = pr_ * 2 + jj
                                k.op("pool", lambda e, j=j, jj=jj, p1=p1, p2=p2: e.tensor_tensor(
                                    out=Dd[:, 0, j * 128:(j + 1) * 128], in0=p1[:, jj * 256:jj * 256 + 128],
                                    in1=p2[:, jj * 256 + 128:jj * 256 + 256], op=ALU.add), reads=[p1, p2], writes=[Dd])
                                k.op("pool", lambda e, j=j, jj=jj, p1=p1, p2=p2: e.tensor_tensor(
                                    out=Dd[:, 1, j * 128:(j + 1) * 128], in0=p1[:, jj * 256 + 128:jj * 256 + 256],
                                    in1=p2[:, jj * 256:jj * 256 + 128], op=ALU.subtract), reads=[p1, p2], writes=[Dd])
                        py = self.psb[0]
                        k.op("pe", lambda e: e.matmul(py[0:nr, :], lhsT=Fc[:, 0:nr], rhs=Dd[:, 0, :], start=True, stop=False),
                             reads=[Dd, dft], writes=[py])
                        k.op("pe", lambda e: e.matmul(py[0:nr, :], lhsT=Fs[:, 0:nr], rhs=Dd[:, 1, :], start=False, stop=True),
                             reads=[Dd, dft], writes=[py])
                        k.op("act", lambda e: e.activation(out=yo[0:nr, :], in_=py[0:nr, :], func=AF.Copy, scale=1.0 / 16384.0),
                             reads=[py], writes=[yo])
                        k.dma("pool", self.YH[zrow + c0:zrow + c0 + 4, 0:ncol].rearrange("c (a b) -> a c b", b=128),
                              yo[0:nr, :].rearrange("a (c b) -> a c b", b=128), reads=[yo])
                        gi += 1
            with k.phase():
                mk = lambda nm: [k.sb([128, 2048], F32, nm) for _ in range(2)]
                ys, zs, x0s = mk("hy"), mk("hz"), mk("hx0")
                obs = [k.sb([128, 2048], BF16, "hob") for _ in range(2)]
                bi = 0
                for s_ in sets:
                    zrow = 0 if s_ == 0 else 512
                    blocks = [(b * 2048, 2048) for b in range(4)] if s_ == 0 else [(0, 256)]
                    for cc in range(4):
                        for (t0, n) in blocks:
                            yt, zt, xt, ob = ys[bi % 2], zs[bi % 2], x0s[bi % 2], obs[bi % 2]
                            rows = slice(zrow + cc * 128, zrow + (cc + 1) * 128)
                            k.dma("sp", yt[:, :n], self.YH[rows, t0:t0 + n], writes=[yt])
                            k.dma("sp", zt[:, :n], self.ZT[rows, t0:t0 + n], writes=[zt])
                            k.dma("sp", xt[:, :n], self.X0T[rows, t0:t0 + n], writes=[xt])
                            k.op("dve", lambda e, yt=yt, cc=cc, n=n, s_=s_: e.tensor_scalar(
                                out=yt[:, :n], in0=yt[:, :n], scalar1=rn_all[:, s_, cc:cc + 1], scalar2=None, op0=ALU.mult),
                                reads=[yt, rn_all], writes=[yt])
                            k.op("dve", lambda e, yt=yt, zt=zt, cc=cc, n=n: e.scalar_tensor_tensor(
                                out=yt[:, :n], in0=zt[:, :n], scalar=skipc[:, cc:cc + 1], in1=yt[:, :n], op0=ALU.mult, op1=ALU.add),
                                reads=[yt, zt, skipc], writes=[yt])
                            k.op("pool", lambda e, yt=yt, xt=xt, ob=ob, n=n: e.tensor_tensor(out=ob[:, :n], in0=yt[:, :n], in1=xt[:, :n],
                                                                                           op=ALU.mult), reads=[yt, xt], writes=[ob])
                            tcol = t0 if s_ == 0 else L
                            k.dma("pool", self.YT[512 + cc * 128:512 + (cc + 1) * 128, tcol:tcol + n], ob[:, :n], reads=[ob])
                            bi += 1

    def phase_out(self, layer, with_ctx):
        k = self.k
        odd = layer % 2
        li = layer // 2
        wsrc = (self.w_out_o if odd else self.w_out_e)[li]
        with k.phase():
            WO = k.sb([128, 8, D], BF16, "wo")
            stage = [k.sb([128, 8, 512], F32, "stg") for _ in range(2)]
            self.load_weight(WO, wsrc, D, stage)
            ytbs = [k.sb([128, 8, 512], BF16, "ytb") for _ in range(2)]
            xts = [k.sb([128, D], F32, "xt") for _ in range(2)]
            tmps = [k.sb([128, D], F32, "tmp") for _ in range(2)]
            groups = [(g * 512, 512, 0) for g in range(16)] + ([(L, 256, 1)] if with_ctx else [])
            c = 0
            for gi, (r0, nt, lc) in enumerate(groups):
                ytb = ytbs[gi % 2]
                for kc in range(8):
                    k.dma("sp", ytb[:, kc, :nt], self.YT[kc * 128:(kc + 1) * 128, r0:r0 + nt], writes=[ytb])
                gt = self.G[lc]
                for i in range(nt // 128):
                    xt, tmp = xts[c % 2], tmps[c % 2]
                    rr = r0 + i * 128
                    k.dma("sp", xt[:, :], self.XR[rr:rr + 128, :], writes=[xt])
                    for half in range(2):
                        pp = self.psb[2 * (c % 2) + half]
                        for kc in range(8):
                            k.op("pe", lambda e, kc=kc, pp=pp, half=half: e.matmul(
                                pp[:, :], lhsT=ytb[:, kc, i * 128:(i + 1) * 128], rhs=WO[:, kc, half * 512:(half + 1) * 512],
                                start=(kc == 0), stop=(kc == 7)), reads=[ytb, WO], writes=[pp])
                        k.op("dve", lambda e, pp=pp, half=half: e.tensor_tensor(
                            out=tmp[:, half * 512:(half + 1) * 512], in0=pp[:, :], in1=gt[:, half * 512:(half + 1) * 512], op=ALU.mult),
                            reads=[pp, gt], writes=[tmp])
                    k.op("pool", lambda e: e.tensor_tensor(out=tmp[:, :], in0=tmp[:, :], in1=xt[:, :], op=ALU.add),
                         reads=[tmp, xt], writes=[tmp])
                    k.dma("pool", self.XR[rr:rr + 128, :], tmp[:, :], reads=[tmp])
                    c += 1

    def phase_ffn(self, layer, with_ctx, last):
        k = self.k
        with k.phase():
            WU = k.sb([128, 8, 2 * FH], BF16, "wu")
            WDn = k.sb([128, NCH_F, D], BF16, "wd")
            with k.phase():
                stage = [k.sb([128, 8, 512], F32, "stg") for _ in range(2)]
                self.load_weight(WU, self.w_up[layer], 2 * FH, stage)
            with k.phase():
                stage = [k.sb([128, NCH_F, 128], F32, "stg") for _ in range(2)]
                self.load_weight(WDn, self.w_down[layer], D, stage, bw=128)
            fw = k.sb([128, 3, NCH_F], F32, "fw")
            fb = k.sb([128, NCH_F], F32, "fb")
            for j in range(3):
                k.dma("sp", fw[:, j, :], self.fcw[layer, j:j + 1, :].rearrange("o (c p) -> p (o c)", p=128), writes=[fw],
                      allow_slow_non_contiguous=True)
            k.dma("sp", fb[:], self.fcb[layer:layer + 1, :].rearrange("o (c p) -> p (o c)", p=128), writes=[fb],
                  allow_slow_non_contiguous=True)
            build = self.mk_build(SH2, SC2)
            hts = [k.sb([128, 8, 258], BF16, "ht") for _ in range(2)]
            AT = k.sb([128, NCH_F, 256], BF16, "at")
            accs = [k.sb([128, 256], F32, "acc") for _ in range(2)]
            sgs = [k.sb([128, 256], F32, "sg") for _ in range(2)]
            xr = k.sb([128, D], F32, "xr")
            xo = k.sb([128, D], F32, "xo")
            sgroups = [(s * 256, 0) for s in range(32)] + ([(L, 1)] if with_ctx else [])

            def build_sg(si):
                r0, lc = sgroups[si]
                ht = hts[si % 2]
                prev = hts[(si - 1) % 2]
                for i in range(2):
                    build(ht, 1 + i * 128, r0 + i * 128, lc)
                if si == 0 or lc == 1:
                    k.op("pool", lambda e: e.memset(ht[:, :, 0:1], 0.0), writes=[ht])
                else:
                    k.op("pool", lambda e: e.tensor_copy(out=ht[:, :, 0:1], in_=prev[:, :, 256:257]), reads=[prev], writes=[ht])
                if lc == 1:
                    k.op("pool", lambda e: e.memset(ht[:, :, 257:258], 0.0), writes=[ht])
                if si > 0:
                    if lc == 1:
                        k.op("pool", lambda e: e.memset(prev[:, :, 257:258], 0.0), writes=[prev])
                    else:
                        k.op("pool", lambda e: e.tensor_copy(out=prev[:, :, 257:258], in_=ht[:, :, 1:2]), reads=[ht], writes=[prev])

            def run_sg(si):
                r0, lc = sgroups[si]
                ht = hts[si % 2]
                if si == len(sgroups) - 1 and lc == 0:
                    k.op("pool", lambda e: e.memset(ht[:, :, 257:258], 0.0), writes=[ht])
                for j in range(NCH_F):
                    pg, pv = self.psb[2 + 2 * (j % 2)], self.psb[3 + 2 * (j % 2)]
                    acc, sg = accs[j % 2], sgs[j % 2]
                    for kc in range(8):
                        k.op("pe", lambda e, kc=kc: e.matmul(pg[:, 0:258], lhsT=WU[:, kc, j * 128:(j + 1) * 128], rhs=ht[:, kc, 0:258],
                                                              start=(kc == 0), stop=(kc == 7)), reads=[WU, ht], writes=[pg])
                    for kc in range(8):
                        k.op("pe", lambda e, kc=kc: e.matmul(pv[:, 0:256], lhsT=WU[:, kc, FH + j * 128:FH + (j + 1) * 128],
                                                              rhs=ht[:, kc, 1:257], start=(kc == 0), stop=(kc == 7)),
                             reads=[WU, ht], writes=[pv])
                    k.op("dve", lambda e: e.tensor_scalar(out=acc[:, :], in0=pg[:, 0:256], scalar1=fw[:, 0, j:j + 1], scalar2=fb[:, j:j + 1],
                                                          op0=ALU.mult, op1=ALU.add), reads=[pg, fw, fb], writes=[acc])
                    for t in (1, 2):
                        k.op("dve", lambda e, t=t: e.scalar_tensor_tensor(out=acc[:, :], in0=pg[:, t:t + 256], scalar=fw[:, t, j:j + 1],
                                                                          in1=acc[:, :], op0=ALU.mult, op1=ALU.add),
                             reads=[pg, fw, acc], writes=[acc])
                    k.op("act", lambda e: e.activation(out=sg[:, :], in_=acc[:, :], func=AF.Silu), reads=[acc], writes=[sg])
                    k.op("dve", lambda e: e.tensor_tensor(out=AT[:, j, :], in0=sg[:, :], in1=pv[:, 0:256], op=ALU.mult),
                         reads=[sg, pv], writes=[AT])
                gt = self.G[2 + lc]
                for i in range(2):
                    rr = r0 + i * 128
                    k.dma("sp", xr[:, :], self.XR[rr:rr + 128, :], writes=[xr])
                    for half in range(2):
                        pp = self.psb[6 + half]
                        for j in range(NCH_F):
                            k.op("pe", lambda e, j=j, pp=pp, half=half: e.matmul(
                                pp[:, :], lhsT=AT[:, j, i * 128:(i + 1) * 128], rhs=WDn[:, j, half * 512:(half + 1) * 512],
                                start=(j == 0), stop=(j == NCH_F - 1)), reads=[AT, WDn], writes=[pp])
                        k.op("dve", lambda e, pp=pp, half=half: e.tensor_tensor(
                            out=xo[:, half * 512:(half + 1) * 512], in0=pp[:, :], in1=gt[:, half * 512:(half + 1) * 512], op=ALU.mult),
                            reads=[pp, gt], writes=[xo])
                    k.op("pool", lambda e: e.tensor_tensor(out=xo[:, :], in0=xo[:, :], in1=xr[:, :], op=ALU.add),
                         reads=[xo, xr], writes=[xo])
                    if last and lc == 0:
                        k.dma("pool", self.y[rr:rr + 128, :], xo[:, :], reads=[xo])
                    else:
                        k.dma("pool", self.XR[rr:rr + 128, :], xo[:, :], reads=[xo])

            build_sg(0)
            for si in range(len(sgroups)):
                if si + 1 < len(sgroups):
                    build_sg(si + 1)
                run_sg(si)


    def build(self):
        k = self.k
        stop = getattr(self, "stop", "full")
        with k.phase():
            self.copy_in()
        for layer in getattr(self, "layers", range(self.n_layers)):
            with_ctx = layer < DEPTH - 1
            last = layer == DEPTH - 1
            fin = layer == list(getattr(self, "layers", range(self.n_layers)))[-1]
            self.phase_ada(layer)
            if fin and stop == "ada":
                break
            self.phase_in(layer)
            if fin and stop == "in":
                break
            if layer % 2 == 0:
                self.phase_na(layer, with_ctx)
                self.phase_da(layer, with_ctx)
            else:
                if "nogqa" not in stop:
                    self.phase_gqa(layer, with_ctx)
                if "nohy" not in stop:
                    self.phase_hyena(layer, with_ctx)
            if fin and stop.startswith("attn"):
                break
            self.phase_out(layer, with_ctx)
            if fin and stop == "out":
                break
            self.phase_ffn(layer, with_ctx, last)
            if not fin:
                k.fresh()
        k.barrier()
        return self.nc


_CACHE = {}


def _host_inputs(inputs):
    f = lambda a: np.ascontiguousarray(np.asarray(a, dtype=np.float32))
    inp = {n: f(v) for n, v in inputs.items()}
    rc, rs = _rope_tables()
    zT, tlin, ndel, fc, fs, tc_, ts_ = _hy_tables()
    shared = {n: inp[n] for n in ("w_ada", "b_ada", "w_up", "ffn_conv_w", "ffn_conv_b", "w_down", "w_in_e", "w_out_e",
                                  "w_in_o", "w_out_o", "na_q_gain", "na_k_gain", "da_q_gain", "da_k_gain", "da_lambda_q1",
                                  "da_lambda_k1", "da_lambda_q2", "da_lambda_k2", "gqa_q_gain", "gqa_k_gain", "hy_b1", "hy_b2",
                                  "hy_b3", "hy_freq", "da_subln_gain", "hy_conv_w", "hy_conv_b", "hy_w1", "hy_w2", "hy_w3",
                                  "hy_w4", "hy_skip")}
    shared["rope_cos"] = rc
    shared["rope_sin"] = rs
    shared["na_tab"] = np.stack([_na_tables(inp["na_rpb"][i]) for i in range(2)]).reshape(2, 4, 35, 128, 256)
    shared["hy_zT"] = zT
    shared["hy_tlin"] = tlin
    shared["hy_ndel"] = ndel
    shared["dft"] = np.stack([fc, fs, tc_, ts_])
    shared["consts"] = _consts()
    maps = []
    for core in range(8):
        b = core % 4
        m = dict(shared)
        m["x"] = inp["x"][b]
        m["ctx"] = inp["ctx"][b]
        m["cc"] = np.ascontiguousarray(np.stack([inp["c"][b], inp["c_ctx"]]))
        maps.append(m)
    return maps


def _layer_maps(maps, layer, xs, cs):
    out = []
    for core, m in enumerate(maps):
        d = {}
        for n, v in m.items():
            if n in _PER4:
                d[n] = np.ascontiguousarray(v[layer:layer + 1])
            elif n in _PER2:
                d[n] = np.ascontiguousarray(v[layer // 2:layer // 2 + 1])
            else:
                d[n] = v
        if xs is not None:
            d["x"] = xs[core]
            d["ctx"] = cs[core]
        out.append(d)
    return out


def kernel(**inputs):
    maps = _host_inputs(inputs)
    if "fused" not in _CACHE:
        _CACHE["fused"] = Prog().build()
    res = run_bass_kernel_spmd(_CACHE["fused"], maps, core_ids=list(range(8)))
    return np.stack([np.asarray(res.results[b]["y"], dtype=np.float32) for b in range(4)])
```

```python
import contextlib
import math
import numpy as np
import concourse.bass as bass
import concourse.mybir as mybir
from concourse.bass_utils import run_bass_kernel_spmd

F32 = mybir.dt.float32
BF16 = mybir.dt.bfloat16
AF = mybir.ActivationFunctionType
ALU = mybir.AluOpType

L = 8192
C = 256
LT = L + C
D = 1024
DEPTH = 4
EPS = 1e-6
FH = 2816
NCH_F = 22
EP = 16000
KSLOT = 8
EPD = 200


class Tile:
    __slots__ = ("t", "w", "r", "name")

    def __init__(self, t, name=""):
        self.t = t
        self.w = None
        self.r = {}
        self.name = name

    def __getitem__(self, k):
        return self.t[k]


class LTile(Tile):
    __slots__ = ("base",)

    def __init__(self, t, name="", base=0):
        Tile.__init__(self, t, name)
        self.base = base

    def __getitem__(self, k):
        b = self.base
        if not isinstance(k, tuple):
            k = (k,)
        f = k[0]
        if isinstance(f, slice):
            f = slice(f.start - b, f.stop - b)
        else:
            f = f - b
        k = (f,) + tuple(k[1:])
        return self.t[k if len(k) > 1 else k[0]]


class K:
    def __init__(self, nc):
        self.nc = nc
        self.es = contextlib.ExitStack()
        self.eng = {"pe": nc.tensor, "act": nc.scalar, "dve": nc.vector, "pool": nc.gpsimd, "sp": nc.sync}
        self.cnt = {e: 0 for e in self.eng}
        self.csem = {e: [] for e in self.eng}
        self.seen = {e: {} for e in self.eng}
        self.dcnt = {}
        self.dsem = {}
        self.uid = 0
        self.phase_stack = None
        self.gen = 0

    def _name(self, p):
        self.uid += 1
        return "%s_%d" % (p, self.uid)

    def sb(self, shape, dt, name="t"):
        n = self._name(name)
        return Tile(self.phase_stack.enter_context(self.nc.sbuf_tensor(n, list(shape), dt)), n)

    def sb_global(self, shape, dt, name="g"):
        n = self._name(name)
        return Tile(self.es.enter_context(self.nc.sbuf_tensor(n, list(shape), dt)), n)

    def ps(self, name="ps"):
        n = self._name(name)
        return Tile(self.es.enter_context(self.nc.psum_tensor(n, [128, 512], F32)), n)

    def dram(self, name, shape, dt, kind="Internal"):
        return Tile(self.nc.dram_tensor(name, list(shape), dt, kind=kind).ap(), name)

    def reg(self, name="r"):
        return Tile(None, name)

    def _csem(self, e, idx):
        ep = (idx - 1) // EP
        lst = self.csem[e]
        while len(lst) <= ep:
            lst.append(self.es.enter_context(self.nc.semaphore(self._name("s" + e))))
        return lst[ep], (idx - 1) % EP + 1

    def _dsem(self, q, i):
        k = i % KSLOT
        u = i // KSLOT
        ep = u // EPD
        d = self.dsem.setdefault(q, {})
        key = (k, ep)
        if key not in d:
            d[key] = self.es.enter_context(self.nc.semaphore(self._name("d" + q)))
        return d[key], 16 * (u % EPD + 1)

    def _waits(self, e, deps):
        out = {}
        seen = self.seen[e]
        for tok in deps:
            if tok is None or tok[-1] != self.gen:
                continue
            if tok[0] == "c":
                _, de, idx, _g = tok
                if de == e and e in ("pe", "sp"):
                    continue
                if de == e:
                    if idx >= self.cnt[e] - 0 and False:
                        pass
                key = ("c", de)
                if seen.get(key, 0) >= idx:
                    continue
                if out.get(key, 0) < idx:
                    out[key] = idx
            else:
                _, q, i, _g = tok
                key = ("d", q, i % KSLOT)
                if seen.get(key, -1) >= i:
                    continue
                if out.get(key, -1) < i:
                    out[key] = i
        h = self.eng[e]
        for key, v in out.items():
            seen[key] = v
            if key[0] == "c":
                s, val = self._csem(key[1], v)
            else:
                s, val = self._dsem(key[1], v)
            h.wait_ge(s, val)

    def _deps(self, reads, writes):
        deps = []
        for t in reads:
            deps.append(t.w)
        for t in writes:
            deps.append(t.w)
            deps.extend(t.r.values())
        return deps

    def _mark(self, tok, reads, writes):
        for t in reads:
            if tok[0] == "c":
                t.r[("c", tok[1])] = tok
            else:
                t.r[("d", tok[1], tok[2] % KSLOT)] = tok
        for t in writes:
            t.w = tok
            t.r = {}

    def op(self, e, fn, reads=(), writes=()):
        self._waits(e, self._deps(reads, writes))
        ins = fn(self.eng[e])
        self.cnt[e] += 1
        idx = self.cnt[e]
        s, val = self._csem(e, idx)
        ins.then_inc(s, 1)
        self._mark(("c", e, idx, self.gen), reads, writes)

    def dma(self, q, out, in_, reads=(), writes=(), **kw):
        i = self.dcnt.get(q, 0)
        deps = self._deps(reads, writes)
        if i >= KSLOT:
            deps.append(("d", q, i - KSLOT, self.gen))
        self._waits(q, deps)
        s, val = self._dsem(q, i)
        self.eng[q].dma_start(out=out, in_=in_, **kw).then_inc(s, 16)
        self.dcnt[q] = i + 1
        self._mark(("d", q, i, self.gen), reads, writes)

    def barrier(self):
        toks = []
        for e in ("pe", "act", "dve", "pool"):
            if self.cnt[e]:
                toks.append(("c", e, self.cnt[e], self.gen))
        for q, n in self.dcnt.items():
            for i in range(max(0, n - KSLOT), n):
                toks.append(("d", q, i, self.gen))
        for e in self.eng:
            self._waits(e, [t for t in toks if not (t[0] == "c" and t[1] == e)])

    def fresh(self):
        self.barrier()
        self.gen += 1
        self.cnt = {e: 0 for e in self.eng}
        self.csem = {e: [] for e in self.eng}
        self.seen = {e: {} for e in self.eng}
        self.dcnt = {}
        self.dsem = {}

    @contextlib.contextmanager
    def phase(self):
        old = self.phase_stack
        self.phase_stack = contextlib.ExitStack()
        try:
            yield
        finally:
            self.barrier()
            self.phase_stack.close()
            self.phase_stack = old


def _rope_tables():
    t = np.arange(L, dtype=np.int32)
    row = (t // 64).astype(np.float32)
    col = (t % 64).astype(np.float32)
    inv = (np.float32(10000.0) ** (-np.arange(16, dtype=np.float32) / np.float32(16))).astype(np.float32)
    ang = np.concatenate([row[:, None] * inv, col[:, None] * inv], axis=-1).astype(np.float32)
    cos = np.cos(ang).astype(np.float32)
    sin = np.sin(ang).astype(np.float32)
    p = np.arange(128)
    i = (p % 64) // 2
    return np.ascontiguousarray(cos[:, i].T), np.ascontiguousarray(sin[:, i].T)


def _na_cls(qt):
    return 0 if qt == 0 else 1 if qt == 1 else 3 if qt == 62 else 4 if qt == 63 else 2


def _na_kt0(qt):
    return min(max(qt - 2, 0), 59)


def _na_tables(rpb):
    out = np.full((4, 5, 7, 128, 2, 128), -30000.0, np.float32)
    out[:, :, 5:7] = 0.0
    reps = {0: 0, 1: 1, 2: 2, 3: 62, 4: 63}
    p = np.arange(128)
    for cls, qt in reps.items():
        kt0 = _na_kt0(qt)
        r = 2 * qt + p // 64
        c = p % 64
        r0 = np.clip(r - 4, 0, 120)
        c0 = np.clip(c - 8, 0, 48)
        for rel in range(5):
            kt = kt0 + rel
            kr = 2 * kt + p // 64
            kc = p % 64
            valid = ((kr[:, None] >= r0[None, :]) & (kr[:, None] < r0[None, :] + 8)
                     & (kc[:, None] >= c0[None, :]) & (kc[:, None] < c0[None, :] + 16))
            ro = np.clip(kr[:, None] - r[None, :] + 7, 0, 14)
            co = np.clip(kc[:, None] - c[None, :] + 15, 0, 30)
            for h in range(8):
                g = rpb[h][ro, co]
                out[h // 2, cls, rel, :, h % 2, :] = np.where(valid, g, np.float32(-30000.0))
    return out


def _hy_tables():
    def zemb(n):
        t = np.linspace(0.0, 1.0, n, dtype=np.float32)[:, None]
        w = (np.float32(2.0 * math.pi) * np.arange(n, dtype=np.float32)[:, None] / np.float32(n)).astype(np.float32)
        f = np.linspace(1e-4, 15, 16, dtype=np.float32)[None, :]
        z = np.concatenate([t, np.cos(f * w), -np.sin(f * w)], axis=-1).astype(np.float32)
        return z, t[:, 0]
    zl, tl = zemb(L)
    zc, tc = zemb(C)
    zT = np.zeros((2, 33, L), np.float32)
    zT[0] = zl.T
    zT[1, :, :C] = zc.T
    tlin = np.full((2, L), 1.0e4, np.float32)
    tlin[0] = tl
    tlin[1, :C] = tc
    max_decay = math.log(1e-2) / 0.3
    min_decay = math.log(1e-2) / 1.5
    deltas = np.abs(np.linspace(min_decay, max_decay, 512, dtype=np.float32))
    ndel = np.ascontiguousarray((-deltas).reshape(4, 128).T)
    n = np.arange(128, dtype=np.float64)
    a = 2.0 * math.pi * np.outer(n, n) / 128.0
    fc = np.cos(a).astype(np.float32)
    fs = (-np.sin(a)).astype(np.float32)
    a2 = 2.0 * math.pi * np.outer(n, n) / 16384.0
    tc_ = np.cos(a2).astype(np.float32)
    ts_ = (-np.sin(a2)).astype(np.float32)
    return zT, tlin, ndel, fc, fs, tc_, ts_


def _consts():
    ident = np.eye(128, dtype=np.float32)
    bd = np.zeros((128, 128), np.float32)
    bd[:64, :64] = 1.0 / 64
    bd[64:, 64:] = 1.0 / 64
    o128 = np.full((128, 128), 1.0 / 128, np.float32)
    rot = np.zeros((128, 128), np.float32)
    for i in range(64):
        rot[2 * i + 1, 2 * i] = -1.0
        rot[2 * i, 2 * i + 1] = 1.0
    return np.stack([ident, bd, o128, rot, np.ones((128, 128), np.float32)])


SH1, SC1, SH2, SC2 = 0, 1, 2, 3
_PER4 = ("w_ada", "b_ada", "w_up", "ffn_conv_w", "ffn_conv_b", "w_down")
_PER2E = ("w_in_e", "w_out_e", "na_q_gain", "na_k_gain", "da_q_gain", "da_k_gain", "da_lambda_q1", "da_lambda_k1",
          "da_lambda_q2", "da_lambda_k2", "da_subln_gain", "na_tab")
_PER2O = ("w_in_o", "w_out_o", "gqa_q_gain", "gqa_k_gain", "hy_conv_w", "hy_conv_b", "hy_w1", "hy_b1", "hy_w2", "hy_b2",
          "hy_w3", "hy_b3", "hy_w4", "hy_freq", "hy_skip")
_PER2 = _PER2E + _PER2O


class Prog:
    def __init__(self, n_layers=DEPTH, debug=False, single=None):
        self.debug = debug
        self.n_layers = n_layers
        self.single = single
        if single is not None:
            self.layers = [single]
        nc = bass.Bass("TRN2", target_bir_lowering=False)
        self.nc = nc
        k = K(nc)
        self.k = k
        def I(name, shape, dt=F32):
            if single is not None and name in _PER4:
                return LTile(nc.dram_tensor(name, [1] + list(shape[1:]), dt, kind="ExternalInput").ap(), name, single)
            if single is not None and name in _PER2:
                return LTile(nc.dram_tensor(name, [1] + list(shape[1:]), dt, kind="ExternalInput").ap(), name, single // 2)
            return k.dram(name, shape, dt, kind="ExternalInput")
        self.x = I("x", [L, D])
        self.ctx = I("ctx", [C, D])
        self.cc = I("cc", [2, D])
        self.w_ada = I("w_ada", [4, D, 6 * D])
        self.b_ada = I("b_ada", [4, 6 * D])
        self.w_up = I("w_up", [4, D, 2 * FH])
        self.fcw = I("ffn_conv_w", [4, 3, FH])
        self.fcb = I("ffn_conv_b", [4, FH])
        self.w_down = I("w_down", [4, FH, D])
        self.w_in_e = I("w_in_e", [2, D, 3072])
        self.w_out_e = I("w_out_e", [2, D, D])
        self.w_in_o = I("w_in_o", [2, D, 2304])
        self.w_out_o = I("w_out_o", [2, D, D])
        self.vec64 = {}
        for n in ("na_q_gain", "na_k_gain", "da_q_gain", "da_k_gain", "da_lambda_q1", "da_lambda_k1",
                  "da_lambda_q2", "da_lambda_k2", "gqa_q_gain", "gqa_k_gain", "hy_b1", "hy_b2", "hy_b3", "hy_freq"):
            self.vec64[n] = I(n, [2, 64])
        self.subln = I("da_subln_gain", [2, 128])
        self.hy_conv_w = I("hy_conv_w", [2, 3, 1536])
        self.hy_conv_b = I("hy_conv_b", [2, 1536])
        self.hy_w1 = I("hy_w1", [2, 33, 64])
        self.hy_w2 = I("hy_w2", [2, 64, 64])
        self.hy_w3 = I("hy_w3", [2, 64, 64])
        self.hy_w4 = I("hy_w4", [2, 64, 1024])
        self.hy_skip = I("hy_skip", [2, 512])
        self.rope_cos = I("rope_cos", [128, L])
        self.rope_sin = I("rope_sin", [128, L])
        self.na_tab = I("na_tab", [2, 4, 35, 128, 256])
        self.hy_zT = I("hy_zT", [2, 33, L])
        self.hy_tlin = I("hy_tlin", [2, L])
        self.hy_ndel = I("hy_ndel", [128, 4])
        self.dft = I("dft", [4, 128, 128])
        self.consts_d = I("consts", [5, 128, 128])
        self.y = k.dram("y", [L, D], F32, kind="ExternalOutput")
        sk = "ExternalOutput" if debug else "Internal"
        self.XR = k.dram("XR", [LT, D], F32, kind="ExternalOutput" if (debug or single is not None) else "Internal")
        self.QT = k.dram("QT", [1024, LT], BF16, kind="Internal")
        self.KT = k.dram("KT", [1024, LT], BF16, kind="Internal")
        self.V = k.dram("V", [LT, 1024], BF16, kind="Internal")
        self.YT = k.dram("YT", [1024, LT], BF16, kind=sk)
        self.X0T = k.dram("X0T", [1024, L], F32, kind="Internal")
        self.ZT = k.dram("ZT", [1024, L], F32, kind="Internal")
        self.HF = k.dram("HF", [2, 1024, L], F32, kind="Internal")
        self.YH = k.dram("YH", [1024, L], F32, kind="Internal")
        self.r_xr = [k.reg("xr%d" % i) for i in range(LT // 128)]
        self.r_q = k.reg("q")
        self.r_k = k.reg("k")
        self.r_v = k.reg("v")
        self.r_y = k.reg("y")
        self.r_in = k.reg("in")
        self.psb = [k.ps() for _ in range(8)]
        self.cst = k.sb_global([128, 5, 128], F32, "cst")
        k.dma("sp", self.cst[:], self.consts_d[:].rearrange("a p n -> p a n"), writes=[self.cst])
        self.ident = self.cst[:, 0, :]
        self.bd64 = self.cst[:, 1, :]
        self.o128 = self.cst[:, 2, :]
        self.rot = self.cst[:, 3, :]
        self.ones32 = self.cst[:, 4, :]
        self.epsc = k.sb_global([128, 1], F32, "epsc")
        k.op("dve", lambda e: e.memset(self.epsc[:], EPS), writes=[self.epsc])
        self.modT = k.sb_global([128, 4, 8, 2], F32, "modT")
        self.G = [k.sb_global([128, D], F32, "gate") for _ in range(4)]

    def xr_rows(self, r0, n):
        return self.XR[r0:r0 + n, :]

    def copy_in(self):
        k = self.k
        for i in range(0, L, 1024):
            k.dma("sp", self.XR[i:i + 1024, :], self.x[i:i + 1024, :],
                  writes=self.r_xr[i // 128:(i + 1024) // 128])
        k.dma("sp", self.XR[L:LT, :], self.ctx[:, :], writes=self.r_xr[64:66])

    def phase_ada(self, layer):
        k = self.k
        with k.phase():
            sT = k.sb([128, 2, 8], F32, "sT")
            for j in range(2):
                k.dma("sp", sT[:, j, :], self.cc[j:j + 1, :].rearrange("o (kc p) -> p (o kc)", p=128), writes=[sT],
                      allow_slow_non_contiguous=True)
            k.op("act", lambda e: e.activation(out=sT[:], in_=sT[:], func=AF.Silu), reads=[sT], writes=[sT])
            srep = [k.sb([128, 8, 128], F32, "srep") for _ in range(2)]
            for j in range(2):
                for kc in range(8):
                    k.op("dve", lambda e, j=j, kc=kc: e.tensor_copy(out=srep[j][:, kc, :],
                                                                     in_=sT[:, j, kc:kc + 1].to_broadcast([128, 128])),
                         reads=[sT], writes=[srep[j]])
            brow = k.sb([1, 6 * D], F32, "brow")
            k.dma("sp", brow[:], self.b_ada[layer:layer + 1, :], writes=[brow])
            one1 = k.sb([1, 128], F32, "one1")
            k.op("dve", lambda e: e.memset(one1[:], 1.0), writes=[one1])
            wb = [k.sb([128, 8, 512], F32, "wada") for _ in range(2)]
            pcol = self.psb[0]
            roles = [("col", SH1, 0), ("col", SH1, 4), ("col", SC1, 0), ("col", SC1, 4),
                     ("g", 0, 0), ("g", 0, 512), ("col", SH2, 0), ("col", SH2, 4),
                     ("col", SC2, 0), ("col", SC2, 4), ("g", 2, 0), ("g", 2, 512)]
            for blk in range(12):
                w = wb[blk % 2]
                n0 = blk * 512
                k.dma("sp", w[:], self.w_ada[layer, :, n0:n0 + 512].rearrange("(kc p) n -> p kc n", p=128),
                      writes=[w])
                kind, a, b = roles[blk]
                if kind == "col":
                    for c4 in range(4):
                        ch = b + c4
                        col = (a * 8 + ch) * 2
                        for kc in range(8):
                            k.op("pe", lambda e, kc=kc, c4=c4, col=col, w=w: e.matmul(
                                pcol[:, col:col + 2], lhsT=w[:, kc, c4 * 128:(c4 + 1) * 128], rhs=sT[:, :, kc],
                                start=(kc == 0), stop=False), reads=[w, sT], writes=[pcol])
                        k.op("pe", lambda e, c4=c4, col=col, n0=n0: e.matmul(
                            pcol[:, col:col + 2], lhsT=brow[0:1, n0 + c4 * 128:n0 + (c4 + 1) * 128], rhs=one1[0:1, 0:2],
                            start=False, stop=True), reads=[brow, one1], writes=[pcol])
                else:
                    for j in range(2):
                        pg = self.psb[1 + j]
                        for kc in range(8):
                            k.op("pe", lambda e, kc=kc, j=j, pg=pg, w=w: e.matmul(
                                pg[:, :], lhsT=srep[j][:, kc, :], rhs=w[:, kc, :], start=(kc == 0), stop=False),
                                reads=[w, srep[j]], writes=[pg])
                        k.op("pe", lambda e, pg=pg, n0=n0: e.matmul(
                            pg[:, :], lhsT=one1[0:1, :], rhs=brow[0:1, n0:n0 + 512], start=False, stop=True),
                            reads=[brow, one1], writes=[pg])
                        gt = self.G[a + j]
                        k.op("act", lambda e, pg=pg, gt=gt, b=b: e.activation(out=gt[:, b:b + 512], in_=pg[:, :], func=AF.Copy),
                             reads=[pg], writes=[gt])
            mflat = self.modT[:].rearrange("p a c j -> p (a c j)")
            k.op("dve", lambda e: e.tensor_copy(out=mflat, in_=pcol[:, 0:64]), reads=[pcol], writes=[self.modT])
            for a in (SC1, SC2):
                v = self.modT[:, a, :, :]
                k.op("dve", lambda e, v=v: e.tensor_scalar(out=v, in0=v, scalar1=1.0, scalar2=None, op0=ALU.add),
                     reads=[self.modT], writes=[self.modT])

    def load_weight(self, dst, src2d, N, stage, bw=512, engs=("dve", "pool")):
        k = self.k
        for bi, n0 in enumerate(range(0, N, bw)):
            n = min(bw, N - n0)
            st = stage[bi % len(stage)]
            k.dma("sp", st[:, :, :n], src2d[:, n0:n0 + n].rearrange("(kc p) n -> p kc n", p=128), writes=[st])
            k.op(engs[bi % len(engs)], lambda e, st=st, n0=n0, n=n: e.tensor_copy(out=dst[:, :, n0:n0 + n], in_=st[:, :, :n]),
                 reads=[st], writes=[dst])

    def col64x2(self, dst, j, src_row):
        k = self.k
        for h in range(2):
            k.dma("sp", dst[h * 64:(h + 1) * 64, j:j + 1], src_row.rearrange("o d -> d o"), writes=[dst],
                  allow_slow_non_contiguous=True)

    def mk_build(self, a_sh, a_sc):
        k = self.k
        xts = [k.sb([128, D], F32, "xt") for _ in range(2)]
        xns = [k.sb([128, D], F32, "xn") for _ in range(2)]
        junk = k.sb([128, D], BF16, "junk")
        ssqs = [k.sb([128, 1], F32, "ssq") for _ in range(2)]
        rstds = [k.sb([128, 1], F32, "rstd") for _ in range(2)]
        state = {"i": 0}

        def build(ht, col0, r0, lc, n=128):
            i = state["i"]
            state["i"] += 1
            xt, xn, ssq, rstd = xts[i % 2], xns[i % 2], ssqs[i % 2], rstds[i % 2]
            k.dma("sp", xt[:n, :], self.XR[r0:r0 + n, :], writes=[xt])
            k.op("act", lambda e: e.activation(out=junk[:n, :], in_=xt[:n, :], func=AF.Square, accum_out=ssq[:n, 0:1]),
                 reads=[xt], writes=[junk, ssq])
            k.op("dve", lambda e: e.tensor_scalar(out=rstd[:n, :], in0=ssq[:n, :], scalar1=1.0 / D, scalar2=EPS,
                                                  op0=ALU.mult, op1=ALU.add), reads=[ssq], writes=[rstd])
            k.op("act", lambda e: e.activation(out=rstd[:n, :], in_=rstd[:n, :], func=AF.Sqrt), reads=[rstd], writes=[rstd])
            k.op("dve", lambda e: e.reciprocal(out=rstd[:n, :], in_=rstd[:n, :]), reads=[rstd], writes=[rstd])
            k.op("dve", lambda e: e.tensor_scalar(out=xn[:n, :], in0=xt[:n, :], scalar1=rstd[:n, 0:1], scalar2=None, op0=ALU.mult),
                 reads=[xt, rstd], writes=[xn])
            for half in range(2):
                pt = self.psb[half]
                for j in range(4):
                    kc = half * 4 + j
                    k.op("pe", lambda e, j=j, kc=kc, pt=pt: e.transpose(out=pt[:, j * 128:j * 128 + n], in_=xn[:n, kc * 128:(kc + 1) * 128],
                                                                        identity=self.ident[:n, :n]), reads=[xn, self.cst], writes=[pt])
                for j in range(4):
                    kc = half * 4 + j
                    k.op("act", lambda e, j=j, kc=kc, pt=pt: e.activation(
                        out=ht[:, kc, col0:col0 + n], in_=pt[:, j * 128:j * 128 + n], func=AF.Identity,
                        scale=self.modT[:, a_sc, kc, lc:lc + 1], bias=self.modT[:, a_sh, kc, lc:lc + 1]),
                        reads=[pt, self.modT], writes=[ht])
        return build

    def phase_in(self, layer):
        k = self.k
        odd = layer % 2
        li = layer // 2
        NIN = 2304 if odd else 3072
        wsrc = (self.w_in_o if odd else self.w_in_e)[li]
        with k.phase():
            W = k.sb([128, 8, NIN], BF16, "win")
            stage = [k.sb([128, 8, 512], F32, "stg") for _ in range(2)]
            self.load_weight(W, wsrc, NIN, stage)
            gcol = k.sb([128, 4], F32, "gcol")
            names = ["gqa_q_gain", "gqa_k_gain"] if odd else ["na_q_gain", "na_k_gain", "da_q_gain", "da_k_gain"]
            for j, nme in enumerate(names):
                self.col64x2(gcol, j, self.vec64[nme][li:li + 1, :])
            if odd:
                fm = [(c * 128, 0, True, self.QT, c * 128) for c in range(4)] + [(512, 1, True, self.KT, 0)]
                vblocks = [(640, 128, 0)]
                cw = k.sb([128, 3, 12], F32, "hcw")
                cb = k.sb([128, 12], F32, "hcb")
                for j in range(3):
                    k.dma("sp", cw[:, j, :], self.hy_conv_w[li, j:j + 1, :].rearrange("o (c p) -> p (o c)", p=128), writes=[cw],
                          allow_slow_non_contiguous=True)
                k.dma("sp", cb[:], self.hy_conv_b[li:li + 1, :].rearrange("o (c p) -> p (o c)", p=128), writes=[cb],
                      allow_slow_non_contiguous=True)
            else:
                fm = ([(c * 128, 0, False, self.QT, c * 128) for c in range(4)]
                      + [(512 + c * 128, 1, False, self.KT, c * 128) for c in range(4)]
                      + [(1536 + c * 128, 2, True, self.QT, 512 + c * 128) for c in range(4)]
                      + [(2048 + c * 128, 3, True, self.KT, 512 + c * 128) for c in range(4)])
                vblocks = [(1024, 512, 0), (2560, 512, 512)]
            build = self.mk_build(SH1, SC1)
            hts = [k.sb([128, 8, 514], BF16, "ht") for _ in range(2)]
            css = [k.sb([128, 2, 512], F32, "cs") for _ in range(2)]
            mk = lambda shape, dt, nm, n=2: [k.sb(shape, dt, nm) for _ in range(n)]
            sqs, rss, qns, t1s, t2s = (mk([128, 512], F32, x) for x in ("sq", "rs", "qn", "t1", "t2"))
            obs = mk([128, 512], BF16, "ob", 3)
            vts = mk([128, 512], BF16, "vt", 3)
            if odd:
                us = mk([128, 514], F32, "u", 3)
                accs = mk([128, 512], F32, "acc", 3)
            groups = [(g * 512, 512, 0) for g in range(16)] + [(L, 256, 1)]
            ctr = {"c": 0}

            def build_group(gi):
                r0, nt, lc = groups[gi]
                ht = hts[gi % 2]
                for i in range(nt // 128):
                    build(ht, 1 + i * 128, r0 + i * 128, lc)
                if odd:
                    prev = hts[(gi - 1) % 2]
                    if gi == 0 or lc == 1:
                        k.op("pool", lambda e: e.memset(ht[:, :, 0:1], 0.0), writes=[ht])
                    else:
                        k.op("pool", lambda e: e.tensor_copy(out=ht[:, :, 0:1], in_=prev[:, :, 512:513]), reads=[prev], writes=[ht])
                    if lc == 1:
                        k.op("pool", lambda e: e.memset(ht[:, :, nt + 1:nt + 2], 0.0), writes=[ht])
                        k.op("pool", lambda e: e.memset(prev[:, :, 513:514], 0.0), writes=[prev])
                    elif gi > 0:
                        k.op("pool", lambda e: e.tensor_copy(out=prev[:, :, 513:514], in_=ht[:, :, 1:2]), reads=[ht], writes=[prev])

            def project(gi):
                r0, nt, lc = groups[gi]
                ht = hts[gi % 2]
                cs = css[gi % 2]
                if lc == 0:
                    k.dma("sp", cs[:, 0, :], self.rope_cos[:, r0:r0 + nt], writes=[cs])
                    k.dma("sp", cs[:, 1, :], self.rope_sin[:, r0:r0 + nt], writes=[cs])
                for (wc, gk, rope, dst, drow) in fm:
                    c = ctr["c"]
                    ctr["c"] += 1
                    pp, pm, pr = self.psb[2 + c % 2], self.psb[4 + c % 2], self.psb[6 + c % 2]
                    sq, rs, qn, t1, t2, ob = sqs[c % 2], rss[c % 2], qns[c % 2], t1s[c % 2], t2s[c % 2], obs[c % 3]
                    for kc in range(8):
                        k.op("pe", lambda e, kc=kc: e.matmul(pp[:, :nt], lhsT=W[:, kc, wc:wc + 128], rhs=ht[:, kc, 1:1 + nt],
                                                              start=(kc == 0), stop=(kc == 7)), reads=[W, ht], writes=[pp])
                    k.op("act", lambda e: e.activation(out=sq[:, :nt], in_=pp[:, :nt], func=AF.Square), reads=[pp], writes=[sq])
                    k.op("pe", lambda e: e.matmul(pm[:, :nt], lhsT=self.bd64, rhs=sq[:, :nt], start=True, stop=True),
                         reads=[sq, self.cst], writes=[pm])
                    k.op("act", lambda e: e.activation(out=rs[:, :nt], in_=pm[:, :nt], func=AF.Sqrt, bias=self.epsc[:, 0:1]),
                         reads=[pm, self.epsc], writes=[rs])
                    k.op("dve", lambda e: e.reciprocal(out=rs[:, :nt], in_=rs[:, :nt]), reads=[rs], writes=[rs])
                    do_rope = rope and lc == 0
                    qo = qn if do_rope else ob
                    k.op("dve", lambda e: e.scalar_tensor_tensor(out=qo[:, :nt], in0=pp[:, :nt], scalar=gcol[:, gk:gk + 1],
                                                                 in1=rs[:, :nt], op0=ALU.mult, op1=ALU.mult),
                         reads=[pp, rs, gcol], writes=[qo])
                    if do_rope:
                        k.op("pe", lambda e: e.matmul(pr[:, :nt], lhsT=self.rot, rhs=qn[:, :nt], start=True, stop=True),
                             reads=[qn, self.cst], writes=[pr])
                        k.op("pool", lambda e: e.tensor_tensor(out=t1[:, :nt], in0=qn[:, :nt], in1=cs[:, 0, :nt], op=ALU.mult),
                             reads=[qn, cs], writes=[t1])
                        k.op("dve", lambda e: e.tensor_tensor(out=t2[:, :nt], in0=pr[:, :nt], in1=cs[:, 1, :nt], op=ALU.mult),
                             reads=[pr, cs], writes=[t2])
                        k.op("pool", lambda e: e.tensor_tensor(out=ob[:, :nt], in0=t1[:, :nt], in1=t2[:, :nt], op=ALU.add),
                             reads=[t1, t2], writes=[ob])
                    k.dma("pool", dst[drow:drow + 128, r0:r0 + nt], ob[:, :nt], reads=[ob])
                for (wc, wn, vcol) in vblocks:
                    for i in range(nt // 128):
                        c = ctr["c"]
                        ctr["c"] += 1
                        pp = self.psb[2 + c % 2]
                        vt = vts[c % 3]
                        for kc in range(8):
                            k.op("pe", lambda e, kc=kc: e.matmul(pp[:, :wn], lhsT=ht[:, kc, 1 + i * 128:1 + (i + 1) * 128],
                                                                  rhs=W[:, kc, wc:wc + wn], start=(kc == 0), stop=(kc == 7)),
                                 reads=[W, ht], writes=[pp])
                        k.op("act", lambda e: e.activation(out=vt[:, :wn], in_=pp[:, :wn], func=AF.Copy), reads=[pp], writes=[vt])
                        k.dma("pool", self.V[r0 + i * 128:r0 + (i + 1) * 128, vcol:vcol + wn], vt[:, :wn], reads=[vt])
                if odd:
                    hw = (nt + 2) // 2
                    zrow = 0 if lc == 0 else 512
                    tcol = r0 if lc == 0 else 0
                    for c4 in range(4):
                        cv = []
                        for part in range(3):
                            ch = part * 4 + c4
                            wc = 768 + ch * 128
                            c = ctr["c"]
                            ctr["c"] += 1
                            pa, pb = self.psb[2 + 2 * (c % 3)], self.psb[3 + 2 * (c % 3)]
                            u, acc = us[c % 3], accs[c % 3]
                            for hlf, pz in ((0, pa), (1, pb)):
                                for kc in range(8):
                                    k.op("pe", lambda e, kc=kc, hlf=hlf, pz=pz: e.matmul(
                                        pz[:, :hw], lhsT=W[:, kc, wc:wc + 128], rhs=ht[:, kc, hlf * hw:(hlf + 1) * hw],
                                        start=(kc == 0), stop=(kc == 7)), reads=[W, ht], writes=[pz])
                                k.op("act", lambda e, hlf=hlf, pz=pz: e.activation(out=u[:, hlf * hw:(hlf + 1) * hw], in_=pz[:, :hw],
                                                                                   func=AF.Copy), reads=[pz], writes=[u])
                            k.op("dve", lambda e: e.tensor_scalar(out=acc[:, :nt], in0=u[:, 0:nt], scalar1=cw[:, 0, ch:ch + 1],
                                                                  scalar2=cb[:, ch:ch + 1], op0=ALU.mult, op1=ALU.add),
                                 reads=[u, cw, cb], writes=[acc])
                            for j in (1, 2):
                                k.op("dve", lambda e, j=j: e.scalar_tensor_tensor(out=acc[:, :nt], in0=u[:, j:j + nt],
                                                                                  scalar=cw[:, j, ch:ch + 1], in1=acc[:, :nt],
                                                                                  op0=ALU.mult, op1=ALU.add),
                                     reads=[u, cw, acc], writes=[acc])
                            cv.append(acc)
                        k.dma("pool", self.X0T[zrow + c4 * 128:zrow + (c4 + 1) * 128, tcol:tcol + nt], cv[0][:, :nt], reads=[cv[0]])
                        k.op("pool", lambda e: e.tensor_tensor(out=cv[2][:, :nt], in0=cv[2][:, :nt], in1=cv[1][:, :nt], op=ALU.mult),
                             reads=[cv[1], cv[2]], writes=[cv[2]])
                        k.dma("pool", self.ZT[zrow + c4 * 128:zrow + (c4 + 1) * 128, tcol:tcol + nt], cv[2][:, :nt], reads=[cv[2]])

            build_group(0)
            for gi in range(len(groups)):
                if gi + 1 < len(groups):
                    build_group(gi + 1)
                project(gi)

    def attn_unit(self, S, keys, PV, Wd, pts, st, hook=None):
        k = self.k
        nk = len(keys)

        def scores(ki):
            kt, _ = keys[ki]
            ps = self.psb[(0, 1, 2, 7)[ki % 4]]
            for (kfn, ktile, rhs, rtile, c0, n) in S:
                k.op("pe", lambda e, kfn=kfn, rhs=rhs, c0=c0, n=n: e.matmul(ps[:, c0:c0 + n], lhsT=kfn(kt), rhs=rhs,
                                                                           start=True, stop=True),
                     reads=[ktile, rtile], writes=[ps])

        for ki0 in range(min(3, nk)):
            scores(ki0)
        for ki in range(nk):
            kt, bias = keys[ki]
            ps = self.psb[(0, 1, 2, 7)[ki % 4]]
            pt = pts[st["p"] % len(pts)]
            st["p"] += 1
            if bias is not None:
                btile, bap = bias
                tmp = st["tmp"][ki % 2]
                k.op("dve", lambda e, bap=bap, tmp=tmp: e.scalar_tensor_tensor(out=tmp[:, :Wd], in0=ps[:, :Wd], scalar=0.125, in1=bap,
                                                                               op0=ALU.mult, op1=ALU.add),
                     reads=[ps, btile], writes=[tmp])
                k.op("act", lambda e, tmp=tmp: e.activation(out=pt[:, :Wd], in_=tmp[:, :Wd], func=AF.Exp), reads=[tmp], writes=[pt])
            else:
                k.op("act", lambda e: e.activation(out=pt[:, :Wd], in_=ps[:, :Wd], func=AF.Exp, scale=0.125), reads=[ps], writes=[pt])
            for (vfn, vtile, c0, n, acc_ap, acct) in PV:
                k.op("pe", lambda e, vfn=vfn, c0=c0, n=n, acc_ap=acc_ap: e.matmul(acc_ap, lhsT=vfn(kt), rhs=pt[:, c0:c0 + n],
                                                                                 start=(ki == 0), stop=(ki == nk - 1)),
                     reads=[vtile, pt], writes=[acct])
            if hook is not None:
                hook(ki, pt)
            if ki + 3 < nk:
                scores(ki + 3)

    def load_vx(self, vx, c0, w, ones):
        k = self.k
        if ones:
            k.op("pool", lambda e: e.memset(vx[:, :, w:128], 1.0), writes=[vx])
        for a in range(0, 66, 11):
            k.dma("sp", vx[:, a:a + 11, 0:w], self.V[a * 128:(a + 11) * 128, c0:c0 + w].rearrange("(kt p) d -> p kt d", p=128),
                  writes=[vx])

    def epi_shift(self, pieces, Wd, dshs, yts, ui, dst_ap):
        k = self.k
        dsh, yt = dshs[ui % 2], yts[ui % 2]
        for (acct, a0, o0, n) in pieces:
            k.op("dve", lambda e, acct=acct, a0=a0, o0=o0, n=n: e.tensor_copy(out=dsh[0:64, o0:o0 + n], in_=acct[64:128, a0:a0 + n]),
                 reads=[acct], writes=[dsh])
        k.op("dve", lambda e: e.reciprocal(out=dsh[0:64, :Wd], in_=dsh[0:64, :Wd]), reads=[dsh], writes=[dsh])
        for (acct, a0, o0, n) in pieces:
            k.op("dve", lambda e, acct=acct, a0=a0, o0=o0, n=n: e.tensor_tensor(out=yt[0:64, o0:o0 + n], in0=acct[0:64, a0:a0 + n],
                                                                              in1=dsh[0:64, o0:o0 + n], op=ALU.mult),
                 reads=[acct, dsh], writes=[yt])
        k.dma("pool", dst_ap, yt[0:64, :Wd], reads=[yt])

    def phase_gqa(self, layer, with_ctx):
        k = self.k
        with k.phase():
            pts = [k.sb([128, 512], BF16, "pt") for _ in range(4)]
            dshs = [k.sb([64, 512], F32, "dsh") for _ in range(2)]
            yts = [k.sb([64, 512], BF16, "yt") for _ in range(2)]
            qs = [k.sb([64, 512], BF16, "q") for _ in range(3)]
            st = {"p": 0}
            ui = 0
            for g in range(2):
              with k.phase():
                ktg = k.sb([64, LT], BF16, "ktg")
                vx = k.sb([128, 66, 128], BF16, "vx")
                k.dma("sp", ktg[:, :], self.KT[g * 64:(g + 1) * 64, :], writes=[ktg])
                self.load_vx(vx, g * 64, 64, True)
                units = [(qb * 128, list(range(66))) for qb in range(64)]
                if with_ctx:
                    units += [(L + qb * 128, [64, 65]) for qb in range(2)]
                for (c0, kts) in units:
                    q = qs[ui % 3]
                    k.dma("sp", q[:, :].rearrange("d (h t) -> d h t", h=4),
                          self.QT[g * 256:(g + 1) * 256, c0:c0 + 128].rearrange("(h d) t -> d h t", d=64), writes=[q])
                    acct = self.psb[3 + ui % 2]
                    S = [(lambda kt: ktg[:, kt * 128:(kt + 1) * 128], ktg, q[:, :], q, 0, 512)]
                    PV = [(lambda kt: vx[:, kt, :], vx, 0, 512, acct[:, 0:512], acct)]
                    self.attn_unit(S, [(kt, None) for kt in kts], PV, 512, pts, st)
                    dst = self.YT[g * 256:(g + 1) * 256, c0:c0 + 128].rearrange("(h d) t -> d h t", d=64)
                    self.epi_shift([(acct, 0, 0, 512)], 512, dshs, yts, ui, dst)
                    ui += 1

    def phase_na(self, layer, with_ctx):
        k = self.k
        li = layer // 2
        with k.phase():
            pts = [k.sb([128, 512], BF16, "pt") for _ in range(3)]
            tmps = [k.sb([128, 512], F32, "tmp") for _ in range(2)]
            dshs = [k.sb([64, 512], F32, "dsh") for _ in range(2)]
            yts = [k.sb([64, 512], BF16, "yt") for _ in range(2)]
            qs = [k.sb([64, 256], BF16, "q") for _ in range(3)]
            ui = 0
            pc = 0
            for hg in range(4):
              with k.phase():
                kth = [k.sb([64, LT], BF16, "kth") for _ in range(2)]
                vxs = [k.sb([128, 66, 128], BF16, "vx") for _ in range(2)]
                tab = k.sb([128, 35, 256], F32, "tab")
                k.dma("sp", tab[:], self.na_tab[li, hg, :, :, :].rearrange("t p n -> p t n"), writes=[tab])
                for i in range(2):
                    h = 2 * hg + i
                    k.dma("sp", kth[i][:, :], self.KT[h * 64:(h + 1) * 64, :], writes=[kth[i]])
                    self.load_vx(vxs[i], h * 64, 64, True)
                units = []
                for qt in range(64):
                    cls, kt0 = _na_cls(qt), _na_kt0(qt)
                    keys = [(kt0 + r, cls * 7 + r) for r in range(5)] + [(64, cls * 7 + 5), (65, cls * 7 + 6)]
                    units.append((qt * 128, [keys[0:2], keys[2:4], keys[4:6], keys[6:7]]))
                if with_ctx:
                    units += [(L + cq * 128, [[(64, 5), (65, 6)]]) for cq in range(2)]
                for (c0, steps) in units:
                    q = qs[ui % 3]
                    k.dma("sp", q[:, :].rearrange("d (h t) -> d h t", h=2),
                          self.QT[hg * 128:(hg + 1) * 128, c0:c0 + 128].rearrange("(h d) t -> d h t", d=64), writes=[q])
                    accs_ = [self.psb[3 + 2 * i + ui % 2] for i in range(2)]
                    ns = len(steps)

                    def scores(si):
                        ps = self.psb[si % 3]
                        for a, (kt, _) in enumerate(steps[si]):
                            for i in range(2):
                                k.op("pe", lambda e, a=a, kt=kt, i=i, ps=ps: e.matmul(
                                    ps[:, a * 256 + i * 128:a * 256 + (i + 1) * 128], lhsT=kth[i][:, kt * 128:(kt + 1) * 128],
                                    rhs=q[:, i * 128:(i + 1) * 128], start=True, stop=True), reads=[kth[i], q], writes=[ps])

                    scores(0)
                    if ns > 1:
                        scores(1)
                    for si in range(ns):
                        ps = self.psb[si % 3]
                        pt, tmp = pts[pc % 3], tmps[pc % 2]
                        pc += 1
                        n = len(steps[si])
                        Wd = 256 * n
                        t0 = steps[si][0][1]
                        bap = tab[:, t0:t0 + n, :].rearrange("p a n -> p (a n)")
                        k.op("dve", lambda e, ps=ps, tmp=tmp, bap=bap, Wd=Wd: e.scalar_tensor_tensor(
                            out=tmp[:, :Wd], in0=ps[:, :Wd], scalar=0.125, in1=bap, op0=ALU.mult, op1=ALU.add),
                            reads=[ps, tab], writes=[tmp])
                        k.op("act", lambda e, tmp=tmp, pt=pt, Wd=Wd: e.activation(out=pt[:, :Wd], in_=tmp[:, :Wd], func=AF.Exp),
                             reads=[tmp], writes=[pt])
                        for a, (kt, _) in enumerate(steps[si]):
                            first = (si == 0 and a == 0)
                            lastk = (si == ns - 1 and a == n - 1)
                            for i in range(2):
                                k.op("pe", lambda e, a=a, kt=kt, i=i, pt=pt, first=first, lastk=lastk: e.matmul(
                                    accs_[i][:, 0:128], lhsT=vxs[i][:, kt, :], rhs=pt[:, a * 256 + i * 128:a * 256 + (i + 1) * 128],
                                    start=first, stop=lastk), reads=[vxs[i], pt], writes=[accs_[i]])
                        if si + 2 < ns:
                            scores(si + 2)
                    dst = self.YT[hg * 128:(hg + 1) * 128, c0:c0 + 128].rearrange("(h d) t -> d h t", d=64)
                    self.epi_shift([(accs_[i], 0, i * 128, 128) for i in range(2)], 256, dshs, yts, ui, dst)
                    ui += 1

    def phase_da(self, layer, with_ctx):
        k = self.k
        li = layer // 2
        lam_init = 0.8 - 0.6 * math.exp(-0.3 * layer)
        with k.phase():
            lq = k.sb([128, 4, 64], F32, "lq")
            for j, nme in enumerate(("da_lambda_q1", "da_lambda_k1", "da_lambda_q2", "da_lambda_k2")):
                k.dma("sp", lq[:, j, :], self.vec64[nme][li:li + 1, :].partition_broadcast(128), writes=[lq])
            pr = k.sb([128, 2, 64], F32, "lpr")
            ssum = k.sb([128, 2], F32, "lsum")
            neglam = k.sb([128, 1], F32, "neglam")
            for j in range(2):
                k.op("dve", lambda e, j=j: e.tensor_tensor(out=pr[:, j, :], in0=lq[:, 2 * j, :], in1=lq[:, 2 * j + 1, :], op=ALU.mult),
                     reads=[lq], writes=[pr])
                k.op("dve", lambda e, j=j: e.tensor_reduce(out=ssum[:, j:j + 1], in_=pr[:, j, :], axis=mybir.AxisListType.X, op=ALU.add),
                     reads=[pr], writes=[ssum])
            k.op("act", lambda e: e.activation(out=ssum[:, :], in_=ssum[:, :], func=AF.Exp), reads=[ssum], writes=[ssum])
            k.op("dve", lambda e: e.tensor_tensor(out=neglam[:, :], in0=ssum[:, 1:2], in1=ssum[:, 0:1], op=ALU.subtract),
                 reads=[ssum], writes=[neglam])
            k.op("dve", lambda e: e.tensor_scalar(out=neglam[:, :], in0=neglam[:, :], scalar1=-lam_init, scalar2=None, op0=ALU.add),
                 reads=[neglam], writes=[neglam])
            slcol = k.sb([128, 1], F32, "slcol")
            k.dma("sp", slcol[:, :], self.subln[li:li + 1, :].rearrange("o d -> d o"), writes=[slcol], allow_slow_non_contiguous=True)
            k.op("dve", lambda e: e.tensor_scalar(out=slcol[:, :], in0=slcol[:, :], scalar1=1.0 - lam_init, scalar2=None, op0=ALU.mult),
                 reads=[slcol], writes=[slcol])
            pts = [k.sb([128, 512], BF16, "pt") for _ in range(4)]
            qs = [k.sb([64, 512], BF16, "q") for _ in range(3)]
            mk = lambda nm: [k.sb([128, 512], F32, nm) for _ in range(2)]
            paccs, rds, ys = mk("pacc"), mk("rd"), mk("yy")
            outs = [k.sb([128, 256], BF16, "ob") for _ in range(2)]
            st = {"p": 0}
            ui = 0
            for j in range(4):
              with k.phase():
                k1 = k.sb([64, LT], BF16, "k1")
                k2 = k.sb([64, LT], BF16, "k2")
                vj = k.sb([128, 66, 128], BF16, "vj")
                r0 = 512 + j * 128
                k.dma("sp", k1[:, :], self.KT[r0:r0 + 64, :], writes=[k1])
                k.dma("sp", k2[:, :], self.KT[r0 + 64:r0 + 128, :], writes=[k2])
                self.load_vx(vj, r0, 128, False)
                units = [(qb * 256, list(range(66))) for qb in range(32)]
                if with_ctx:
                    units.append((L, [64, 65]))
                for (c0, kts) in units:
                    q = qs[ui % 3]
                    k.dma("sp", q[:, :].rearrange("d (h t) -> d h t", h=2),
                          self.QT[r0:r0 + 128, c0:c0 + 256].rearrange("(h d) t -> d h t", d=64), writes=[q])
                    accn = self.psb[3 + ui % 2]
                    pd, pm = self.psb[5], self.psb[6]
                    pacc, rd, y = paccs[ui % 2], rds[ui % 2], ys[ui % 2]
                    ob = outs[ui % 2]
                    S = [(lambda kt: k1[:, kt * 128:(kt + 1) * 128], k1, q[:, 0:256], q, 0, 256),
                         (lambda kt: k2[:, kt * 128:(kt + 1) * 128], k2, q[:, 256:512], q, 256, 256)]
                    PV = [(lambda kt: vj[:, kt, :], vj, 0, 512, accn[:, 0:512], accn)]

                    def hook(ki, pt, pacc=pacc):
                        if ki == 0:
                            k.op("dve", lambda e: e.tensor_copy(out=pacc[:, :], in_=pt[:, :]), reads=[pt], writes=[pacc])
                        else:
                            k.op("dve", lambda e: e.tensor_tensor(out=pacc[:, :], in0=pacc[:, :], in1=pt[:, :], op=ALU.add),
                                 reads=[pt, pacc], writes=[pacc])
                    self.attn_unit(S, [(kt, None) for kt in kts], PV, 512, pts, st, hook=hook)
                    k.op("pe", lambda e: e.matmul(pd[:, :], lhsT=self.ones32, rhs=pacc[:, :], start=True, stop=True),
                         reads=[pacc, self.cst], writes=[pd])
                    k.op("dve", lambda e: e.reciprocal(out=rd[:, :], in_=pd[:, :]), reads=[pd], writes=[rd])
                    k.op("dve", lambda e: e.tensor_tensor(out=rd[:, :], in0=accn[:, :], in1=rd[:, :], op=ALU.mult),
                         reads=[accn, rd], writes=[rd])
                    k.op("dve", lambda e: e.scalar_tensor_tensor(out=y[:, 0:256], in0=rd[:, 256:512], scalar=neglam[:, 0:1],
                                                                 in1=rd[:, 0:256], op0=ALU.mult, op1=ALU.add),
                         reads=[rd, neglam], writes=[y])
                    k.op("act", lambda e: e.activation(out=y[:, 256:512], in_=y[:, 0:256], func=AF.Square), reads=[y], writes=[y])
                    k.op("pe", lambda e: e.matmul(pm[:, 0:256], lhsT=self.o128, rhs=y[:, 256:512], start=True, stop=True),
                         reads=[y, self.cst], writes=[pm])
                    k.op("act", lambda e: e.activation(out=rd[:, 0:256], in_=pm[:, 0:256], func=AF.Sqrt, bias=self.epsc[:, 0:1]),
                         reads=[pm, self.epsc], writes=[rd])
                    k.op("dve", lambda e: e.reciprocal(out=rd[:, 0:256], in_=rd[:, 0:256]), reads=[rd], writes=[rd])
                    k.op("dve", lambda e: e.scalar_tensor_tensor(out=ob[:, :], in0=y[:, 0:256], scalar=slcol[:, 0:1], in1=rd[:, 0:256],
                                                                 op0=ALU.mult, op1=ALU.mult), reads=[y, slcol, rd], writes=[ob])
                    k.dma("pool", self.YT[r0:r0 + 128, c0:c0 + 256], ob[:, :], reads=[ob])
                    ui += 1


    def phase_hyena(self, layer, with_ctx):
        k = self.k
        li = layer // 2
        PI = math.pi
        sets = [0, 1] if with_ctx else [0]
        rnorms = {}
        with k.phase():
            rn_all = k.sb([128, 2, 4], F32, "rnorm")
            skipc = k.sb([128, 4], F32, "skipc")
            k.dma("sp", skipc[:, :], self.hy_skip[li:li + 1, :].rearrange("o (c p) -> p (o c)", p=128), writes=[skipc],
                  allow_slow_non_contiguous=True)
            with k.phase():
                w1 = k.sb([33, 64], F32, "w1")
                w2 = k.sb([64, 64], F32, "w2")
                w3 = k.sb([64, 64], F32, "w3")
                w4 = k.sb([64, 1024], F32, "w4")
                k.dma("sp", w1[:, :], self.hy_w1[li, :, :], writes=[w1])
                k.dma("sp", w2[:, :], self.hy_w2[li, :, :], writes=[w2])
                k.dma("sp", w3[:, :], self.hy_w3[li, :, :], writes=[w3])
                k.dma("sp", w4[:, :], self.hy_w4[li, :, :], writes=[w4])
                fc = k.sb([64, 4], F32, "fcol")
                for j, nme in enumerate(("hy_freq", "hy_b1", "hy_b2", "hy_b3")):
                    k.dma("sp", fc[:, j:j + 1], self.vec64[nme][li:li + 1, :].rearrange("o d -> d o"), writes=[fc],
                          allow_slow_non_contiguous=True)
                for j in (1, 2, 3):
                    k.op("dve", lambda e, j=j: e.tensor_tensor(out=fc[:, j:j + 1], in0=fc[:, j:j + 1], in1=fc[:, 0:1], op=ALU.mult),
                         reads=[fc], writes=[fc])
                negpi = k.sb([64, 1], F32, "negpi")
                k.op("dve", lambda e: e.memset(negpi[:, :], -PI), writes=[negpi])
                ndel = k.sb([128, 4], F32, "ndel")
                k.dma("sp", ndel[:, :], self.hy_ndel[:, :], writes=[ndel])
                zts = [k.sb([33, 512], F32, "zt") for _ in range(2)]
                tls = [k.sb([128, 512], F32, "tl") for _ in range(2)]
                hs = [k.sb([64, 512], F32, "hh") for _ in range(3)]
                decs = [k.sb([128, 512], F32, "dec") for _ in range(2)]
                hks = [k.sb([128, 512], F32, "hk") for _ in range(3)]
                junk = k.sb([128, 512], F32, "junkf")
                hi_ = k.sb([64, 512], mybir.dt.int32, "hi")
                hf_ = k.sb([64, 512], F32, "hf")
                part = k.sb([128, 8, 16], F32, "part")
                nrm = k.sb([128, 8], F32, "nrm")
                bi = 0
                for s_ in sets:
                    k.op("dve", lambda e: e.memset(part[:], 0.0), writes=[part])
                    nblk = 16 if s_ == 0 else 1
                    for blk in range(nblk):
                        zt, tl = zts[bi % 2], tls[bi % 2]
                        c0 = blk * 512
                        k.dma("sp", zt[:, :], self.hy_zT[s_, :, c0:c0 + 512], writes=[zt])
                        k.dma("sp", tl[:, :], self.hy_tlin[s_:s_ + 1, c0:c0 + 512].partition_broadcast(128), writes=[tl])
                        src, srct, kk = zt[:, :], zt, 33
                        for li_, wm in enumerate((w1, w2, w3)):
                            ps = self.psb[li_ % 2]
                            h = hs[li_]
                            k.op("pe", lambda e, wm=wm, src=src, kk=kk, ps=ps: e.matmul(ps[0:64, :], lhsT=wm[0:kk, :], rhs=src,
                                                                                      start=True, stop=True),
                                 reads=[wm, srct], writes=[ps])
                            k.op("dve", lambda e, ps=ps, h=h, li_=li_: e.tensor_scalar(
                                out=h[:, :], in0=ps[0:64, :], scalar1=fc[:, 0:1], scalar2=fc[:, li_ + 1:li_ + 2],
                                op0=ALU.mult, op1=ALU.add), reads=[ps, fc], writes=[h])
                            k.op("dve", lambda e, h=h: e.tensor_scalar(out=h[:, :], in0=h[:, :], scalar1=1.0 / (2.0 * PI), scalar2=16.5,
                                                                       op0=ALU.mult, op1=ALU.add), reads=[h], writes=[h])
                            k.op("dve", lambda e, h=h: e.tensor_copy(out=hi_[:, :], in_=h[:, :]), reads=[h], writes=[hi_])
                            k.op("dve", lambda e, h=h: e.tensor_copy(out=hf_[:, :], in_=hi_[:, :]), reads=[hi_], writes=[hf_])
                            k.op("dve", lambda e, h=h: e.tensor_tensor(out=h[:, :], in0=h[:, :], in1=hf_[:, :], op=ALU.subtract),
                                 reads=[h, hf_], writes=[h])
                            k.op("dve", lambda e, h=h: e.scalar_tensor_tensor(out=h[:, :], in0=h[:, :], scalar=0.0, in1=h[:, :],
                                                                              op0=ALU.is_lt, op1=ALU.add), reads=[h], writes=[h])
                            k.op("act", lambda e, h=h: e.activation(out=h[:, :], in_=h[:, :], func=AF.Sin, bias=negpi[:, 0:1], scale=2.0 * PI),
                                 reads=[h, negpi], writes=[h])
                            src, srct, kk = h[:, :], h, 64
                        h3 = hs[2]
                        for cch in range(8):
                            cc = cch % 4
                            ps = self.psb[2 + cch % 2]
                            dec, hk = decs[cch % 2], hks[cch % 3]
                            k.op("pe", lambda e, cch=cch, ps=ps: e.matmul(ps[:, :], lhsT=w4[:, cch * 128:(cch + 1) * 128], rhs=h3[:, :],
                                                                         start=True, stop=True), reads=[w4, h3], writes=[ps])
                            k.op("act", lambda e, dec=dec, cc=cc: e.activation(out=dec[:, :], in_=tl[:, :], func=AF.Exp,
                                                                               scale=ndel[:, cc:cc + 1]), reads=[tl, ndel], writes=[dec])
                            k.op("dve", lambda e, ps=ps, dec=dec, hk=hk: e.tensor_tensor(out=hk[:, :], in0=ps[:, :], in1=dec[:, :],
                                                                                       op=ALU.mult), reads=[ps, dec], writes=[hk])
                            if cch >= 4 and blk == 0:
                                k.op("dve", lambda e, hk=hk: e.memset(hk[:, 0:1], 0.0), writes=[hk])
                            k.op("act", lambda e, hk=hk, cch=cch, blk=blk: e.activation(out=junk[:, :], in_=hk[:, :], func=AF.Abs,
                                                                                       accum_out=part[:, cch, blk:blk + 1]),
                                 reads=[hk], writes=[junk, part])
                            k.dma("pool", self.HF[s_, cch * 128:(cch + 1) * 128, c0:c0 + 512], hk[:, :], reads=[hk])
                        bi += 1
                    k.op("dve", lambda e: e.tensor_reduce(out=nrm[:, :], in_=part[:, :, :], axis=mybir.AxisListType.X, op=ALU.add),
                         reads=[part], writes=[nrm])
                    k.op("dve", lambda e, s_=s_: e.tensor_tensor(out=rn_all[:, s_, :], in0=nrm[:, 0:4], in1=nrm[:, 4:8], op=ALU.add),
                         reads=[nrm], writes=[rn_all])
                    k.op("dve", lambda e, s_=s_: e.reciprocal(out=rn_all[:, s_, :], in_=rn_all[:, s_, :]), reads=[rn_all], writes=[rn_all])
            with k.phase():
                dft = k.sb([128, 4, 128], F32, "dft")
                k.dma("sp", dft[:], self.dft[:, :, :].rearrange("a p n -> p a n"), writes=[dft])
                Fc, Fs, TWc, TWs = (dft[:, a, :] for a in range(4))
                fx = k.sb([128, 6, 128], F32, "fx")
                k.op("dve", lambda e: e.tensor_copy(out=fx[:, 0, :], in_=Fc), reads=[dft], writes=[fx])
                k.op("dve", lambda e: e.tensor_copy(out=fx[:, 1, :], in_=Fs), reads=[dft], writes=[fx])
                k.op("dve", lambda e: e.tensor_scalar(out=fx[:, 2, :], in0=Fs, scalar1=-1.0, scalar2=None, op0=ALU.mult),
                     reads=[dft], writes=[fx])
                k.op("dve", lambda e: e.tensor_scalar(out=fx[:, 3, :], in0=Fc, scalar1=-1.0, scalar2=None, op0=ALU.mult),
                     reads=[dft], writes=[fx])
                k.op("dve", lambda e: e.tensor_copy(out=fx[:, 4, :], in_=Fs), reads=[dft], writes=[fx])
                k.op("dve", lambda e: e.tensor_copy(out=fx[:, 5, :], in_=Fc), reads=[dft], writes=[fx])
                FcFs = fx[:, 0:2, :].rearrange("p a n -> p (a n)")
                FcnFs = None
                nFs, nFc = fx[:, 2, :], fx[:, 3, :]
                r1 = k.sb([128, 2, 128], F32, "r1")
                k.op("dve", lambda e: e.tensor_copy(out=r1[:, 0, :], in_=Fc), reads=[dft], writes=[r1])
                k.op("dve", lambda e: e.tensor_copy(out=r1[:, 1, :], in_=nFs), reads=[fx], writes=[r1])
                R1 = r1[:, :, :].rearrange("p a n -> p (a n)")
                R2 = fx[:, 4:6, :].rearrange("p a n -> p (a n)")
                mk = lambda shape, nm, n=2: [k.sb(shape, F32, nm) for _ in range(n)]
                dts = mk([64, 3, 4, 128], "dt")
                p1s, p2s = mk([128, 512], "p1"), mk([128, 512], "p2")
                Bs = mk([128, 3, 2, 512], "B", 2)
                kfs = mk([128, 2, 512], "kf")
                Ys = mk([128, 2, 512], "Y")
                tts = mk([128, 2, 512], "tt")
                Ds = mk([128, 2, 512], "D")
                youts = mk([64, 512], "yo")
                gi = 0
                for s_ in sets:
                    nr = 64 if s_ == 0 else 2
                    ncol = nr * 128
                    zrow = 0 if s_ == 0 else 512
                    for g4 in range(128):
                        c0 = g4 * 4
                        dt_, B, kf, Y, tt, Dd, yo = dts[gi % 2], Bs[gi % 2], kfs[gi % 2], Ys[gi % 2], tts[gi % 2], Ds[gi % 2], youts[gi % 2]
                        srcs = (self.ZT[zrow + c0:zrow + c0 + 4, 0:ncol], self.HF[s_, c0:c0 + 4, 0:ncol],
                                self.HF[s_, 512 + c0:512 + c0 + 4, 0:ncol])
                        for sg in range(3):
                            k.dma("sp", dt_[0:nr, sg, :, :], srcs[sg].rearrange("c (a b) -> a c b", b=128), writes=[dt_])
                        for sg in range(3):
                            for pr_ in range(2):
                                pb = self.psb[(sg * 2 + pr_) % 2]
                                p1, p2 = p1s[(sg * 2 + pr_) % 2], p2s[(sg * 2 + pr_) % 2]
                                for jj in range(2):
                                    j = pr_ * 2 + jj
                                    k.op("pe", lambda e, sg=sg, j=j, jj=jj, pb=pb: e.matmul(
                                        pb[:, jj * 256:(jj + 1) * 256], lhsT=dt_[0:nr, sg, j, :], rhs=FcFs[0:nr, :], start=True, stop=True),
                                        reads=[dt_, fx], writes=[pb])
                                pbv = pb[:, :].rearrange("p (a n) -> p a n", n=128)
                                k.op("dve", lambda e, pbv=pbv, p1=p1: e.tensor_tensor(
                                    out=p1[:, :].rearrange("p (a n) -> p a n", n=128), in0=pbv, in1=TWc.unsqueeze(1).to_broadcast([128, 4, 128]),
                                    op=ALU.mult), reads=[pb, dft], writes=[p1])
                                k.op("dve", lambda e, pbv=pbv, p2=p2: e.tensor_tensor(
                                    out=p2[:, :].rearrange("p (a n) -> p a n", n=128), in0=pbv, in1=TWs.unsqueeze(1).to_broadcast([128, 4, 128]),
                                    op=ALU.mult), reads=[pb, dft], writes=[p2])
                                for jj in range(2):
                                    j = pr_ * 2 + jj
                                    k.op("pool", lambda e, sg=sg, j=j, jj=jj, p1=p1, p2=p2: e.tensor_tensor(
                                        out=B[:, sg, 0, j * 128:(j + 1) * 128], in0=p1[:, jj * 256:jj * 256 + 128],
                                        in1=p2[:, jj * 256 + 128:jj * 256 + 256], op=ALU.subtract), reads=[p1, p2], writes=[B])
                                    k.op("pool", lambda e, sg=sg, j=j, jj=jj, p1=p1, p2=p2: e.tensor_tensor(
                                        out=B[:, sg, 1, j * 128:(j + 1) * 128], in0=p2[:, jj * 256:jj * 256 + 128],
                                        in1=p1[:, jj * 256 + 128:jj * 256 + 256], op=ALU.add), reads=[p1, p2], writes=[B])
                        xre, xim, kre, kim = self.psb[2], self.psb[3], self.psb[4], self.psb[5]
                        mm = lambda out, lhsT, rhs, st_, sp_, rt: k.op(
                            "pe", lambda e: e.matmul(out[:, :], lhsT=lhsT, rhs=rhs, start=st_, stop=sp_), reads=[rt, fx, dft], writes=[out])
                        mm(xre, Fc, B[:, 0, 0, :], True, False, B)
                        mm(xre, nFs, B[:, 0, 1, :], False, True, B)
                        mm(xim, Fc, B[:, 0, 1, :], True, False, B)
                        mm(xim, Fs, B[:, 0, 0, :], False, True, B)
                        mm(kre, Fc, B[:, 1, 0, :], True, False, B)
                        mm(kre, nFs, B[:, 1, 1, :], False, False, B)
                        mm(kre, Fc, B[:, 2, 0, :], False, False, B)
                        mm(kre, nFs, B[:, 2, 1, :], False, True, B)
                        mm(kim, Fc, B[:, 1, 1, :], True, False, B)
                        mm(kim, Fs, B[:, 1, 0, :], False, False, B)
                        mm(kim, nFc, B[:, 2, 1, :], False, False, B)
                        mm(kim, nFs, B[:, 2, 0, :], False, True, B)
                        k.op("act", lambda e: e.activation(out=kf[:, 0, :], in_=kre[:, :], func=AF.Copy), reads=[kre], writes=[kf])
                        k.op("act", lambda e: e.activation(out=kf[:, 1, :], in_=kim[:, :], func=AF.Copy), reads=[kim], writes=[kf])
                        tt4 = lambda out, a, b, op, rd, wr, eng="dve": k.op(eng, lambda e: e.tensor_tensor(out=out, in0=a, in1=b, op=op),
                                                                            reads=rd, writes=wr)
                        tt4(tt[:, 0, :], xre[:, :], kf[:, 0, :], ALU.mult, [xre, kf], [tt])
                        tt4(tt[:, 1, :], xim[:, :], kf[:, 1, :], ALU.mult, [xim, kf], [tt])
                        tt4(Y[:, 0, :], tt[:, 0, :], tt[:, 1, :], ALU.subtract, [tt], [Y], "pool")
                        tt4(tt[:, 0, :], xre[:, :], kf[:, 1, :], ALU.mult, [xre, kf, Y], [tt])
                        tt4(tt[:, 1, :], xim[:, :], kf[:, 0, :], ALU.mult, [xim, kf], [tt])
                        tt4(Y[:, 1, :], tt[:, 0, :], tt[:, 1, :], ALU.add, [tt], [Y], "pool")
                        for pr_ in range(2):
                            pb = self.psb[6 + pr_]
                            p1, p2 = p1s[pr_], p2s[pr_]
                            for jj in range(2):
                                j = pr_ * 2 + jj
                                k.op("pe", lambda e, j=j, jj=jj, pb=pb: e.matmul(pb[:, jj * 256:(jj + 1) * 256], lhsT=Y[:, 0, j * 128:(j + 1) * 128],
                                                                                rhs=R1, start=True, stop=False), reads=[Y, r1], writes=[pb])
                                k.op("pe", lambda e, j=j, jj=jj, pb=pb: e.matmul(pb[:, jj * 256:(jj + 1) * 256], lhsT=Y[:, 1, j * 128:(j + 1) * 128],
                                                                                rhs=R2, start=False, stop=True), reads=[Y, fx], writes=[pb])
                            pbv = pb[:, :].rearrange("p (a n) -> p a n", n=128)
                            k.op("dve", lambda e, pbv=pbv, p1=p1: e.tensor_tensor(
                                out=p1[:, :].rearrange("p (a n) -> p a n", n=128), in0=pbv, in1=TWc.unsqueeze(1).to_broadcast([128, 4, 128]),
                                op=ALU.mult), reads=[pb, dft], writes=[p1])
                            k.op("dve", lambda e, pbv=pbv, p2=p2: e.tensor_tensor(
                                out=p2[:, :].rearrange("p (a n) -> p a n", n=128), in0=pbv, in1=TWs.unsqueeze(1).to_broadcast([128, 4, 128]),
                                op=ALU.mult), reads=[pb, dft], writes=[p2])
                            for jj in range(2):
                                j = pr_ * 2 + jj
                                k.op("pool", lambda e, j=j, jj=jj, p1=p1, p2=p2: e.tensor_tensor(
                                    out=Dd[:, 0, j * 128:(j + 1) * 128], in0=p1[:, jj * 256:jj * 256 + 128],
                                    in1=p2[:, jj * 256 + 128:jj * 256 + 256], op=ALU.add), reads=[p1, p2], writes=[Dd])
                                k.op("pool", lambda e, j=j, jj=jj, p1=p1, p2=p2: e.tensor_tensor(
                                    out=Dd[:, 1, j * 128:(j + 1) * 128], in0=p1[:, jj * 256 + 128:jj * 256 + 256],
                                    in1=p2[:, jj * 256:jj * 256 + 128], op=ALU.subtract), reads=[p1, p2], writes=[Dd])
                        py = self.psb[0]
                        k.op("pe", lambda e: e.matmul(py[0:nr, :], lhsT=Fc[:, 0:nr], rhs=Dd[:, 0, :], start=True, stop=False),
                             reads=[Dd, dft], writes=[py])
                        k.op("pe", lambda e: e.matmul(py[0:nr, :], lhsT=Fs[:, 0:nr], rhs=Dd[:, 1, :], start=False, stop=True),
                             reads=[Dd, dft], writes=[py])
                        k.op("act", lambda e: e.activation(out=yo[0:nr, :], in_=py[0:nr, :], func=AF.Copy, scale=1.0 / 16384.0),
                             reads=[py], writes=[yo])
                        k.dma("pool", self.YH[zrow + c0:zrow + c0 + 4, 0:ncol].rearrange("c (a b) -> a c b", b=128),
                              yo[0:nr, :].rearrange("a (c b) -> a c b", b=128), reads=[yo])
                        gi += 1
            with k.phase():
                mk = lambda nm: [k.sb([128, 2048], F32, nm) for _ in range(2)]
                ys, zs, x0s = mk("hy"), mk("hz"), mk("hx0")
                obs = [k.sb([128, 2048], BF16, "hob") for _ in range(2)]
                bi = 0
                for s_ in sets:
                    zrow = 0 if s_ == 0 else 512
                    blocks = [(b * 2048, 2048) for b in range(4)] if s_ == 0 else [(0, 256)]
                    for cc in range(4):
                        for (t0, n) in blocks:
                            yt, zt, xt, ob = ys[bi % 2], zs[bi % 2], x0s[bi % 2], obs[bi % 2]
                            rows = slice(zrow + cc * 128, zrow + (cc + 1) * 128)
                            k.dma("sp", yt[:, :n], self.YH[rows, t0:t0 + n], writes=[yt])
                            k.dma("sp", zt[:, :n], self.ZT[rows, t0:t0 + n], writes=[zt])
                            k.dma("sp", xt[:, :n], self.X0T[rows, t0:t0 + n], writes=[xt])
                            k.op("dve", lambda e, yt=yt, cc=cc, n=n, s_=s_: e.tensor_scalar(
                                out=yt[:, :n], in0=yt[:, :n], scalar1=rn_all[:, s_, cc:cc + 1], scalar2=None, op0=ALU.mult),
                                reads=[yt, rn_all], writes=[yt])
                            k.op("dve", lambda e, yt=yt, zt=zt, cc=cc, n=n: e.scalar_tensor_tensor(
                                out=yt[:, :n], in0=zt[:, :n], scalar=skipc[:, cc:cc + 1], in1=yt[:, :n], op0=ALU.mult, op1=ALU.add),
                                reads=[yt, zt, skipc], writes=[yt])
                            k.op("pool", lambda e, yt=yt, xt=xt, ob=ob, n=n: e.tensor_tensor(out=ob[:, :n], in0=yt[:, :n], in1=xt[:, :n],
                                                                                           op=ALU.mult), reads=[yt, xt], writes=[ob])
                            tcol = t0 if s_ == 0 else L
                            k.dma("pool", self.YT[512 + cc * 128:512 + (cc + 1) * 128, tcol:tcol + n], ob[:, :n], reads=[ob])
                            bi += 1

    def phase_out(self, layer, with_ctx):
        k = self.k
        odd = layer % 2
        li = layer // 2
        wsrc = (self.w_out_o if odd else self.w_out_e)[li]
        with k.phase():
            WO = k.sb([128, 8, D], BF16, "wo")
            stage = [k.sb([128, 8, 512], F32, "stg") for _ in range(2)]
            self.load_weight(WO, wsrc, D, stage)
            ytbs = [k.sb([128, 8, 512], BF16, "ytb") for _ in range(2)]
            xts = [k.sb([128, D], F32, "xt") for _ in range(2)]
            tmps = [k.sb([128, D], F32, "tmp") for _ in range(2)]
            groups = [(g * 512, 512, 0) for g in range(16)] + ([(L, 256, 1)] if with_ctx else [])
            c = 0
            for gi, (r0, nt, lc) in enumerate(groups):
                ytb = ytbs[gi % 2]
                for kc in range(8):
                    k.dma("sp", ytb[:, kc, :nt], self.YT[kc * 128:(kc + 1) * 128, r0:r0 + nt], writes=[ytb])
                gt = self.G[lc]
                for i in range(nt // 128):
                    xt, tmp = xts[c % 2], tmps[c % 2]
                    rr = r0 + i * 128
                    k.dma("sp", xt[:, :], self.XR[rr:rr + 128, :], writes=[xt])
                    for half in range(2):
                        pp = self.psb[2 * (c % 2) + half]
                        for kc in range(8):
                            k.op("pe", lambda e, kc=kc, pp=pp, half=half: e.matmul(
                                pp[:, :], lhsT=ytb[:, kc, i * 128:(i + 1) * 128], rhs=WO[:, kc, half * 512:(half + 1) * 512],
                                start=(kc == 0), stop=(kc == 7)), reads=[ytb, WO], writes=[pp])
                        k.op("dve", lambda e, pp=pp, half=half: e.tensor_tensor(
                            out=tmp[:, half * 512:(half + 1) * 512], in0=pp[:, :], in1=gt[:, half * 512:(half + 1) * 512], op=ALU.mult),
                            reads=[pp, gt], writes=[tmp])
                    k.op("pool", lambda e: e.tensor_tensor(out=tmp[:, :], in0=tmp[:, :], in1=xt[:, :], op=ALU.add),
                         reads=[tmp, xt], writes=[tmp])
                    k.dma("pool", self.XR[rr:rr + 128, :], tmp[:, :], reads=[tmp])
                    c += 1

    def phase_ffn(self, layer, with_ctx, last):
        k = self.k
        with k.phase():
            WU = k.sb([128, 8, 2 * FH], BF16, "wu")
            WDn = k.sb([128, NCH_F, D], BF16, "wd")
            with k.phase():
                stage = [k.sb([128, 8, 512], F32, "stg") for _ in range(2)]
                self.load_weight(WU, self.w_up[layer], 2 * FH, stage)
            with k.phase():
                stage = [k.sb([128, NCH_F, 128], F32, "stg") for _ in range(2)]
                self.load_weight(WDn, self.w_down[layer], D, stage, bw=128)
            fw = k.sb([128, 3, NCH_F], F32, "fw")
            fb = k.sb([128, NCH_F], F32, "fb")
            for j in range(3):
                k.dma("sp", fw[:, j, :], self.fcw[layer, j:j + 1, :].rearrange("o (c p) -> p (o c)", p=128), writes=[fw],
                      allow_slow_non_contiguous=True)
            k.dma("sp", fb[:], self.fcb[layer:layer + 1, :].rearrange("o (c p) -> p (o c)", p=128), writes=[fb],
                  allow_slow_non_contiguous=True)
            build = self.mk_build(SH2, SC2)
            hts = [k.sb([128, 8, 258], BF16, "ht") for _ in range(2)]
            AT = k.sb([128, NCH_F, 256], BF16, "at")
            accs = [k.sb([128, 256], F32, "acc") for _ in range(2)]
            sgs = [k.sb([128, 256], F32, "sg") for _ in range(2)]
            xr = k.sb([128, D], F32, "xr")
            xo = k.sb([128, D], F32, "xo")
            sgroups = [(s * 256, 0) for s in range(32)] + ([(L, 1)] if with_ctx else [])

            def build_sg(si):
                r0, lc = sgroups[si]
                ht = hts[si % 2]
                prev = hts[(si - 1) % 2]
                for i in range(2):
                    build(ht, 1 + i * 128, r0 + i * 128, lc)
                if si == 0 or lc == 1:
                    k.op("pool", lambda e: e.memset(ht[:, :, 0:1], 0.0), writes=[ht])
                else:
                    k.op("pool", lambda e: e.tensor_copy(out=ht[:, :, 0:1], in_=prev[:, :, 256:257]), reads=[prev], writes=[ht])
                if lc == 1:
                    k.op("pool", lambda e: e.memset(ht[:, :, 257:258], 0.0), writes=[ht])
                if si > 0:
                    if lc == 1:
                        k.op("pool", lambda e: e.memset(prev[:, :, 257:258], 0.0), writes=[prev])
                    else:
                        k.op("pool", lambda e: e.tensor_copy(out=prev[:, :, 257:258], in_=ht[:, :, 1:2]), reads=[ht], writes=[prev])

            def run_sg(si):
                r0, lc = sgroups[si]
                ht = hts[si % 2]
                if si == len(sgroups) - 1 and lc == 0:
                    k.op("pool", lambda e: e.memset(ht[:, :, 257:258], 0.0), writes=[ht])
                for j in range(NCH_F):
                    pg, pv = self.psb[2 + 2 * (j % 2)], self.psb[3 + 2 * (j % 2)]
                    acc, sg = accs[j % 2], sgs[j % 2]
                    for kc in range(8):
                        k.op("pe", lambda e, kc=kc: e.matmul(pg[:, 0:258], lhsT=WU[:, kc, j * 128:(j + 1) * 128], rhs=ht[:, kc, 0:258],
                                                              start=(kc == 0), stop=(kc == 7)), reads=[WU, ht], writes=[pg])
                    for kc in range(8):
                        k.op("pe", lambda e, kc=kc: e.matmul(pv[:, 0:256], lhsT=WU[:, kc, FH + j * 128:FH + (j + 1) * 128],
                                                              rhs=ht[:, kc, 1:257], start=(kc == 0), stop=(kc == 7)),
                             reads=[WU, ht], writes=[pv])
                    k.op("dve", lambda e: e.tensor_scalar(out=acc[:, :], in0=pg[:, 0:256], scalar1=fw[:, 0, j:j + 1], scalar2=fb[:, j:j + 1],
                                                          op0=ALU.mult, op1=ALU.add), reads=[pg, fw, fb], writes=[acc])
                    for t in (1, 2):
                        k.op("dve", lambda e, t=t: e.scalar_tensor_tensor(out=acc[:, :], in0=pg[:, t:t + 256], scalar=fw[:, t, j:j + 1],
                                                                          in1=acc[:, :], op0=ALU.mult, op1=ALU.add),
                             reads=[pg, fw, acc], writes=[acc])
                    k.op("act", lambda e: e.activation(out=sg[:, :], in_=acc[:, :], func=AF.Silu), reads=[acc], writes=[sg])
                    k.op("dve", lambda e: e.tensor_tensor(out=AT[:, j, :], in0=sg[:, :], in1=pv[:, 0:256], op=ALU.mult),
                         reads=[sg, pv], writes=[AT])
                gt = self.G[2 + lc]
                for i in range(2):
                    rr = r0 + i * 128
                    k.dma("sp", xr[:, :], self.XR[rr:rr + 128, :], writes=[xr])
                    for half in range(2):
                        pp = self.psb[6 + half]
                        for j in range(NCH_F):
                            k.op("pe", lambda e, j=j, pp=pp, half=half: e.matmul(
                                pp[:, :], lhsT=AT[:, j, i * 128:(i + 1) * 128], rhs=WDn[:, j, half * 512:(half + 1) * 512],
                                start=(j == 0), stop=(j == NCH_F - 1)), reads=[AT, WDn], writes=[pp])
                        k.op("dve", lambda e, pp=pp, half=half: e.tensor_tensor(
                            out=xo[:, half * 512:(half + 1) * 512], in0=pp[:, :], in1=gt[:, half * 512:(half + 1) * 512], op=ALU.mult),
                            reads=[pp, gt], writes=[xo])
                    k.op("pool", lambda e: e.tensor_tensor(out=xo[:, :], in0=xo[:, :], in1=xr[:, :], op=ALU.add),
                         reads=[xo, xr], writes=[xo])
                    if last and lc == 0:
                        k.dma("pool", self.y[rr:rr + 128, :], xo[:, :], reads=[xo])
                    else:
                        k.dma("pool", self.XR[rr:rr + 128, :], xo[:, :], reads=[xo])

            build_sg(0)
            for si in range(len(sgroups)):
                if si + 1 < len(sgroups):
                    build_sg(si + 1)
                run_sg(si)


    def build(self):
        k = self.k
        stop = getattr(self, "stop", "full")
        with k.phase():
            self.copy_in()
        for layer in getattr(self, "layers", range(self.n_layers)):
            with_ctx = layer < DEPTH - 1
            last = layer == DEPTH - 1
            fin = layer == list(getattr(self, "layers", range(self.n_layers)))[-1]
            self.phase_ada(layer)
            if fin and stop == "ada":
                break
            self.phase_in(layer)
            if fin and stop == "in":
                break
            if layer % 2 == 0:
                self.phase_na(layer, with_ctx)
                self.phase_da(layer, with_ctx)
            else:
                if "nogqa" not in stop:
                    self.phase_gqa(layer, with_ctx)
                if "nohy" not in stop:
                    self.phase_hyena(layer, with_ctx)
            if fin and stop.startswith("attn"):
                break
            self.phase_out(layer, with_ctx)
            if fin and stop == "out":
                break
            self.phase_ffn(layer, with_ctx, last)
            if not fin:
                k.fresh()
        k.barrier()
        return self.nc


_CACHE = {}


def _host_inputs(inputs):
    f = lambda a: np.ascontiguousarray(np.asarray(a, dtype=np.float32))
    inp = {n: f(v) for n, v in inputs.items()}
    rc, rs = _rope_tables()
    zT, tlin, ndel, fc, fs, tc_, ts_ = _hy_tables()
    shared = {n: inp[n] for n in ("w_ada", "b_ada", "w_up", "ffn_conv_w", "ffn_conv_b", "w_down", "w_in_e", "w_out_e",
                                  "w_in_o", "w_out_o", "na_q_gain", "na_k_gain", "da_q_gain", "da_k_gain", "da_lambda_q1",
                                  "da_lambda_k1", "da_lambda_q2", "da_lambda_k2", "gqa_q_gain", "gqa_k_gain", "hy_b1", "hy_b2",
                                  "hy_b3", "hy_freq", "da_subln_gain", "hy_conv_w", "hy_conv_b", "hy_w1", "hy_w2", "hy_w3",
                                  "hy_w4", "hy_skip")}
    shared["rope_cos"] = rc
    shared["rope_sin"] = rs
    shared["na_tab"] = np.stack([_na_tables(inp["na_rpb"][i]) for i in range(2)]).reshape(2, 4, 35, 128, 256)
    shared["hy_zT"] = zT
    shared["hy_tlin"] = tlin
    shared["hy_ndel"] = ndel
    shared["dft"] = np.stack([fc, fs, tc_, ts_])
    shared["consts"] = _consts()
    maps = []
    for core in range(8):
        b = core % 4
        m = dict(shared)
        m["x"] = inp["x"][b]
        m["ctx"] = inp["ctx"][b]
        m["cc"] = np.ascontiguousarray(np.stack([inp["c"][b], inp["c_ctx"]]))
        maps.append(m)
    return maps


def _layer_maps(maps, layer, xs, cs):
    out = []
    for core, m in enumerate(maps):
        d = {}
        for n, v in m.items():
            if n in _PER4:
                d[n] = np.ascontiguousarray(v[layer:layer + 1])
            elif n in _PER2:
                d[n] = np.ascontiguousarray(v[layer // 2:layer // 2 + 1])
            else:
                d[n] = v
        if xs is not None:
            d["x"] = xs[core]
            d["ctx"] = cs[core]
        out.append(d)
    return out


def kernel(**inputs):
    maps = _host_inputs(inputs)
    if "fused" not in _CACHE:
        _CACHE["fused"] = Prog().build()
    res = run_bass_kernel_spmd(_CACHE["fused"], maps, core_ids=list(range(8)))
    return np.stack([np.asarray(res.results[b]["y"], dtype=np.float32) for b in range(4)])
```
